# Optimizing a Trainium2 kernel written in Bass

```python
import math
import jax, jax.numpy as jnp
from jax import lax
import numpy as np

D_MODEL = 2048
BATCH = 2
SEQ = 8192
DEPTH = 4

N_MIXERS = 4
ROPE_THETA = 500000.0
ROT_FRACTION = 4
NORM_EPS = 1e-6
NEG_INF = -1e30
Q_BLOCK = 128

MOBA_HEADS = D_MODEL // 128
MOBA_HEAD_DIM = 128
MOBA_BLOCK = 256
MOBA_TOPK = 3
MOBA_Q_CHUNK = 32

MLA_HEADS = D_MODEL // 128
MLA_Q_RANK = D_MODEL // 4
MLA_KV_RANK = D_MODEL // 4
MLA_NOPE_DIM = 128
MLA_ROPE_DIM = 64
MLA_V_DIM = 128

SWA_HEADS = D_MODEL // 64
SWA_KV_HEADS = SWA_HEADS // 8
SWA_HEAD_DIM = 64
SWA_WINDOW = 128

DIFF_HEADS = D_MODEL // 256
DIFF_HEAD_DIM = 128

D_FF = ((8 * D_MODEL // 3 + 255) // 256) * 256
CONV_WIDTH = 3

kernel_name = 'hybrid_moba_mla_swa_diff_convffn'

F32 = jnp.float32


def _n_layers_of(m):
    return (DEPTH - m + N_MIXERS - 1) // N_MIXERS


def rms_norm(x, g):
    xf = x.astype(F32)
    y = xf * lax.rsqrt(jnp.mean(xf * xf, axis=-1, keepdims=True) + NORM_EPS)
    return (y * g.astype(F32)).astype(x.dtype)


def rope(x, pos, rot_dim):
    half = rot_dim // 2
    inv = ROPE_THETA ** (-jnp.arange(half, dtype=F32) / half)
    ang = pos.astype(F32)[:, None] * inv[None, :]
    cos = jnp.cos(ang)[None, :, None, :]
    sin = jnp.sin(ang)[None, :, None, :]
    xr = x[..., :rot_dim].astype(F32)
    x1, x2 = xr[..., :half], xr[..., half:]
    rot = jnp.concatenate([x1 * cos - x2 * sin, x2 * cos + x1 * sin], axis=-1).astype(x.dtype)
    return jnp.concatenate([rot, x[..., rot_dim:]], axis=-1)


def sweep_query_blocks(fn, arrays, block):
    B, S = arrays[0].shape[:2]
    nq = S // block
    blocks = tuple(jnp.moveaxis(a.reshape((B, nq, block) + a.shape[2:]), 1, 0) for a in arrays)
    out = lax.map(lambda xs: fn(xs[0], *xs[1]), (jnp.arange(nq), blocks))
    return jnp.moveaxis(out, 0, 1).reshape((B, S) + out.shape[3:])


def moba_attention(h, pos, wq, wk, wv, gq, gk, wo):
    B, S, _ = h.shape
    H, dh, BLK, QC = MOBA_HEADS, MOBA_HEAD_DIM, MOBA_BLOCK, MOBA_Q_CHUNK
    rot = dh // ROT_FRACTION
    q = rope(rms_norm((h @ wq).reshape(B, S, H, dh), gq), pos, rot)
    k = rope(rms_norm((h @ wk).reshape(B, S, H, dh), gk), pos, rot)
    v = (h @ wv).reshape(B, S, H, dh)
    pad = (-S) % BLK
    q, k, v = (jnp.pad(a, ((0, 0), (0, pad), (0, 0), (0, 0))) for a in (q, k, v))
    nb = (S + pad) // BLK
    topk = min(MOBA_TOPK, nb)
    kb = k.reshape(B, nb, BLK, H, dh).transpose(0, 3, 1, 2, 4)
    vb = v.reshape(B, nb, BLK, H, dh).transpose(0, 3, 1, 2, 4)
    k_mean = jnp.mean(kb.astype(F32), axis=3).astype(k.dtype)
    scale = dh ** -0.5
    b_idx = jnp.arange(B)[:, None, None, None]
    h_idx = jnp.arange(H)[None, :, None, None]
    blk_ids = jnp.arange(nb)
    key_off = jnp.arange(BLK)

    def chunk(c, qc):
        q_pos = c * QC + jnp.arange(QC)
        own = (c * QC) // BLK
        gate = jnp.einsum('bqhd,bhnd->bhqn', qc, k_mean).astype(F32)
        gate = jnp.where(blk_ids < own, gate, NEG_INF)
        _, idx = lax.top_k(gate, topk)
        valid = idx < own
        k_sel = kb[b_idx, h_idx, idx]
        v_sel = vb[b_idx, h_idx, idx]
        s_sel = jnp.einsum('bqhd,bhqnkd->bhqnk', qc, k_sel).astype(F32) * scale
        s_sel = jnp.where(valid[..., None], s_sel, NEG_INF).reshape(B, H, QC, topk * BLK)
        k_own = lax.dynamic_index_in_dim(kb, own, axis=2, keepdims=False)
        v_own = lax.dynamic_index_in_dim(vb, own, axis=2, keepdims=False)
        s_own = jnp.einsum('bqhd,bhkd->bhqk', qc, k_own).astype(F32) * scale
        s_own = jnp.where(own * BLK + key_off[None, :] <= q_pos[:, None], s_own, NEG_INF)
        p = jax.nn.softmax(jnp.concatenate([s_sel, s_own], axis=-1), axis=-1).astype(v.dtype)
        p_sel = p[..., :topk * BLK].reshape(B, H, QC, topk, BLK)
        p_own = p[..., topk * BLK:]
        return (jnp.einsum('bhqnk,bhqnkd->bqhd', p_sel, v_sel)
                + jnp.einsum('bhqk,bhkd->bqhd', p_own, v_own))

    o = sweep_query_blocks(chunk, (q,), QC)[:, :S]
    return o.reshape(B, S, H * dh) @ wo


def mla_attention(h, pos, wq_a, g_qa, wq_b, wkv_a, g_kva, wkv_b, g_qn, g_kn, g_qr, g_kr, wo):
    B, S, _ = h.shape
    H, NOPE, RD, VD = MLA_HEADS, MLA_NOPE_DIM, MLA_ROPE_DIM, MLA_V_DIM
    cq = rms_norm(h @ wq_a, g_qa)
    q = (cq @ wq_b).reshape(B, S, H, NOPE + RD)
    q_nope = rms_norm(q[..., :NOPE], g_qn)
    q_rope = rope(rms_norm(q[..., NOPE:], g_qr), pos, RD)
    kv_a = h @ wkv_a
    ckv = rms_norm(kv_a[..., :MLA_KV_RANK], g_kva)
    k_rope = rope(rms_norm(kv_a[..., MLA_KV_RANK:], g_kr)[:, :, None, :], pos, RD)[:, :, 0]
    kv = (ckv @ wkv_b).reshape(B, S, H, NOPE + VD)
    k_nope = rms_norm(kv[..., :NOPE], g_kn)
    v = kv[..., NOPE:]
    scale = (NOPE + RD) ** -0.5

    def block(i, qn, qr):
        q_pos = i * Q_BLOCK + jnp.arange(Q_BLOCK)
        s = (jnp.einsum('bqhd,bkhd->bhqk', qn, k_nope)
             + jnp.einsum('bqhr,bkr->bhqk', qr, k_rope)).astype(F32) * scale
        s = jnp.where(pos[None, :] <= q_pos[:, None], s, NEG_INF)
        p = jax.nn.softmax(s, axis=-1).astype(v.dtype)
        return jnp.einsum('bhqk,bkhd->bqhd', p, v)

    o = sweep_query_blocks(block, (q_nope, q_rope), Q_BLOCK)
    return o.reshape(B, S, H * VD) @ wo


def swa_attention(h, pos, wq, wk, wv, gq, gk, sinks, wo):
    B, S, _ = h.shape
    HQ, HKV, dh, W = SWA_HEADS, SWA_KV_HEADS, SWA_HEAD_DIM, SWA_WINDOW
    G = HQ // HKV
    rot = dh // ROT_FRACTION
    q = rope(rms_norm((h @ wq).reshape(B, S, HQ, dh), gq), pos, rot)
    k = rope(rms_norm((h @ wk).reshape(B, S, HKV, dh), gk), pos, rot)
    v = (h @ wv).reshape(B, S, HKV, dh)
    nb = S // W
    qb = q.reshape(B, nb, W, HKV, G, dh)

    def with_prev(a):
        ab = a.reshape(B, nb, W, HKV, dh)
        prev = jnp.pad(ab, ((0, 0), (1, 0), (0, 0), (0, 0), (0, 0)))[:, :nb]
        return jnp.concatenate([prev, ab], axis=2)

    kk, vv = with_prev(k), with_prev(v)
    s = jnp.einsum('bnqkgd,bnjkd->bnkgqj', qb, kk).astype(F32) * dh ** -0.5
    qi = jnp.arange(W)[:, None]
    kj = jnp.arange(2 * W)[None, :]
    dist = qi + W - kj
    band = (dist >= 0) & (dist < W)
    valid = band[None] & ((kj >= W)[None] | (jnp.arange(nb) > 0)[:, None, None])
    s = jnp.where(valid[None, :, None, None], s, NEG_INF)
    sink = jnp.broadcast_to(sinks.astype(F32).reshape(1, 1, HKV, G, 1, 1), s.shape[:-1] + (1,))
    p = jax.nn.softmax(jnp.concatenate([s, sink], axis=-1), axis=-1)[..., :2 * W].astype(v.dtype)
    o = jnp.einsum('bnkgqj,bnjkd->bnqkgd', p, vv).reshape(B, S, HQ * dh)
    return o @ wo


def diff_attention(h, pos, wq, wk, wv, gq, gk, lq1, lk1, lq2, lk2, g_sub, wo, lambda_init):
    B, S, _ = h.shape
    H, dh = DIFF_HEADS, DIFF_HEAD_DIM
    rot = dh // ROT_FRACTION

    def qk(w, g):
        a = rms_norm((h @ w).reshape(B, S, H * 2, dh), g)
        return rope(a, pos, rot).reshape(B, S, H, 2, dh)

    q = qk(wq, gq)
    k = qk(wk, gk)
    v = (h @ wv).reshape(B, S, H, 2 * dh)
    lam = (jnp.exp(jnp.sum(lq1.astype(F32) * lk1.astype(F32)))
           - jnp.exp(jnp.sum(lq2.astype(F32) * lk2.astype(F32))) + lambda_init)
    scale = dh ** -0.5

    def block(i, qb):
        q_pos = i * Q_BLOCK + jnp.arange(Q_BLOCK)
        s = jnp.einsum('bqhcd,bkhcd->bhcqk', qb, k).astype(F32) * scale
        s = jnp.where(pos[None, :] <= q_pos[:, None], s, NEG_INF)
        p = jax.nn.softmax(s, axis=-1)
        a = (p[:, :, 0] - lam * p[:, :, 1]).astype(v.dtype)
        return jnp.einsum('bhqk,bkhe->bqhe', a, v)

    o = sweep_query_blocks(block, (q,), Q_BLOCK)
    o = rms_norm(o, g_sub) * (1.0 - lambda_init)
    return o.reshape(B, S, H * 2 * dh) @ wo


def conv_glu_ffn(h, w_gate, w_up, conv_w, conv_b, w_down):
    g = h @ w_gate
    g = lax.conv_general_dilated(
        g, conv_w[:, None, :].astype(g.dtype), window_strides=(1,),
        padding=[(CONV_WIDTH - 1, 0)], dimension_numbers=('NWC', 'WIO', 'NWC'),
        feature_group_count=g.shape[-1]) + conv_b.astype(g.dtype)
    return (jax.nn.silu(g) * (h @ w_up)) @ w_down


def setup_inputs(seed: int = 0) -> dict:
    key = jax.random.key(seed)
    ks = iter(jax.random.split(key, 64))
    D = D_MODEL
    nA, nB, nC, nD = (_n_layers_of(m) for m in range(N_MIXERS))
    out_s = (2 * DEPTH) ** -0.5

    def w(shape, fan_in, scale=1.0):
        return jax.random.normal(next(ks), shape, F32) * (scale * fan_in ** -0.5)

    def gain(shape):
        return 1.0 + 0.02 * jax.random.normal(next(ks), shape, F32)

    def small(shape, s):
        return s * jax.random.normal(next(ks), shape, F32)

    inp = {}
    inp['x'] = jax.random.normal(next(ks), (BATCH, SEQ, D), F32)
    inp['attn_norm'] = gain((DEPTH, D))
    inp['ffn_norm'] = gain((DEPTH, D))
    ad = MOBA_HEADS * MOBA_HEAD_DIM
    inp['moba_wq'] = w((nA, D, ad), D)
    inp['moba_wk'] = w((nA, D, ad), D)
    inp['moba_wv'] = w((nA, D, ad), D)
    inp['moba_gq'] = gain((nA, MOBA_HEAD_DIM))
    inp['moba_gk'] = gain((nA, MOBA_HEAD_DIM))
    inp['moba_wo'] = w((nA, ad, D), ad, out_s)
    inp['mla_wq_a'] = w((nB, D, MLA_Q_RANK), D)
    inp['mla_g_qa'] = gain((nB, MLA_Q_RANK))
    inp['mla_wq_b'] = w((nB, MLA_Q_RANK, MLA_HEADS * (MLA_NOPE_DIM + MLA_ROPE_DIM)), MLA_Q_RANK)
    inp['mla_wkv_a'] = w((nB, D, MLA_KV_RANK + MLA_ROPE_DIM), D)
    inp['mla_g_kva'] = gain((nB, MLA_KV_RANK))
    inp['mla_wkv_b'] = w((nB, MLA_KV_RANK, MLA_HEADS * (MLA_NOPE_DIM + MLA_V_DIM)), MLA_KV_RANK)
    inp['mla_g_qn'] = gain((nB, MLA_NOPE_DIM))
    inp['mla_g_kn'] = gain((nB, MLA_NOPE_DIM))
    inp['mla_g_qr'] = gain((nB, MLA_ROPE_DIM))
    inp['mla_g_kr'] = gain((nB, MLA_ROPE_DIM))
    inp['mla_wo'] = w((nB, MLA_HEADS * MLA_V_DIM, D), MLA_HEADS * MLA_V_DIM, out_s)
    inp['swa_wq'] = w((nC, D, SWA_HEADS * SWA_HEAD_DIM), D)
    inp['swa_wk'] = w((nC, D, SWA_KV_HEADS * SWA_HEAD_DIM), D)
    inp['swa_wv'] = w((nC, D, SWA_KV_HEADS * SWA_HEAD_DIM), D)
    inp['swa_gq'] = gain((nC, SWA_HEAD_DIM))
    inp['swa_gk'] = gain((nC, SWA_HEAD_DIM))
    inp['swa_sinks'] = small((nC, SWA_HEADS), 0.5)
    inp['swa_wo'] = w((nC, SWA_HEADS * SWA_HEAD_DIM, D), SWA_HEADS * SWA_HEAD_DIM, out_s)
    dq = DIFF_HEADS * 2 * DIFF_HEAD_DIM
    inp['diff_wq'] = w((nD, D, dq), D)
    inp['diff_wk'] = w((nD, D, dq), D)
    inp['diff_wv'] = w((nD, D, dq), D)
    inp['diff_gq'] = gain((nD, DIFF_HEAD_DIM))
    inp['diff_gk'] = gain((nD, DIFF_HEAD_DIM))
    inp['diff_lq1'] = small((nD, DIFF_HEAD_DIM), 0.1)
    inp['diff_lk1'] = small((nD, DIFF_HEAD_DIM), 0.1)
    inp['diff_lq2'] = small((nD, DIFF_HEAD_DIM), 0.1)
    inp['diff_lk2'] = small((nD, DIFF_HEAD_DIM), 0.1)
    inp['diff_g_sub'] = gain((nD, 2 * DIFF_HEAD_DIM))
    inp['diff_wo'] = w((nD, dq, D), dq, out_s)
    inp['ffn_w_gate'] = w((DEPTH, D, D_FF), D)
    inp['ffn_w_up'] = w((DEPTH, D, D_FF), D)
    inp['ffn_conv_w'] = w((DEPTH, CONV_WIDTH, D_FF), CONV_WIDTH)
    inp['ffn_conv_b'] = small((DEPTH, D_FF), 0.02)
    inp['ffn_w_down'] = w((DEPTH, D_FF, D), D_FF, out_s)
    return inp


def reference(x, attn_norm, ffn_norm,
              moba_wq, moba_wk, moba_wv, moba_gq, moba_gk, moba_wo,
              mla_wq_a, mla_g_qa, mla_wq_b, mla_wkv_a, mla_g_kva, mla_wkv_b,
              mla_g_qn, mla_g_kn, mla_g_qr, mla_g_kr, mla_wo,
              swa_wq, swa_wk, swa_wv, swa_gq, swa_gk, swa_sinks, swa_wo,
              diff_wq, diff_wk, diff_wv, diff_gq, diff_gk,
              diff_lq1, diff_lk1, diff_lq2, diff_lk2, diff_g_sub, diff_wo,
              ffn_w_gate, ffn_w_up, ffn_conv_w, ffn_conv_b, ffn_w_down):
    pos = jnp.arange(x.shape[1], dtype=jnp.int32)
    for i in range(DEPTH):
        m, j = i % N_MIXERS, i // N_MIXERS
        h = rms_norm(x, attn_norm[i])
        if m == 0:
            y = moba_attention(h, pos, moba_wq[j], moba_wk[j], moba_wv[j],
                               moba_gq[j], moba_gk[j], moba_wo[j])
        elif m == 1:
            y = mla_attention(h, pos, mla_wq_a[j], mla_g_qa[j], mla_wq_b[j], mla_wkv_a[j],
                              mla_g_kva[j], mla_wkv_b[j], mla_g_qn[j], mla_g_kn[j],
                              mla_g_qr[j], mla_g_kr[j], mla_wo[j])
        elif m == 2:
            y = swa_attention(h, pos, swa_wq[j], swa_wk[j], swa_wv[j], swa_gq[j],
                              swa_gk[j], swa_sinks[j], swa_wo[j])
        else:
            lambda_init = 0.8 - 0.6 * math.exp(-0.3 * i)
            y = diff_attention(h, pos, diff_wq[j], diff_wk[j], diff_wv[j], diff_gq[j],
                               diff_gk[j], diff_lq1[j], diff_lk1[j], diff_lq2[j],
                               diff_lk2[j], diff_g_sub[j], diff_wo[j], lambda_init)
        x = x + y
        h = rms_norm(x, ffn_norm[i])
        x = x + conv_glu_ffn(h, ffn_w_gate[i], ffn_w_up[i], ffn_conv_w[i],
                             ffn_conv_b[i], ffn_w_down[i])
    return x
```

```python
import contextlib
import os
import math
import numpy as np
import ml_dtypes
import concourse.bass as bass
import concourse.mybir as mybir
from concourse.bass_utils import run_bass_kernel_spmd

F32 = mybir.dt.float32
BF16 = mybir.dt.bfloat16
ALU = mybir.AluOpType
AF = mybir.ActivationFunctionType
AX = mybir.AxisListType
NPBF16 = ml_dtypes.bfloat16

D_MODEL = 2048
D_FF = 5632
EPS = 1e-6
ROPE_THETA = 500000.0
NCORES = 8


class Prog:
    CENG = ("pe", "act", "dve", "pool")

    def __init__(self, nc, ndma=6):
        self.nc = nc
        self.es = contextlib.ExitStack()
        self.ops = {e: [] for e in ("pe", "act", "dve", "pool", "sp")}
        self.csem = {e: self.es.enter_context(nc.semaphore("c_" + e)) for e in self.CENG}
        self.ccnt = {e: 0 for e in self.CENG}
        self.dsem = {q: [self.es.enter_context(nc.semaphore(f"d_{q}{i}")) for i in range(ndma)]
                     for q in ("sp", "pool")}
        self.dval = {q: [0] * ndma for q in ("sp", "pool")}
        self.dnext = {q: 0 for q in ("sp", "pool")}
        self.seen = {e: {} for e in self.ops}
        self.lastw = {}
        self.readers = {}
        self.semobj = {}

    def close(self):
        self.es.close()

    def _key(self, s):
        k = id(s)
        self.semobj[k] = s
        return k

    def op(self, eng, fn, reads=(), writes=(), dma=False, excl=()):
        writes = list(writes) + list(excl)
        deps = {}

        def add(ev):
            if ev is None:
                return
            k, v = ev
            if deps.get(k, 0) < v:
                deps[k] = v

        for b in reads:
            add(self.lastw.get(b))
        for b in writes:
            add(self.lastw.get(b))
            for ev in self.readers.get(b, {}).items():
                add(ev)
        if dma:
            q = eng
            i = self.dnext[q]
            self.dnext[q] = (i + 1) % len(self.dsem[q])
            sem = self.dsem[q][i]
            k = self._key(sem)
            if self.dval[q][i] > 0:
                add((k, self.dval[q][i]))
            self.dval[q][i] += 16
            ev = (k, self.dval[q][i])
            inc = 16
        else:
            sem = self.csem[eng]
            k = self._key(sem)
            self.ccnt[eng] += 1
            ev = (k, self.ccnt[eng])
            inc = 1
            if eng == "pe":
                deps.pop(k, None)
        seen = self.seen[eng]
        waits = []
        for dk, dv in deps.items():
            if seen.get(dk, 0) >= dv:
                continue
            seen[dk] = dv
            waits.append((self.semobj[dk], dv))
        for b in writes:
            self.lastw[b] = ev
            self.readers[b] = {}
        for b in reads:
            r = self.readers.setdefault(b, {})
            if r.get(ev[0], 0) < ev[1]:
                r[ev[0]] = ev[1]
        self.ops[eng].append((fn, waits, sem, inc))
        return ev

    def dma(self, q, out, in_, reads=(), writes=()):
        return self.op(q, lambda e: e.dma_start(out=out, in_=in_), reads, writes, dma=True)

    def emit(self):
        finals = []
        for e in self.CENG:
            if self.ccnt[e] > 0:
                finals.append((self._key(self.csem[e]), self.csem[e], self.ccnt[e]))
        for q in ("sp", "pool"):
            for s, v in zip(self.dsem[q], self.dval[q]):
                if v > 0:
                    finals.append((self._key(s), s, v))
        ops = self.ops
        seen = self.seen

        def mk(ename):
            def body(engine):
                for fn, waits, sem, inc in ops[ename]:
                    for s, v in waits:
                        engine.wait_ge(s, v)
                    fn(engine).then_inc(sem, inc)
                for k, s, v in finals:
                    if seen[ename].get(k, 0) < v:
                        engine.wait_ge(s, v)
                        seen[ename][k] = v
            return body

        with self.nc.Block() as block:
            block.tensor(mk("pe"))
            block.scalar(mk("act"))
            block.vector(mk("dve"))
            block.gpsimd(mk("pool"))
            block.sync(mk("sp"))
        self.ops = {e: [] for e in self.ops}


def build_identity(P, nc, ident, es):
    it_p = es.enter_context(nc.sbuf_tensor(uname("it_p"), [128, 128], F32))
    it_j = es.enter_context(nc.sbuf_tensor(uname("it_j"), [128, 128], F32))
    P.op("pool", lambda e: e.iota(it_p[:], [[0, 128]], base=0, channel_multiplier=1,
                                  allow_small_or_imprecise_dtypes=True), writes=["it_p"])
    P.op("pool", lambda e: e.iota(it_j[:], [[1, 128]], base=0, channel_multiplier=0,
                                  allow_small_or_imprecise_dtypes=True), writes=["it_j"])
    P.op("dve", lambda e: e.tensor_tensor(ident[:], it_p[:], it_j[:], ALU.is_equal),
         reads=["it_p", "it_j"], writes=["ident"])
    return it_p, it_j


def emit_rstd(P, out, ss, okey, skey, inv_n, epsb):
    P.op("act", lambda e: e.activation(out, ss, AF.Sqrt, bias=epsb[:, 0:1], scale=inv_n),
         reads=[skey, "consts"], writes=[okey])
    P.op("dve", lambda e: e.reciprocal(out, out), reads=[okey], writes=[okey])


def emit_norm_transpose(P, nc, xt, xkeys, g_rep, ident, scr, hT_out, hT_key, pst, pst_key):
    D = D_MODEL
    sq, ss, rstd, hn = scr["sq"], scr["ss"], scr["rstd"], scr["hn"]
    P.op("act", lambda e: e.activation(sq[:], xt, AF.Square, accum_out=ss[:]),
         reads=list(xkeys), writes=["sq", "ss"])
    emit_rstd(P, rstd[:], ss[:], "rstd", "ss", 1.0 / D, scr["epsb"])
    P.op("dve", lambda e: e.scalar_tensor_tensor(hn[:], xt, rstd[:, 0:1], g_rep, ALU.mult, ALU.mult),
         reads=list(xkeys) + ["rstd", "consts"], writes=["hn"])
    for half in range(2):
        pt = pst[half]
        for j in range(8):
            c = half * 8 + j
            P.op("pe", lambda e, c=c, j=j, pt=pt: e.transpose(pt[:, j * 128:(j + 1) * 128],
                                                              hn[:, c * 128:(c + 1) * 128], ident[:]),
                 reads=["hn", "ident"], writes=[pst_key[half]])
        eng = "act" if half == 0 else "dve"
        if eng == "act":
            P.op("act", lambda e, pt=pt, half=half: e.activation(
                hT_out[:, half * 8:(half + 1) * 8, :], pt[:].rearrange("p (c t) -> p c t", c=8), AF.Copy),
                 reads=[pst_key[half]], writes=[hT_key])
        else:
            P.op("dve", lambda e, pt=pt, half=half: e.tensor_copy(
                hT_out[:, half * 8:(half + 1) * 8, :], pt[:].rearrange("p (c t) -> p c t", c=8)),
                 reads=[pst_key[half]], writes=[hT_key])


_UID = [0]


def uname(name):
    _UID[0] += 1
    return f"{name}_u{_UID[0]}"


def make_T(es, nc):
    def T(name, shape, dt):
        return es.enter_context(nc.sbuf_tensor(uname(name), shape, dt))
    return T


def psum_T(es, nc, name, shape, dt):
    return es.enter_context(nc.psum_tensor(uname(name), shape, dt))


def phase_outproj_norm(P, nc, TOK, ko, x, xh, oT, oTh, wo, gf, xmid, hTd):
    D = D_MODEL
    NT = TOK // 128
    NKo = D // ko
    with contextlib.ExitStack() as es:
        T = make_T(es, nc)
        wo_sb = T("wo_sb", [ko, NKo, D], BF16)
        oT_sb = T("oT_sb", [ko, NKo, TOK], BF16)
        oTh_sb = T("oTh_sb", [ko, NKo, 128], BF16)
        g_rep = T("g_rep", [128, D], F32)
        ident = T("ident", [128, 128], BF16)
        scr = dict(sq=T("sq", [128, D], F32), ss=T("ss", [128, 1], F32), rstd=T("rstd", [128, 1], F32),
                   hn=T("hn", [128, D], BF16), epsb=T("epsb", [128, 1], F32))
        xt = [T(f"xt{i}", [128, D], F32) for i in range(2)]
        xm = [T(f"xm{i}", [128, D], F32) for i in range(2)]
        hT_sb = [T(f"hT_sb{i}", [128, 16, 128], BF16) for i in range(2)]
        py = [psum_T(es, nc, f"py{i}", [128, 512], F32) for i in range(4)]
        pst = [psum_T(es, nc, f"pst{i}", [128, 1024], BF16) for i in range(2)]
        build_identity(P, nc, ident, es)
        P.op("pool", lambda e: e.memset(scr["epsb"][:], EPS), writes=["consts"])
        P.dma("sp", g_rep[:], gf[0:1, :].partition_broadcast(128), writes=["consts"])
        for c in range(NKo):
            P.dma("pool", wo_sb[:, c, :], wo[c * ko:(c + 1) * ko, :], writes=[f"wo{c}"])
        P.dma("sp", oTh_sb[:], oTh.rearrange("c p t -> p c t"), writes=["oTh"])
        for c in range(NKo):
            P.dma("sp", oT_sb[:, c, :], oT[c], writes=[f"oT{c}"])
        order = [NT] + list(range(NT))
        for n, i in enumerate(order):
            b = n % 2
            halo = (i == NT)
            xsrc = xh[:, :] if halo else x[i * 128:(i + 1) * 128, :]
            P.dma("sp", xt[b][:], xsrc, writes=[f"xt{b}"])
            for nb in range(4):
                for c in range(NKo):
                    if halo:
                        lhsT = oTh_sb[:, c, :]
                        rk = "oTh"
                    else:
                        lhsT = oT_sb[:, c, i * 128:(i + 1) * 128]
                        rk = f"oT{c}"
                    P.op("pe", lambda e, nb=nb, c=c, lhsT=lhsT: e.matmul(
                        py[nb][:], lhsT, wo_sb[:, c, nb * 512:(nb + 1) * 512], start=(c == 0), stop=(c == NKo - 1)),
                        reads=[rk, f"wo{c}"], writes=[f"py{nb}"])
                P.op("dve", lambda e, nb=nb, b=b: e.tensor_tensor(
                    xm[b][:, nb * 512:(nb + 1) * 512], xt[b][:, nb * 512:(nb + 1) * 512], py[nb][:], ALU.add),
                    reads=[f"xt{b}", f"py{nb}"], writes=[f"xm{b}_{nb}"])
            xkeys = [f"xm{b}_{nb}" for nb in range(4)]
            if not halo:
                P.dma("sp", xmid[i * 128:(i + 1) * 128, :], xm[b][:], reads=xkeys, writes=[f"xmid{i}"])
            emit_norm_transpose(P, nc, xm[b][:], xkeys, g_rep[:], ident, scr, hT_sb[b], f"hTsb{b}",
                                pst, ["pst0", "pst1"])
            P.dma("sp", hTd[i], hT_sb[b][:].rearrange("p c t -> p (c t)"), reads=[f"hTsb{b}"], writes=[f"hTd{i}"])
        P.emit()


def phase_ffn(P, nc, TOK, xmid, hTd, wg, wu, wd, cw, xo):
    D = D_MODEL
    NT = TOK // 128
    NSB = TOK // 512
    NFB = D_FF // 512
    NFC = D_FF // 128
    with contextlib.ExitStack() as es:
        T = make_T(es, nc)
        hT_sb = T("f_hT", [128, 16, 512], BF16)
        hTh = T("f_hTh", [128, 16, 128], BF16)
        xacc = T("f_xacc", [128, 4, D], F32)
        wg_sb = [T(f"f_wg{i}", [128, 16, 512], BF16) for i in range(2)]
        wu_sb = [T(f"f_wu{i}", [128, 16, 512], BF16) for i in range(2)]
        wd_sb = [T(f"f_wd{i}", [128, 4, D], BF16) for i in range(2)]
        aT = [T(f"f_aT{i}", [128, 4, 512], BF16) for i in range(2)]
        gs = [T(f"f_gs{i}", [128, 514], F32) for i in range(2)]
        c1 = [T(f"f_c1{i}", [128, 512], F32) for i in range(2)]
        carry = T("f_carry", [128, NFC, 2], F32)
        cw_sb = T("f_cw", [128, 4 * NFC], F32)
        psg = [psum_T(es, nc, f"psg{i}", [128, 512], F32) for i in range(2)]
        psu = [psum_T(es, nc, f"psu{i}", [128, 512], F32) for i in range(2)]
        pso = [psum_T(es, nc, f"pso{i}", [128, 512], F32) for i in range(3)]
        ph = psum_T(es, nc, "ph", [128, 2 * NFC], F32)
        P.dma("sp", cw_sb[:], cw[:, :], writes=["cw"])
        P.dma("sp", hTh[:], hTd[NT].rearrange("p (c t) -> p c t", c=16), writes=["hTh"])
        npo = 0
        for sb in range(NSB):
            for tt in range(4):
                P.dma("sp", hT_sb[:, :, tt * 128:(tt + 1) * 128],
                      hTd[sb * 4 + tt].rearrange("p (c t) -> p c t", c=16), writes=[f"hT{tt}"])
                P.dma("sp", xacc[:, tt, :], xmid[(sb * 4 + tt) * 128:(sb * 4 + tt + 1) * 128, :],
                      writes=[f"xacc{tt}_{nb}" for nb in range(4)])
            hkeys = [f"hT{tt}" for tt in range(4)]
            for fb in range(NFB):
                wb = fb % 2
                P.dma("pool", wg_sb[wb][:], wg[:, fb * 512:(fb + 1) * 512].rearrange("(c p) f -> p c f", p=128),
                      writes=[f"wg{wb}"])
                P.dma("pool", wu_sb[wb][:], wu[:, fb * 512:(fb + 1) * 512].rearrange("(c p) f -> p c f", p=128),
                      writes=[f"wu{wb}"])
                P.dma("pool", wd_sb[wb][:], wd[fb * 512:(fb + 1) * 512, :].rearrange("(c p) n -> p c n", p=128),
                      writes=[f"wd{wb}"])
                for fcl in range(4):
                    fc = fb * 4 + fcl
                    pb = fc % 2
                    for kc in range(16):
                        P.op("pe", lambda e, kc=kc, fcl=fcl, wb=wb, pb=pb: e.matmul(
                            psg[pb][:], wg_sb[wb][:, kc, fcl * 128:(fcl + 1) * 128], hT_sb[:, kc, :],
                            start=(kc == 0), stop=(kc == 15)),
                            reads=[f"wg{wb}"] + hkeys, writes=[f"psg{pb}"])
                    if sb == 0:
                        for kc in range(16):
                            P.op("pe", lambda e, kc=kc, fcl=fcl, wb=wb, fc=fc: e.matmul(
                                ph[:, 2 * fc:2 * fc + 2], wg_sb[wb][:, kc, fcl * 128:(fcl + 1) * 128],
                                hTh[:, kc, 126:128], start=(kc == 0), stop=(kc == 15)),
                                reads=[f"wg{wb}", "hTh"], writes=["ph"])
                        P.op("act", lambda e, fc=fc: e.activation(carry[:, fc, :], ph[:, 2 * fc:2 * fc + 2], AF.Copy),
                             excl=["ph"], writes=[f"carry{fc}"])
                    for kc in range(16):
                        P.op("pe", lambda e, kc=kc, fcl=fcl, wb=wb, pb=pb: e.matmul(
                            psu[pb][:], wu_sb[wb][:, kc, fcl * 128:(fcl + 1) * 128], hT_sb[:, kc, :],
                            start=(kc == 0), stop=(kc == 15)),
                            reads=[f"wu{wb}"] + hkeys, writes=[f"psu{pb}"])
                    g_ = gs[pb]
                    c_ = c1[pb]
                    P.op("pool", lambda e, g_=g_, fc=fc: e.tensor_copy(g_[:, 0:2], carry[:, fc, :]),
                         reads=[f"carry{fc}"], writes=[f"gsh{pb}"])
                    P.op("act", lambda e, g_=g_, pb=pb: e.activation(g_[:, 2:514], psg[pb][:], AF.Copy),
                         reads=[f"psg{pb}"], writes=[f"gs{pb}"])
                    P.op("pool", lambda e, g_=g_, fc=fc: e.tensor_copy(carry[:, fc, :], g_[:, 512:514]),
                         reads=[f"gs{pb}", f"gsh{pb}"], writes=[f"carry{fc}"])
                    P.op("act", lambda e, c_=c_, pb=pb, fc=fc: e.activation(
                        c_[:], psg[pb][:], AF.Identity, bias=cw_sb[:, 3 * NFC + fc:3 * NFC + fc + 1],
                        scale=cw_sb[:, 2 * NFC + fc:2 * NFC + fc + 1]),
                        reads=[f"psg{pb}", "cw"], writes=[f"c1{pb}"])
                    P.op("dve", lambda e, c_=c_, g_=g_, fc=fc: e.scalar_tensor_tensor(
                        c_[:], g_[:, 1:513], cw_sb[:, NFC + fc:NFC + fc + 1], c_[:], ALU.mult, ALU.add),
                        reads=[f"gs{pb}", f"gsh{pb}", "cw", f"c1{pb}"], writes=[f"c1{pb}"])
                    P.op("dve", lambda e, c_=c_, g_=g_, fc=fc: e.scalar_tensor_tensor(
                        c_[:], g_[:, 0:512], cw_sb[:, fc:fc + 1], c_[:], ALU.mult, ALU.add),
                        reads=[f"gs{pb}", f"gsh{pb}", "cw", f"c1{pb}"], writes=[f"c1{pb}"])
                    P.op("act", lambda e, c_=c_: e.activation(c_[:], c_[:], AF.Silu),
                         reads=[f"c1{pb}"], writes=[f"c1{pb}"])
                    P.op("dve", lambda e, c_=c_, wb=wb, fcl=fcl, pb=pb: e.tensor_tensor(
                        aT[wb][:, fcl, :], c_[:], psu[pb][:], ALU.mult),
                        reads=[f"c1{pb}", f"psu{pb}"], writes=[f"aT{wb}_{fcl}"])
                for tt in range(4):
                    for nb in range(4):
                        pi = npo % 3
                        npo += 1
                        for fcl in range(4):
                            P.op("pe", lambda e, pi=pi, wb=wb, fcl=fcl, tt=tt, nb=nb: e.matmul(
                                pso[pi][:], aT[wb][:, fcl, tt * 128:(tt + 1) * 128],
                                wd_sb[wb][:, fcl, nb * 512:(nb + 1) * 512], start=(fcl == 0), stop=(fcl == 3)),
                                reads=[f"aT{wb}_{fcl}", f"wd{wb}"], writes=[f"pso{pi}"])
                        P.op("dve", lambda e, pi=pi, tt=tt, nb=nb: e.tensor_tensor(
                            xacc[:, tt, nb * 512:(nb + 1) * 512], xacc[:, tt, nb * 512:(nb + 1) * 512],
                            pso[pi][:], ALU.add),
                            reads=[f"pso{pi}", f"xacc{tt}_{nb}"], writes=[f"xacc{tt}_{nb}"])
            for tt in range(4):
                P.dma("sp", xo[(sb * 4 + tt) * 128:(sb * 4 + tt + 1) * 128, :], xacc[:, tt, :],
                      reads=[f"xacc{tt}_{nb}" for nb in range(4)], writes=[f"xo{sb}_{tt}"])
        P.emit()


def build_k3(TOK, ko):
    nc = bass.Bass("TRN2", target_bir_lowering=False)
    D = D_MODEL
    NT = TOK // 128
    NKo = D // ko
    x = nc.dram_tensor("x", [TOK, D], F32, kind="ExternalInput").ap()
    xh = nc.dram_tensor("xh", [128, D], F32, kind="ExternalInput").ap()
    oT = nc.dram_tensor("oT", [NKo, ko, TOK], BF16, kind="ExternalInput").ap()
    oTh = nc.dram_tensor("oTh", [NKo, ko, 128], BF16, kind="ExternalInput").ap()
    wo = nc.dram_tensor("wo", [D, D], F32, kind="ExternalInput").ap()
    gf = nc.dram_tensor("gf", [1, D], F32, kind="ExternalInput").ap()
    wg = nc.dram_tensor("wg", [D, D_FF], F32, kind="ExternalInput").ap()
    wu = nc.dram_tensor("wu", [D, D_FF], F32, kind="ExternalInput").ap()
    wd = nc.dram_tensor("wd", [D_FF, D], F32, kind="ExternalInput").ap()
    cw = nc.dram_tensor("cw", [128, 4 * (D_FF // 128)], F32, kind="ExternalInput").ap()
    xo = nc.dram_tensor("xo", [TOK, D], F32, kind="ExternalOutput").ap()
    xmid = nc.dram_tensor("xmid", [TOK, D], F32).ap()
    hTd = nc.dram_tensor("hTd", [NT + 1, 128, 16 * 128], BF16).ap()
    P = Prog(nc)
    phase_outproj_norm(P, nc, TOK, ko, x, xh, oT, oTh, wo, gf, xmid, hTd)
    phase_ffn(P, nc, TOK, xmid, hTd, wg, wu, wd, cw, xo)
    P.close()
    return nc


def host_cw(conv_w, conv_b):
    NFC = D_FF // 128
    a = np.concatenate([conv_w, conv_b[None, :]], axis=0)
    return np.ascontiguousarray(a.reshape(4, NFC, 128).transpose(2, 0, 1).reshape(128, 4 * NFC))


def phase_norm(P, nc, TOK, x, g, hTd):
    D = D_MODEL
    NT = TOK // 128
    with contextlib.ExitStack() as es:
        T = make_T(es, nc)
        g_rep = T("n_g_rep", [128, D], F32)
        ident = T("n_ident", [128, 128], BF16)
        scr = dict(sq=T("n_sq", [128, D], F32), ss=T("n_ss", [128, 1], F32), rstd=T("n_rstd", [128, 1], F32),
                   hn=T("n_hn", [128, D], BF16), epsb=T("n_epsb", [128, 1], F32))
        xt = [T(f"n_xt{i}", [128, D], F32) for i in range(2)]
        hT_sb = [T(f"n_hT{i}", [128, 16, 128], BF16) for i in range(2)]
        pst = [psum_T(es, nc, f"n_pst{i}", [128, 1024], BF16) for i in range(2)]
        build_identity(P, nc, ident, es)
        P.op("pool", lambda e: e.memset(scr["epsb"][:], EPS), writes=["consts"])
        P.dma("sp", g_rep[:], g[0:1, :].partition_broadcast(128), writes=["consts"])
        for i in range(NT):
            b = i % 2
            P.dma("sp", xt[b][:], x[i * 128:(i + 1) * 128, :], writes=[f"xt{b}"])
            emit_norm_transpose(P, nc, xt[b][:], [f"xt{b}"], g_rep[:], ident, scr, hT_sb[b], f"hTsb{b}",
                                pst, ["pst0", "pst1"])
            P.dma("sp", hTd[:, :, i * 128:(i + 1) * 128].rearrange("c p t -> p c t"), hT_sb[b][:],
                  reads=[f"hTsb{b}"], writes=[f"hTd{i}"])
        P.emit()


def phase_proj(P, nc, TOK, actT, nk, Kp, W, blocks, gains, pos, tag):
    NT = TOK // 128
    rots = sorted({s["rot"] for _, _, segs in blocks for s in segs if s["rot"]})
    gnames = sorted({s["gain"] for _, _, segs in blocks for s in segs if s["gain"]})
    with contextlib.ExitStack() as es:
        T = make_T(es, nc)
        act = T(tag + "act", [Kp, nk, TOK], BF16)
        wsb = [T(f"{tag}w{i}", [Kp, nk, 512], BF16) for i in range(2)]
        ident = T(tag + "ident", [128, 128], BF16)
        epsb = T(tag + "epsb", [128, 1], F32)
        negpi = T(tag + "negpi", [128, 1], F32)
        pos_sb = T(tag + "pos", [128, NT], F32)
        grep = {}
        for gname in gnames:
            wdt = gains[gname].shape[1]
            grep[gname] = T(f"{tag}g_{gname}", [128, wdt], F32)
        tabs = {}
        for R in rots:
            half = R // 2
            tabs[R] = dict(cos=T(f"{tag}cos{R}", [128, NT, half], F32), sin=T(f"{tag}sin{R}", [128, NT, half], F32),
                           inv=T(f"{tag}inv{R}", [128, half], F32), ang=T(f"{tag}ang{R}", [128, NT, half], F32),
                           ki=T(f"{tag}ki{R}", [128, NT, half], mybir.dt.int32),
                           kf=T(f"{tag}kf{R}", [128, NT, half], F32))
        sq = T(tag + "sq", [128, 512], F32)
        ss = [T(f"{tag}ss{i}", [128, 8], F32) for i in range(2)]
        rstd = [T(f"{tag}rstd{i}", [128, 8], F32) for i in range(2)]
        qn = [T(f"{tag}qn{i}", [128, 512], F32) for i in range(2)]
        qb = [T(f"{tag}qb{i}", [128, 512], BF16) for i in range(2)]
        rt = [T(f"{tag}rt{i}", [128, 256], F32) for i in range(4)]
        stg = [T(f"{tag}stg{i}", [128, 4, TOK], BF16) for i in range(2)]
        ps = [psum_T(es, nc, f"{tag}ps{i}", [128, 512], F32) for i in range(2)]
        pT = [psum_T(es, nc, f"{tag}pT{i}", [128, 1024], BF16) for i in range(2)]
        build_identity(P, nc, ident, es)
        P.op("pool", lambda e: e.memset(epsb[:], EPS), writes=["c_eps"])
        P.op("pool", lambda e: e.memset(negpi[:], -math.pi), writes=["c_negpi"])
        P.dma("sp", pos_sb[:], pos[:, :], writes=["pos"])
        for gname in gnames:
            P.dma("sp", grep[gname][:], gains[gname][0:1, :].partition_broadcast(128), writes=["g_" + gname])
        for R in rots:
            half = R // 2
            tb = tabs[R]
            P.op("pool", lambda e, tb=tb, half=half: e.iota(tb["inv"][:], [[1, half]], base=0, channel_multiplier=0,
                                                            allow_small_or_imprecise_dtypes=True),
                 writes=[f"inv{R}"])
            P.op("act", lambda e, tb=tb, half=half: e.activation(tb["inv"][:], tb["inv"][:], AF.Exp,
                                                                 scale=-math.log(ROPE_THETA) / half),
                 reads=[f"inv{R}"], writes=[f"inv{R}"])
            for t in range(NT):
                P.op("dve", lambda e, tb=tb, t=t: e.tensor_scalar(tb["ang"][:, t, :], tb["inv"][:],
                                                                  pos_sb[:, t:t + 1], None, ALU.mult),
                     reads=[f"inv{R}", "pos"], writes=[f"ang{R}_{t}"])
            akeys = [f"ang{R}_{t}" for t in range(NT)]
            for nm, shift in (("sin", 0.0), ("cos", 0.5 * math.pi)):
                dst = tb[nm]
                key = f"{nm}{R}"
                P.op("dve", lambda e, tb=tb, dst=dst, shift=shift: e.tensor_scalar(
                    dst[:], tb["ang"][:], shift, None, ALU.add), reads=akeys, writes=[key])
                P.op("dve", lambda e, tb=tb, dst=dst: e.tensor_scalar(
                    tb["ki"][:], dst[:], 1.0 / (2 * math.pi), None, ALU.mult), reads=[key], writes=[f"ki{R}"])
                P.op("dve", lambda e, tb=tb: e.tensor_copy(tb["kf"][:], tb["ki"][:]),
                     reads=[f"ki{R}"], writes=[f"kf{R}"])
                P.op("dve", lambda e, tb=tb, dst=dst: e.scalar_tensor_tensor(
                    dst[:], tb["kf"][:], -2 * math.pi, dst[:], ALU.mult, ALU.add),
                    reads=[f"kf{R}", key], writes=[key])
                P.op("dve", lambda e, tb=tb, dst=dst: e.tensor_scalar(
                    tb["kf"][:], dst[:], math.pi, -2 * math.pi, ALU.is_gt, ALU.mult),
                    reads=[key], writes=[f"kf{R}"])
                P.op("dve", lambda e, tb=tb, dst=dst: e.tensor_tensor(dst[:], dst[:], tb["kf"][:], ALU.add),
                     reads=[key, f"kf{R}"], writes=[key])
                P.op("act", lambda e, dst=dst: e.activation(dst[:], dst[:], AF.Sin),
                     reads=[key], writes=[key])
        for c in range(nk):
            P.dma("sp", act[:, c, :], actT[c], writes=[f"act{c}"])
        actkeys = [f"act{c}" for c in range(nk)]
        n = 0
        for bi, (col0, w, segs) in enumerate(blocks):
            wb = bi % 2
            P.dma("pool", wsb[wb][:, :, 0:w], W[:, col0:col0 + w].rearrange("(c p) f -> p c f", p=Kp),
                  writes=[f"w{wb}"])
            tch = []
            for si, s in enumerate(segs):
                if s["kind"] == "T":
                    for ci, o in enumerate(range(0, s["w"], 128)):
                        tch.append((si, ci, s["off"] + o, min(128, s["w"] - o), len(tch)))
            assert len(tch) <= 4
            sg = stg[bi % 2]
            for t in range(NT):
                pb = n % 2
                n += 1
                for kc in range(nk):
                    P.op("pe", lambda e, kc=kc, t=t, pb=pb, wb=wb, w=w: e.matmul(
                        ps[pb][:, 0:w], act[:, kc, t * 128:(t + 1) * 128], wsb[wb][:, kc, 0:w],
                        start=(kc == 0), stop=(kc == nk - 1)),
                        reads=actkeys + [f"w{wb}"], writes=[f"ps{pb}"])
                normed = [(si, s) for si, s in enumerate(segs) if s["gain"]]
                for si, s in normed:
                    o, sw = s["off"], s["w"]
                    P.op("act", lambda e, o=o, sw=sw, si=si, pb=pb: e.activation(
                        sq[:, o:o + sw], ps[pb][:, o:o + sw], AF.Square, scale=float(sw) ** -0.5,
                        accum_out=ss[pb][:, si:si + 1]),
                        excl=[f"ps{pb}"], writes=[f"sq{si}", f"ss{pb}_{si}"])
                if normed:
                    ns = len(segs)
                    P.op("act", lambda e, pb=pb, ns=ns: e.activation(rstd[pb][:, 0:ns], ss[pb][:, 0:ns], AF.Sqrt,
                                                                     bias=epsb[:, 0:1]),
                         reads=[f"ss{pb}_{si}" for si, _ in normed] + ["c_eps"], writes=[f"rstd{pb}"])
                    P.op("dve", lambda e, pb=pb, ns=ns: e.reciprocal(rstd[pb][:, 0:ns], rstd[pb][:, 0:ns]),
                         reads=[f"rstd{pb}"], writes=[f"rstd{pb}"])
                for si, s in enumerate(segs):
                    o, sw = s["off"], s["w"]
                    if s["gain"]:
                        gr = grep[s["gain"]]
                        P.op("dve", lambda e, o=o, sw=sw, si=si, pb=pb, gr=gr: e.scalar_tensor_tensor(
                            qn[pb][:, o:o + sw], ps[pb][:, o:o + sw], rstd[pb][:, si:si + 1], gr[:, 0:sw],
                            ALU.mult, ALU.mult),
                            reads=[f"rstd{pb}", "g_" + s["gain"]], excl=[f"ps{pb}"], writes=[f"qn{pb}_{si}"])
                        P.op("act", lambda e, o=o, sw=sw, pb=pb: e.activation(qb[pb][:, o:o + sw], qn[pb][:, o:o + sw],
                                                                              AF.Copy),
                             reads=[f"qn{pb}_{si}"], writes=[f"qb{pb}_{si}"])
                    else:
                        P.op("act", lambda e, o=o, sw=sw, pb=pb: e.activation(qb[pb][:, o:o + sw], ps[pb][:, o:o + sw],
                                                                              AF.Copy),
                             excl=[f"ps{pb}"], writes=[f"qb{pb}_{si}"])
                    if s["rot"]:
                        R = s["rot"]
                        half = R // 2
                        tb = tabs[R]
                        x1 = qn[pb][:, o:o + half]
                        x2 = qn[pb][:, o + half:o + R]
                        cs = tb["cos"][:, t, :]
                        sn = tb["sin"][:, t, :]
                        r = [rt[k][:, 0:half] for k in range(4)]
                        rk = [f"rt{k}" for k in range(4)]
                        tk = [f"cos{R}", f"sin{R}", f"qn{pb}_{si}"]
                        P.op("pool", lambda e, r=r, x1=x1, cs=cs: e.tensor_tensor(r[0], x1, cs, ALU.mult),
                             reads=tk, writes=[rk[0]])
                        P.op("pool", lambda e, r=r, x2=x2, sn=sn: e.tensor_tensor(r[1], x2, sn, ALU.mult),
                             reads=tk, writes=[rk[1]])
                        P.op("pool", lambda e, r=r, x2=x2, cs=cs: e.tensor_tensor(r[2], x2, cs, ALU.mult),
                             reads=tk, writes=[rk[2]])
                        P.op("pool", lambda e, r=r, x1=x1, sn=sn: e.tensor_tensor(r[3], x1, sn, ALU.mult),
                             reads=tk, writes=[rk[3]])
                        P.op("dve", lambda e, r=r, o=o, half=half, pb=pb: e.tensor_tensor(
                            qb[pb][:, o:o + half], r[0], r[1], ALU.subtract),
                            reads=[rk[0], rk[1]], writes=[f"qb{pb}_{si}"])
                        P.op("dve", lambda e, r=r, o=o, half=half, R=R, pb=pb: e.tensor_tensor(
                            qb[pb][:, o + half:o + R], r[2], r[3], ALU.add),
                            reads=[rk[2], rk[3]], writes=[f"qb{pb}_{si}"])
                    if s["kind"] == "M":
                        mdst = s["dst"](t) if callable(s["dst"]) else s["dst"][t * 128:(t + 1) * 128, :]
                        P.dma("sp", mdst, qb[pb][:, o:o + sw],
                              reads=[f"qb{pb}_{si}"], writes=[f"{tag}M{bi}_{si}_{t}"])
                import os
                if tch and not os.environ.get("SKIP_TR"):
                    for (si, ci, co, wc, slot) in tch:
                        P.op("pe", lambda e, co=co, wc=wc, slot=slot, pb=pb: e.transpose(
                            pT[pb][0:wc, slot * 128:(slot + 1) * 128], qb[pb][:, co:co + wc], ident[:]),
                            reads=[f"qb{pb}_{si}", "ident"], writes=[f"pT{pb}"])
                    for (si, ci, co, wc, slot) in tch:
                        if pb == 0:
                            P.op("act", lambda e, wc=wc, slot=slot, pb=pb, t=t, sg=sg: e.activation(
                                sg[0:wc, slot, t * 128:(t + 1) * 128], pT[pb][0:wc, slot * 128:(slot + 1) * 128], AF.Copy),
                                excl=[f"pT{pb}"], writes=[f"stg{bi % 2}_{slot}_{t}"])
                        else:
                            P.op("dve", lambda e, wc=wc, slot=slot, pb=pb, t=t, sg=sg: e.tensor_copy(
                                sg[0:wc, slot, t * 128:(t + 1) * 128], pT[pb][0:wc, slot * 128:(slot + 1) * 128]),
                                excl=[f"pT{pb}"], writes=[f"stg{bi % 2}_{slot}_{t}"])
            import os
            for (si, ci, co, wc, slot) in tch:
                if os.environ.get("SKIP_TDMA"):
                    continue
                P.dma("sp", segs[si]["dst"][ci], sg[0:wc, slot, :],
                      reads=[f"stg{bi % 2}_{slot}_{t}" for t in range(NT)], writes=[f"{tag}T{bi}_{slot}"])
        P.emit()


def _dram_in(nc, name, shape, dt):
    return nc.dram_tensor(name, list(shape), dt, kind="ExternalInput").ap()


def _dram_out(nc, name, shape, dt):
    return nc.dram_tensor(name, list(shape), dt, kind="ExternalOutput").ap()


def k1_body(P, nc, mixer, TOK, x, g, pos, Wd, Gd, out, hTd, scratch):
    D = D_MODEL
    phase_norm(P, nc, TOK, x, g, hTd)
    if mixer in ("moba", "diff"):
        for wname, gname, dst in (("wq", "gq", out["qT"]), ("wk", "gk", out["kT"])):
            blocks = []
            for b in range(4):
                segs = [dict(off=j * 128, w=128, gain=gname, rot=32, kind="T", dst=[dst[b * 4 + j]]) for j in range(4)]
                blocks.append((b * 512, 512, segs))
            phase_proj(P, nc, TOK, hTd, 16, 128, Wd[wname], blocks, Gd, pos, wname)
        blocks = [(b * 512, 512, [dict(off=0, w=512, gain=None, rot=0, kind="M", dst=out["v"][:, b * 512:(b + 1) * 512])])
                  for b in range(4)]
        phase_proj(P, nc, TOK, hTd, 16, 128, Wd["wv"], blocks, Gd, pos, "wv")
    elif mixer == "swa":
        blocks = []
        for b in range(4):
            segs = [dict(off=j * 64, w=64, gain="gq", rot=16, kind="T", dst=[out["qT"][b * 8 + j]]) for j in range(8)]
            blocks.append((b * 512, 512, segs))
        blocks2 = []
        for b in range(8):
            segs = [dict(off=j * 64, w=64, gain="gq", rot=16, kind="T", dst=[out["qT"][b * 4 + j]]) for j in range(4)]
            blocks2.append((b * 256, 256, segs))
        phase_proj(P, nc, TOK, hTd, 16, 128, Wd["wq"], blocks2, Gd, pos, "wq")
        segs = [dict(off=j * 64, w=64, gain="gk", rot=16, kind="T", dst=[out["kT"][j]]) for j in range(4)]
        phase_proj(P, nc, TOK, hTd, 16, 128, Wd["wk"], [(0, 256, segs)], Gd, pos, "wk")
        phase_proj(P, nc, TOK, hTd, 16, 128, Wd["wv"],
                   [(0, 256, [dict(off=0, w=256, gain=None, rot=0, kind="M", dst=out["v"])])], Gd, pos, "wv")
    elif mixer == "mla":
        cqT, ckvT = scratch["cqT"], scratch["ckvT"]
        phase_proj(P, nc, TOK, hTd, 16, 128, Wd["wq_a"],
                   [(0, 512, [dict(off=0, w=512, gain="g_qa", rot=0, kind="T", dst=[cqT[c] for c in range(4)])])],
                   Gd, pos, "wqa")
        phase_proj(P, nc, TOK, hTd, 16, 128, Wd["wkv_a"],
                   [(0, 512, [dict(off=0, w=512, gain="g_kva", rot=0, kind="T", dst=[ckvT[c] for c in range(4)])]),
                    (512, 64, [dict(off=0, w=64, gain="g_kr", rot=64, kind="T", dst=[out["krT"]])])],
                   Gd, pos, "wkva")
        blocks = []
        for b in range(8):
            segs = []
            for j in range(2):
                h = b * 2 + j
                segs.append(dict(off=j * 192, w=128, gain="g_qn", rot=0, kind="T", dst=[out["qnT"][h]]))
                segs.append(dict(off=j * 192 + 128, w=64, gain="g_qr", rot=64, kind="T", dst=[out["qrT"][h]]))
            blocks.append((b * 384, 384, segs))
        phase_proj(P, nc, TOK, cqT, 4, 128, Wd["wq_b"], blocks, Gd, pos, "wqb")
        blocks = []
        for b in range(8):
            segs = []
            for j in range(2):
                h = b * 2 + j
                segs.append(dict(off=j * 256, w=128, gain="g_kn", rot=0, kind="T", dst=[out["knT"][h]]))
                segs.append(dict(off=j * 256 + 128, w=128, gain=None, rot=0, kind="M",
                                 dst=out["v"][:, h * 128:(h + 1) * 128]))
            blocks.append((b * 512, 512, segs))
        phase_proj(P, nc, TOK, ckvT, 4, 128, Wd["wkv_b"], blocks, Gd, pos, "wkvb")


K1_SPECS = {
    "moba": dict(W=dict(wq=(2048, 2048), wk=(2048, 2048), wv=(2048, 2048)), G=dict(gq=128, gk=128),
                 out=dict(qT=(16, 128), kT=(16, 128), v=2048)),
    "diff": dict(W=dict(wq=(2048, 2048), wk=(2048, 2048), wv=(2048, 2048)), G=dict(gq=128, gk=128),
                 out=dict(qT=(16, 128), kT=(16, 128), v=2048)),
    "swa": dict(W=dict(wq=(2048, 2048), wk=(2048, 256), wv=(2048, 256)), G=dict(gq=64, gk=64),
                out=dict(qT=(32, 64), kT=(4, 64), v=256)),
    "mla": dict(W=dict(wq_a=(2048, 512), wkv_a=(2048, 576), wq_b=(512, 3072), wkv_b=(512, 4096)),
                G=dict(g_qa=512, g_kva=512, g_qn=128, g_kn=128, g_qr=64, g_kr=64),
                out=dict(qnT=(16, 128), qrT=(16, 64), knT=(16, 128), krT=(64,), v=2048)),
}


def build_k1(mixer, TOK):
    nc = bass.Bass("TRN2", target_bir_lowering=False)
    D = D_MODEL
    NT = TOK // 128
    spec = K1_SPECS[mixer]
    x = _dram_in(nc, "x", [TOK, D], F32)
    g = _dram_in(nc, "g", [1, D], F32)
    pos = _dram_in(nc, "pos", [128, NT], F32)
    Wd = {k: _dram_in(nc, k, shp, F32) for k, shp in spec["W"].items()}
    Gd = {k: _dram_in(nc, k, [1, w], F32) for k, w in spec["G"].items()}
    out = {}
    for k, shp in spec["out"].items():
        if k == "v":
            out[k] = _dram_out(nc, k, [TOK, shp], BF16)
        elif len(shp) == 1:
            out[k] = _dram_out(nc, k, [shp[0], TOK], BF16)
        else:
            out[k] = _dram_out(nc, k, [shp[0], shp[1], TOK], BF16)
    hTd = nc.dram_tensor("hTd", [16, 128, TOK], BF16).ap()
    scratch = {}
    if mixer == "mla":
        scratch["cqT"] = nc.dram_tensor("cqT", [4, 128, TOK], BF16).ap()
        scratch["ckvT"] = nc.dram_tensor("ckvT", [4, 128, TOK], BF16).ap()
    P = Prog(nc)
    k1_body(P, nc, mixer, TOK, x, g, pos, Wd, Gd, out, hTd, scratch)
    P.close()
    return nc


def phase_attn(P, nc, kind, U, S, io, lam_init=0.0):
    NKT = S // 128
    NQB = S // 512
    nmap = 2 if kind == "diff" else 1
    ndv = 2 if kind == "diff" else 1
    dv = 128 * ndv
    scale = {"moba": 128 ** -0.5, "mla": 192 ** -0.5, "diff": 128 ** -0.5}[kind]
    with contextlib.ExitStack() as es:
        T = make_T(es, nc)
        qA = [T(f"a_qA{m}", [128, S], BF16) for m in range(nmap)]
        kA = [T(f"a_kA{m}", [128, S], BF16) for m in range(nmap)]
        vS = T("a_v", [128, NKT, dv], BF16)
        if kind == "mla":
            qB = T("a_qB", [64, S], BF16)
            kB = T("a_kB", [64, S], BF16)
        ones = T("a_ones", [128, 128], BF16)
        mask = T("a_mask", [128, 4, 512], BF16)
        maskf = T("a_maskf", [128, 4, 512], F32)
        NPT = 3
        Pt = [[T(f"a_P{m}_{i}", [128, 512], BF16) for i in range(NPT)] for m in range(nmap)]
        Rr = [T(f"a_R{m}", [128, 512], F32) for m in range(nmap)]
        osb = [T(f"a_osb{i}", [128, 512], BF16) for i in range(2 * ndv)]
        S_ps = [psum_T(es, nc, f"a_S{i}", [128, 512], F32) for i in range(2)]
        if kind == "diff":
            O_ps = [[[psum_T(es, nc, f"a_O{m}{d}", [128, 512], F32) for d in range(2)] for m in range(2)]]
            L_ps = [[psum_T(es, nc, f"a_L{m}", [128, 512], F32) for m in range(2)]]
            nob = 1
        else:
            O_ps = [[[psum_T(es, nc, f"a_O{b}", [128, 512], F32)]] for b in range(2)]
            L_ps = [[psum_T(es, nc, f"a_L{b}", [128, 512], F32)] for b in range(2)]
            nob = 2
        P.op("pool", lambda e: e.memset(ones[:], 1.0), writes=["ones"])
        P.op("pool", lambda e: e.iota(maskf[:], [[-128, 4], [1, 512]], base=0, channel_multiplier=-1,
                                      allow_small_or_imprecise_dtypes=True), writes=["maskf"])
        P.op("dve", lambda e: e.tensor_single_scalar(mask[:], maskf[:], 0.0, ALU.is_ge),
             reads=["maskf"], writes=["mask"])
        if kind == "moba":
            NB = S // 256
            ident = T("a_ident", [128, 128], BF16)
            build_identity(P, nc, ident, es)
            Eall = T("a_Eall", [NB, NB, 128], BF16)
            Ef = T("a_Ef", [NB, NB, 128], F32)
            biasT = T("a_biasT", [NB, S], BF16)
            km = T("a_km", [128, NB], F32)
            kmh = T("a_kmh", [128, NB], BF16)
            kmhf = T("a_kmhf", [128, NB], F32)
            kml = T("a_kml", [128, NB], BF16)
            gpad = T("a_gpad", [128, NB], F32)
            m8 = T("a_m8", [128, 8], F32)
            brow = [T(f"a_brow{i}", [128, NB], BF16) for i in range(2)]
            g_ps = psum_T(es, nc, "a_gps", [128, 512], F32)
            b_ps = psum_T(es, nc, "a_bps", [128, 1024], BF16)
            P.op("pool", lambda e: e.iota(Ef[:], [[-1, NB], [0, 128]], base=0, channel_multiplier=1,
                                          allow_small_or_imprecise_dtypes=True), writes=["Ef"])
            P.op("dve", lambda e: e.tensor_single_scalar(Eall[:], Ef[:], 0.0, ALU.is_equal),
                 reads=["Ef"], writes=["Eall"])
        if kind == "diff":
            lv = {n: T("a_" + n, [128, 128], F32) for n in ("lq1", "lk1", "lq2", "lk2")}
            ltmp = T("a_ltmp", [128, 128], F32)
            lsum = T("a_lsum", [128, 2], F32)
            lam = T("a_lam", [128, 1], F32)
            gsub = T("a_gsub", [128, 2], F32)
            epsb = T("a_epsb", [128, 1], F32)
            od = [T(f"a_od{d}", [128, 512], F32) for d in range(2)]
            t2 = T("a_t2", [128, 512], F32)
            sqb = [T(f"a_sqb{d}", [128, 512], BF16) for d in range(2)]
            rs = T("a_rs", [128, 512], F32)
            P.op("pool", lambda e: e.memset(epsb[:], EPS), writes=["c_eps"])
            for n in lv:
                P.dma("sp", lv[n][:], io[n][0:1, :].partition_broadcast(128), writes=[n])
            P.dma("sp", gsub[:], io["gsub"][:, :], writes=["gsub"])
            for i, (a, b) in enumerate((("lq1", "lk1"), ("lq2", "lk2"))):
                P.op("dve", lambda e, a=a, b=b: e.tensor_tensor(ltmp[:], lv[a][:], lv[b][:], ALU.mult),
                     reads=[a, b], writes=["ltmp"])
                P.op("dve", lambda e, i=i: e.reduce_sum(lsum[:, i:i + 1], ltmp[:], AX.X),
                     reads=["ltmp"], writes=[f"lsum{i}"])
            P.op("act", lambda e: e.activation(lsum[:], lsum[:], AF.Exp), reads=["lsum0", "lsum1"], writes=["lsum"])
            P.op("dve", lambda e: e.tensor_tensor(lam[:], lsum[:, 0:1], lsum[:, 1:2], ALU.subtract),
                 reads=["lsum"], writes=["lam"])
            P.op("dve", lambda e: e.tensor_scalar(lam[:], lam[:], lam_init, None, ALU.add),
                 reads=["lam"], writes=["lam"])
            P.op("dve", lambda e: e.tensor_scalar(gsub[:], gsub[:], 1.0 - lam_init, None, ALU.mult),
                 reads=["gsub"], writes=["gsub"])

        npt = 0
        nsb = 0
        nqb_glob = 0
        for u in range(U):
            for m in range(nmap):
                qsrc = io["qA"][u, m] if kind == "diff" else io["qA"][u]
                ksrc = io["kA"][u, m] if kind == "diff" else io["kA"][u]
                for h in range(4):
                    cs = slice(h * (S // 4), (h + 1) * (S // 4))
                    P.dma("sp", qA[m][:, cs], qsrc[:, cs], writes=[f"qA{m}_{h}"])
                    P.dma("sp", kA[m][:, cs], ksrc[:, cs], writes=[f"kA{m}_{h}"])
            qkeys = [[f"qA{m}_{h}" for h in range(4)] for m in range(nmap)]
            kkeys = [[f"kA{m}_{h}" for h in range(4)] for m in range(nmap)]
            for h in range(4):
                ts = slice(h * (NKT // 4), (h + 1) * (NKT // 4))
                P.dma("sp", vS[:, ts, :], io["v"][u][:, ts, :], writes=[f"v_{h}"])
            vkeys = [f"v_{h}" for h in range(4)]
            if kind == "mla":
                P.dma("sp", qB[:], io["qB"][u], writes=["qB"])
                P.dma("sp", kB[:], io["kB"][u], writes=["kB"])
            if kind == "moba":
                P.op("pool", lambda e: e.memset(biasT[:], 0.0), writes=["biasT"])
                P.op("pool", lambda e: e.memset(gpad[:], -1e30), writes=["gpad"])
                for i in range(2):
                    P.op("pool", lambda e, i=i: e.memset(brow[i][:], 0.0), writes=[f"brow{i}"])
                P.op("dve", lambda e: e.reduce_sum(km[:], kA[0][:].rearrange("p (b k) -> p b k", k=256), AX.X),
                     reads=kkeys[0], writes=["km"])
                P.op("dve", lambda e: e.tensor_scalar(km[:], km[:], 1.0 / 256, None, ALU.mult), reads=["km"], writes=["km"])
                P.op("dve", lambda e: e.tensor_copy(kmh[:], km[:]), reads=["km"], writes=["kmh"])
                P.op("dve", lambda e: e.tensor_copy(kmhf[:], kmh[:]), reads=["kmh"], writes=["kmhf"])
                P.op("dve", lambda e: e.tensor_tensor(kml[:], km[:], kmhf[:], ALU.subtract),
                     reads=["km", "kmhf"], writes=["kml"])
                for qt in range(NKT):
                    ob = qt // 2
                    if ob <= 3 or os.environ.get("SKIP_GATE"):
                        continue
                    bi = qt % 2
                    P.op("pe", lambda e, qt=qt: e.matmul(g_ps[:, 0:NB], qA[0][:, qt * 128:(qt + 1) * 128], kmh[:],
                                                         start=True, stop=False),
                         reads=qkeys[0] + ["kmh"], writes=["g_ps"])
                    P.op("pe", lambda e, qt=qt: e.matmul(g_ps[:, 0:NB], qA[0][:, qt * 128:(qt + 1) * 128], kml[:],
                                                         start=False, stop=True),
                         reads=qkeys[0] + ["kml"], writes=["g_ps"])
                    P.op("dve", lambda e, ob=ob: e.tensor_copy(gpad[:, 0:ob], g_ps[:, 0:ob]),
                         excl=["g_ps"], writes=["gpad"])
                    P.op("dve", lambda e: e.max(m8[:], gpad[:]), reads=["gpad"], writes=["m8"])
                    P.op("dve", lambda e, ob=ob, bi=bi: e.tensor_scalar(brow[bi][:, 0:ob], gpad[:, 0:ob], m8[:, 2:3],
                                                                        -1000.0, ALU.is_lt, ALU.mult),
                         reads=["gpad", "m8"], writes=[f"brow{bi}"])
                    P.op("pe", lambda e, bi=bi: e.transpose(b_ps[0:NB, 0:128], brow[bi][:], ident[:]),
                         reads=[f"brow{bi}", "ident"], writes=["b_ps"])
                    P.op("act", lambda e, qt=qt: e.activation(biasT[:, qt * 128:(qt + 1) * 128], b_ps[0:NB, 0:128], AF.Copy),
                         excl=["b_ps"], writes=["biasT"])
            for qb in range(NQB):
                ob_i = nqb_glob % nob
                nqb_glob += 1
                qcols = slice(qb * 512, (qb + 1) * 512)
                nkt = 4 * (qb + 1)
                Okey = [[f"O{ob_i}_{m}_{d}" for d in range(ndv)] for m in range(nmap)]
                Lkey = [f"L{ob_i}_{m}" for m in range(nmap)]
                for kt in range(nkt):
                    j = kt - 4 * qb
                    kcols = slice(kt * 128, (kt + 1) * 128)
                    kh = (kt * 128) // (S // 4)
                    qh = (qb * 512) // (S // 4)
                    for m in range(nmap):
                        sb = nsb % 2
                        nsb += 1
                        pi = npt % NPT
                        npt += 1
                        Sp = S_ps[sb]
                        Pm = Pt[m][pi]
                        last_s = (kind == "diff")
                        P.op("pe", lambda e, m=m, Sp=Sp, kcols=kcols, qcols=qcols, last_s=last_s: e.matmul(
                            Sp[:], kA[m][:, kcols], qA[m][:, qcols], start=True, stop=last_s or bool(os.environ.get("SKIP_BIAS"))),
                            reads=[f"kA{m}_{kh}", f"qA{m}_{qh}"], writes=[f"S{sb}"])
                        if kind == "mla":
                            P.op("pe", lambda e, Sp=Sp, kcols=kcols, qcols=qcols: e.matmul(
                                Sp[:], kB[:, kcols], qB[:, qcols], start=False, stop=True),
                                reads=["kB", "qB"], writes=[f"S{sb}"])
                        if kind == "moba" and not os.environ.get("SKIP_BIAS"):
                            blk = kt // 2
                            P.op("pe", lambda e, Sp=Sp, blk=blk, qcols=qcols: e.matmul(
                                Sp[:], Eall[:, blk, :], biasT[:, qcols], start=False, stop=True),
                                reads=["Eall", "biasT"], writes=[f"S{sb}"])
                        P.op("act", lambda e, Sp=Sp, Pm=Pm: e.activation(Pm[:], Sp[:], AF.Exp, scale=scale),
                             excl=[f"S{sb}"], writes=[f"P{m}_{pi}"])
                        if j >= 0:
                            P.op("pool", lambda e, Pm=Pm, j=j: e.tensor_tensor(Pm[:], Pm[:], mask[:, j, :], ALU.mult),
                                 reads=["mask"], writes=[f"P{m}_{pi}"])
                        for d in range(ndv):
                            P.op("pe", lambda e, m=m, d=d, kt=kt, Pm=Pm, ob_i=ob_i, nkt=nkt: e.matmul(
                                O_ps[ob_i][m][d][:], vS[:, kt, d * 128:(d + 1) * 128], Pm[:],
                                start=(kt == 0), stop=(kt == nkt - 1)),
                                reads=[f"v_{kt // (NKT // 4)}", f"P{m}_{pi}"], writes=[Okey[m][d]])
                        P.op("pe", lambda e, m=m, kt=kt, Pm=Pm, ob_i=ob_i, nkt=nkt: e.matmul(
                            L_ps[ob_i][m][:], ones[:], Pm[:], start=(kt == 0), stop=(kt == nkt - 1)),
                            reads=["ones", f"P{m}_{pi}"], writes=[Lkey[m]])
                if kind != "diff":
                    ob_sb = osb[ob_i]
                    P.op("dve", lambda e, ob_i=ob_i: e.reciprocal(Rr[0][:], L_ps[ob_i][0][:]),
                         excl=[Lkey[0]], writes=["R0"])
                    P.op("dve", lambda e, ob_i=ob_i, ob_sb=ob_sb: e.tensor_tensor(
                        ob_sb[:], O_ps[ob_i][0][0][:], Rr[0][:], ALU.mult),
                        reads=["R0"], excl=[Okey[0][0]], writes=[f"osb{ob_i}"])
                    P.dma("sp", io["oT"][u][:, qcols], ob_sb[:], reads=[f"osb{ob_i}"], writes=[f"oT{u}_{qb}"])
                else:
                    for m in range(2):
                        P.op("dve", lambda e, m=m: e.reciprocal(Rr[m][:], L_ps[0][m][:]),
                             excl=[Lkey[m]], writes=[f"R{m}"])
                    P.op("dve", lambda e: e.tensor_scalar(Rr[1][:], Rr[1][:], lam[:, 0:1], None, ALU.mult),
                         reads=["R1", "lam"], writes=["R1"])
                    for d in range(2):
                        P.op("dve", lambda e, d=d: e.tensor_tensor(od[d][:], O_ps[0][0][d][:], Rr[0][:], ALU.mult),
                             reads=["R0"], excl=[Okey[0][d]], writes=[f"od{d}"])
                        P.op("dve", lambda e, d=d: e.tensor_tensor(t2[:], O_ps[0][1][d][:], Rr[1][:], ALU.mult),
                             reads=["R1"], excl=[Okey[1][d]], writes=["t2"])
                        P.op("dve", lambda e, d=d: e.tensor_tensor(od[d][:], od[d][:], t2[:], ALU.subtract),
                             reads=["t2", f"od{d}"], writes=[f"od{d}"])
                        P.op("act", lambda e, d=d: e.activation(sqb[d][:], od[d][:], AF.Square),
                             reads=[f"od{d}"], writes=[f"sqb{d}"])
                    ssp = S_ps[0]
                    for d in range(2):
                        P.op("pe", lambda e, d=d, ssp=ssp: e.matmul(ssp[:], ones[:], sqb[d][:], start=(d == 0), stop=(d == 1)),
                             reads=["ones", f"sqb{d}"], writes=["S0"])
                    P.op("act", lambda e, ssp=ssp: e.activation(rs[:], ssp[:], AF.Sqrt, bias=epsb[:, 0:1], scale=1.0 / 256),
                         reads=["c_eps"], excl=["S0"], writes=["rs"])
                    P.op("dve", lambda e: e.reciprocal(rs[:], rs[:]), reads=["rs"], writes=["rs"])
                    for d in range(2):
                        ob_sb = osb[(nqb_glob % 2) * 2 + d]
                        okey = f"osb{(nqb_glob % 2) * 2 + d}"
                        P.op("dve", lambda e, d=d, ob_sb=ob_sb: e.scalar_tensor_tensor(
                            ob_sb[:], od[d][:], gsub[:, d:d + 1], rs[:], ALU.mult, ALU.mult),
                            reads=[f"od{d}", "gsub", "rs"], writes=[okey])
                        P.dma("sp", io["oT"][u][d * 128:(d + 1) * 128, qcols], ob_sb[:], reads=[okey],
                              writes=[f"oT{u}_{qb}_{d}"])
        P.emit()


def build_k2(kind, U, S, lam_init=0.0):
    nc = bass.Bass("TRN2", target_bir_lowering=False)
    NKT = S // 128
    io = {}
    if kind == "diff":
        io["qA"] = _dram_in(nc, "qA", [U, 2, 128, S], BF16)
        io["kA"] = _dram_in(nc, "kA", [U, 2, 128, S], BF16)
        io["v"] = _dram_in(nc, "v", [U, 128, NKT, 256], BF16)
        io["oT"] = _dram_out(nc, "oT", [U, 256, S], BF16)
        for n in ("lq1", "lk1", "lq2", "lk2"):
            io[n] = _dram_in(nc, n, [1, 128], F32)
        io["gsub"] = _dram_in(nc, "gsub", [128, 2], F32)
    else:
        io["qA"] = _dram_in(nc, "qA", [U, 128, S], BF16)
        io["kA"] = _dram_in(nc, "kA", [U, 128, S], BF16)
        io["v"] = _dram_in(nc, "v", [U, 128, NKT, 128], BF16)
        io["oT"] = _dram_out(nc, "oT", [U, 128, S], BF16)
        if kind == "mla":
            io["qB"] = _dram_in(nc, "qB", [U, 64, S], BF16)
            io["kB"] = _dram_in(nc, "kB", [U, 64, S], BF16)
    P = Prog(nc)
    phase_attn(P, nc, kind, U, S, io, lam_init)
    P.close()
    return nc


def phase_swa(P, nc, U, S, io):
    NT = S // 128
    scale = 64 ** -0.5
    with contextlib.ExitStack() as es:
        T = make_T(es, nc)
        qS = T("w_q", [64, 8, S], BF16)
        kS = T("w_k", [64, S], BF16)
        vS = T("w_v", [128, NT, 64], BF16)
        ones = T("w_ones", [128, 64], BF16)
        maskf = T("w_maskf", [128, 128], F32)
        mask2 = T("w_mask2", [128, 2, 256], BF16)
        esink = T("w_esink", [128, 8], F32)
        Pt = [T(f"w_P{i}", [128, 512], BF16) for i in range(3)]
        Lp = [T(f"w_Lp{i}", [64, 256], F32) for i in range(2)]
        ostg = [T(f"w_ostg{i}", [64, 8, 512], BF16) for i in range(2)]
        S_ps = [psum_T(es, nc, f"w_S{i}", [128, 512], F32) for i in range(2)]
        OL_ps = [psum_T(es, nc, f"w_OL{i}", [64, 512], F32) for i in range(2)]
        P.op("pool", lambda e: e.memset(ones[:], 1.0), writes=["ones"])
        P.op("pool", lambda e: e.iota(maskf[:], [[1, 128]], base=0, channel_multiplier=-1,
                                      allow_small_or_imprecise_dtypes=True), writes=["maskf"])
        for h in range(2):
            P.op("dve", lambda e, h=h: e.tensor_single_scalar(mask2[:, h, 0:128], maskf[:], 0.0, ALU.is_ge),
                 reads=["maskf"], writes=[f"mask2_{h}c"])
            P.op("dve", lambda e, h=h: e.tensor_single_scalar(mask2[:, h, 128:256], maskf[:], 0.0, ALU.is_lt),
                 reads=["maskf"], writes=[f"mask2_{h}p"])
        mkeys = ["mask2_0c", "mask2_0p", "mask2_1c", "mask2_1p"]
        cnt = 0
        for u in range(U):
            for h in range(8):
                P.dma("sp", qS[:, h, :], io["qT"][u, h], writes=[f"q{h}"])
            P.dma("sp", kS[:], io["kT"][u], writes=["k"])
            for h in range(4):
                ts = slice(h * (NT // 4), (h + 1) * (NT // 4))
                P.dma("sp", vS[:, ts, :], io["v"][u][:, ts, :], writes=[f"v{h}"])
            vkeys = [f"v{h}" for h in range(4)]
            P.dma("sp", esink[:], io["sinks"][u].partition_broadcast(128), writes=["esink"])
            P.op("act", lambda e: e.activation(esink[:], esink[:], AF.Exp), reads=["esink"], writes=["esink"])
            for n in range(NT):
                g = n // 4
                og = ostg[g % 2]
                tcols = slice(n * 128, (n + 1) * 128)
                pcols = slice((n - 1) * 128, n * 128)
                for hp in range(4):
                    sb = cnt % 2
                    pi = cnt % 3
                    cnt += 1
                    Sp, Pm, OL, Lq = S_ps[sb], Pt[pi], OL_ps[sb], Lp[sb]
                    for hh in range(2):
                        h = hp * 2 + hh
                        P.op("pe", lambda e, Sp=Sp, hh=hh, h=h, tcols=tcols: e.matmul(
                            Sp[:, hh * 256:hh * 256 + 128], kS[:, tcols], qS[:, h, tcols], start=True, stop=True),
                            reads=["k", f"q{h}"], writes=[f"S{sb}"])
                        if n > 0:
                            P.op("pe", lambda e, Sp=Sp, hh=hh, h=h, tcols=tcols, pcols=pcols: e.matmul(
                                Sp[:, hh * 256 + 128:hh * 256 + 256], kS[:, pcols], qS[:, h, tcols], start=True, stop=True),
                                reads=["k", f"q{h}"], writes=[f"S{sb}"])
                    if n > 0:
                        P.op("act", lambda e, Sp=Sp, Pm=Pm: e.activation(Pm[:], Sp[:], AF.Exp, scale=scale),
                             excl=[f"S{sb}"], writes=[f"P{pi}"])
                        P.op("pool", lambda e, Pm=Pm: e.tensor_tensor(
                            Pm[:], Pm[:], mask2[:].rearrange("p h c -> p (h c)"), ALU.mult),
                            reads=mkeys, writes=[f"P{pi}"])
                    else:
                        P3 = Pm[:].rearrange("p (h c) -> p h c", h=2)[:, :, 0:128]
                        S3 = Sp[:].rearrange("p (h c) -> p h c", h=2)[:, :, 0:128]
                        P.op("act", lambda e, S3=S3, P3=P3: e.activation(P3, S3, AF.Exp, scale=scale),
                             excl=[f"S{sb}"], writes=[f"P{pi}"])
                        P.op("pool", lambda e, P3=P3: e.tensor_tensor(P3, P3, mask2[:, :, 0:128], ALU.mult),
                             reads=mkeys, writes=[f"P{pi}"])
                    for hh in range(2):
                        for (dst0, lhs_c, lhs_p) in ((hh * 128, vS[:, n, :], vS[:, n - 1, :] if n > 0 else None),
                                                     (256 + hh * 128, ones[:], ones[:] if n > 0 else None)):
                            P.op("pe", lambda e, OL=OL, dst0=dst0, lhs_c=lhs_c, Pm=Pm, hh=hh, n=n: e.matmul(
                                OL[:, dst0:dst0 + 128], lhs_c, Pm[:, hh * 256:hh * 256 + 128], start=True, stop=(n == 0)),
                                reads=vkeys + ["ones", f"P{pi}"], writes=[f"OL{sb}"])
                            if n > 0:
                                P.op("pe", lambda e, OL=OL, dst0=dst0, lhs_p=lhs_p, Pm=Pm, hh=hh: e.matmul(
                                    OL[:, dst0:dst0 + 128], lhs_p, Pm[:, hh * 256 + 128:hh * 256 + 256],
                                    start=False, stop=True),
                                    reads=vkeys + ["ones", f"P{pi}"], writes=[f"OL{sb}"])
                    for hh in range(2):
                        h = hp * 2 + hh
                        P.op("dve", lambda e, OL=OL, Lq=Lq, hh=hh, h=h: e.tensor_scalar(
                            Lq[:, hh * 128:(hh + 1) * 128], OL[:, 256 + hh * 128:256 + (hh + 1) * 128],
                            esink[0:64, h:h + 1], None, ALU.add),
                            reads=["esink"], excl=[f"OL{sb}"], writes=[f"Lp{sb}_{hh}"])
                    P.op("dve", lambda e, Lq=Lq: e.reciprocal(Lq[:], Lq[:]),
                         reads=[f"Lp{sb}_0", f"Lp{sb}_1"], writes=[f"Lp{sb}_0", f"Lp{sb}_1"])
                    lc = (n % 4) * 128
                    P.op("dve", lambda e, OL=OL, Lq=Lq, og=og, hp=hp, lc=lc: e.tensor_tensor(
                        og[:, hp * 2:hp * 2 + 2, lc:lc + 128], OL[:, 0:256].rearrange("p (h c) -> p h c", h=2),
                        Lq[:].rearrange("p (h c) -> p h c", h=2), ALU.mult),
                        reads=[f"Lp{sb}_0", f"Lp{sb}_1"], excl=[f"OL{sb}"], writes=[f"ostg{g % 2}_{hp}_{n % 4}"])
                if n % 4 == 3:
                    P.dma("sp", io["oT"][u][:, :, g * 512:(g + 1) * 512].rearrange("h p t -> p h t"), og[:],
                          reads=[f"ostg{g % 2}_{hp}_{k}" for hp in range(4) for k in range(4)],
                          writes=[f"oT{u}_{g}"])
        P.emit()


def build_k2_swa(U, S):
    nc = bass.Bass("TRN2", target_bir_lowering=False)
    io = dict(qT=_dram_in(nc, "qT", [U, 8, 64, S], BF16), kT=_dram_in(nc, "kT", [U, 64, S], BF16),
              v=_dram_in(nc, "v", [U, 128, S // 128, 64], BF16), sinks=_dram_in(nc, "sinks", [U, 1, 8], F32),
              oT=_dram_out(nc, "oT", [U, 8, 64, S], BF16))
    P = Prog(nc)
    phase_swa(P, nc, U, S, io)
    P.close()
    return nc


MIXERS = ("moba", "mla", "swa", "diff")
_NC_CACHE = {}


def _get_nc(key, builder):
    if key not in _NC_CACHE:
        _NC_CACHE[key] = builder()
    return _NC_CACHE[key]


def _run(nc, in_maps):
    res = run_bass_kernel_spmd(nc, in_maps, core_ids=list(range(NCORES)))
    return res.results


def _f32(a):
    return np.ascontiguousarray(np.asarray(a, dtype=np.float32))


def _v_units(vfull_b, col0, dv, S):
    a = vfull_b[:, col0:col0 + dv]
    return np.ascontiguousarray(a.reshape(S // 128, 128, dv).transpose(1, 0, 2))


def run_model(inp, B, S):
    D = D_MODEL
    TOK = B * S // NCORES
    CPB = NCORES // B
    NT = TOK // 128
    x = _f32(inp["x"]).reshape(B * S, D)
    xs = [np.ascontiguousarray(x[c * TOK:(c + 1) * TOK]) for c in range(NCORES)]
    pos = []
    for c in range(NCORES):
        p = ((c % CPB) * TOK + np.arange(TOK)).astype(np.float32)
        pos.append(np.ascontiguousarray(p.reshape(NT, 128).T))
    for k1kind in ("moba", "swa", "mla"):
        _get_nc(("k1", k1kind, TOK), lambda: build_k1(k1kind, TOK))
    for mx in ("moba", "mla"):
        _get_nc(("k2", mx, B * 16 // NCORES, S), lambda: build_k2(mx, B * 16 // NCORES, S))
    _get_nc(("k2swa", S), lambda: build_k2_swa(1, S))
    _get_nc(("k2diff", B * 8 // NCORES, S, 3), lambda: build_k2("diff", B * 8 // NCORES, S,
                                                                0.8 - 0.6 * math.exp(-0.3 * 3)))
    _get_nc(("k3", TOK, 128), lambda: build_k3(TOK, 128))
    for layer in range(4):
        mixer = MIXERS[layer % 4]
        j = layer // 4
        if mixer == "moba":
            W = dict(wq=inp["moba_wq"][j], wk=inp["moba_wk"][j], wv=inp["moba_wv"][j])
            G = dict(gq=inp["moba_gq"][j], gk=inp["moba_gk"][j])
        elif mixer == "diff":
            W = dict(wq=inp["diff_wq"][j], wk=inp["diff_wk"][j], wv=inp["diff_wv"][j])
            G = dict(gq=inp["diff_gq"][j], gk=inp["diff_gk"][j])
        elif mixer == "swa":
            W = dict(wq=inp["swa_wq"][j], wk=inp["swa_wk"][j], wv=inp["swa_wv"][j])
            G = dict(gq=inp["swa_gq"][j], gk=inp["swa_gk"][j])
        else:
            W = dict(wq_a=inp["mla_wq_a"][j], wkv_a=inp["mla_wkv_a"][j], wq_b=inp["mla_wq_b"][j],
                     wkv_b=inp["mla_wkv_b"][j])
            G = dict(g_qa=inp["mla_g_qa"][j], g_kva=inp["mla_g_kva"][j], g_qn=inp["mla_g_qn"][j],
                     g_kn=inp["mla_g_kn"][j], g_qr=inp["mla_g_qr"][j], g_kr=inp["mla_g_kr"][j])
        W = {k: _f32(v) for k, v in W.items()}
        G = {k: _f32(v).reshape(1, -1) for k, v in G.items()}
        k1kind = "moba" if mixer == "diff" else mixer
        nc1 = _get_nc(("k1", k1kind, TOK), lambda: build_k1(k1kind, TOK))
        g_attn = _f32(inp["attn_norm"][layer]).reshape(1, D)
        maps = []
        for c in range(NCORES):
            m = dict(x=xs[c], g=g_attn, pos=pos[c])
            m.update(W)
            m.update(G)
            maps.append(m)
        r1 = _run(nc1, maps)

        def cat(name, b):
            return np.concatenate([np.asarray(r1[b * CPB + cc][name]) for cc in range(CPB)], axis=-1)

        def catv(b):
            return np.concatenate([np.asarray(r1[b * CPB + cc]["v"]) for cc in range(CPB)], axis=0)

        if mixer in ("moba", "mla"):
            H = 16
            U = B * H // NCORES
            qn, kn = ("qT", "kT") if mixer == "moba" else ("qnT", "knT")
            qf = [cat(qn, b) for b in range(B)]
            kf = [cat(kn, b) for b in range(B)]
            vf = [catv(b) for b in range(B)]
            if mixer == "mla":
                qrf = [cat("qrT", b) for b in range(B)]
                krf = [cat("krT", b) for b in range(B)]
            maps = []
            for c in range(NCORES):
                units = [divmod(c * U + u, H) for u in range(U)]
                m = dict(qA=np.stack([qf[b][h] for b, h in units]), kA=np.stack([kf[b][h] for b, h in units]),
                         v=np.stack([_v_units(vf[b], h * 128, 128, S) for b, h in units]))
                if mixer == "mla":
                    m["qB"] = np.stack([qrf[b][h] for b, h in units])
                    m["kB"] = np.stack([krf[b] for b, h in units])
                maps.append(m)
            nc2 = _get_nc(("k2", mixer, U, S), lambda: build_k2(mixer, U, S))
            r2 = _run(nc2, maps)
            ofull = np.zeros((B, H, 128, S), NPBF16)
            for c in range(NCORES):
                o = np.asarray(r2[c]["oT"])
                for u in range(U):
                    b, h = divmod(c * U + u, H)
                    ofull[b, h] = o[u]
            ko = 128
            ochunks = ofull
        elif mixer == "swa":
            U = 1
            qf = [cat("qT", b) for b in range(B)]
            kf = [cat("kT", b) for b in range(B)]
            vf = [catv(b) for b in range(B)]
            sinks = _f32(inp["swa_sinks"][j])
            maps = []
            for c in range(NCORES):
                b, kvh = divmod(c, 4)
                maps.append(dict(qT=np.ascontiguousarray(qf[b][kvh * 8:(kvh + 1) * 8][None]),
                                 kT=np.ascontiguousarray(kf[b][kvh][None]),
                                 v=_v_units(vf[b], kvh * 64, 64, S)[None],
                                 sinks=np.ascontiguousarray(sinks[kvh * 8:(kvh + 1) * 8].reshape(1, 1, 8))))
            nc2 = _get_nc(("k2swa", S), lambda: build_k2_swa(1, S))
            r2 = _run(nc2, maps)
            ochunks = np.zeros((B, 32, 64, S), NPBF16)
            for c in range(NCORES):
                b, kvh = divmod(c, 4)
                ochunks[b, kvh * 8:(kvh + 1) * 8] = np.asarray(r2[c]["oT"])[0]
            ochunks = ochunks.reshape(B, 16, 128, S)
            ko = 128
        else:
            H = 8
            U = B * H // NCORES
            lam_init = 0.8 - 0.6 * math.exp(-0.3 * layer)
            qf = [cat("qT", b) for b in range(B)]
            kf = [cat("kT", b) for b in range(B)]
            vf = [catv(b) for b in range(B)]
            gs = _f32(inp["diff_g_sub"][j])
            maps = []
            for c in range(NCORES):
                units = [divmod(c * U + u, H) for u in range(U)]
                m = dict(qA=np.stack([qf[b][2 * h:2 * h + 2] for b, h in units]),
                         kA=np.stack([kf[b][2 * h:2 * h + 2] for b, h in units]),
                         v=np.stack([_v_units(vf[b], h * 256, 256, S) for b, h in units]),
                         gsub=np.ascontiguousarray(gs.reshape(2, 128).T))
                for n in ("lq1", "lk1", "lq2", "lk2"):
                    m[n] = _f32(inp["diff_" + n][j]).reshape(1, 128)
                maps.append(m)
            nc2 = _get_nc(("k2diff", U, S, layer), lambda: build_k2("diff", U, S, lam_init))
            r2 = _run(nc2, maps)
            ochunks = np.zeros((B, 16, 128, S), NPBF16)
            for c in range(NCORES):
                o = np.asarray(r2[c]["oT"])
                for u in range(U):
                    b, h = divmod(c * U + u, H)
                    ochunks[b, 2 * h:2 * h + 2] = o[u].reshape(2, 128, S)
            ko = 128
        wo = _f32(inp[f"{mixer}_wo"][j])
        gf = _f32(inp["ffn_norm"][layer]).reshape(1, D)
        wg = _f32(inp["ffn_w_gate"][layer])
        wu = _f32(inp["ffn_w_up"][layer])
        wd = _f32(inp["ffn_w_down"][layer])
        cw = host_cw(_f32(inp["ffn_conv_w"][layer]), _f32(inp["ffn_conv_b"][layer]))
        nc3 = _get_nc(("k3", TOK, ko), lambda: build_k3(TOK, ko))
        NKo = D // ko
        maps = []
        for c in range(NCORES):
            b, cc = divmod(c, CPB)
            t0 = cc * TOK
            oT = np.ascontiguousarray(ochunks[b][:, :, t0:t0 + TOK])
            if cc == 0:
                xh = np.zeros((128, D), np.float32)
                oTh = np.zeros((NKo, ko, 128), NPBF16)
            else:
                xh = np.ascontiguousarray(xs[c - 1][TOK - 128:])
                oTh = np.ascontiguousarray(ochunks[b][:, :, t0 - 128:t0])
            maps.append(dict(x=xs[c], xh=xh, oT=oT, oTh=oTh, wo=wo, gf=gf, wg=wg, wu=wu, wd=wd, cw=cw))
        r3 = _run(nc3, maps)
        xs = [np.asarray(r3[c]["xo"]) for c in range(NCORES)]
    return np.concatenate(xs, axis=0).reshape(B, S, D).astype(np.float32)


class _Rep:
    def __init__(self, ap):
        self.ap = ap

    def __getitem__(self, i):
        return self.ap


W_SPECS = dict(
    attn_norm=(4, 2048), ffn_norm=(4, 2048),
    moba_wq=(1, 2048, 2048), moba_wk=(1, 2048, 2048), moba_wv=(1, 2048, 2048), moba_gq=(1, 128), moba_gk=(1, 128),
    moba_wo=(1, 2048, 2048),
    mla_wq_a=(1, 2048, 512), mla_g_qa=(1, 512), mla_wq_b=(1, 512, 3072), mla_wkv_a=(1, 2048, 576), mla_g_kva=(1, 512),
    mla_wkv_b=(1, 512, 4096), mla_g_qn=(1, 128), mla_g_kn=(1, 128), mla_g_qr=(1, 64), mla_g_kr=(1, 64),
    mla_wo=(1, 2048, 2048),
    swa_wq=(1, 2048, 2048), swa_wk=(1, 2048, 256), swa_wv=(1, 2048, 256), swa_gq=(1, 64), swa_gk=(1, 64),
    swa_sinks=(1, 32), swa_wo=(1, 2048, 2048),
    diff_wq=(1, 2048, 2048), diff_wk=(1, 2048, 2048), diff_wv=(1, 2048, 2048), diff_gq=(1, 128), diff_gk=(1, 128),
    diff_lq1=(1, 128), diff_lk1=(1, 128), diff_lq2=(1, 128), diff_lk2=(1, 128), diff_wo=(1, 2048, 2048),
    ffn_w_gate=(4, 2048, 5632), ffn_w_up=(4, 2048, 5632), ffn_w_down=(4, 5632, 2048),
)


def build_fused(S):
    nc = bass.Bass("TRN2", target_bir_lowering=False)
    D = D_MODEL
    NG = 4
    GT = S // NG
    NTg = GT // 128
    NKT = S // 128
    x_in = _dram_in(nc, "x", [S, D], F32)
    x_out = _dram_out(nc, "out", [S, D], F32)
    pos = _dram_in(nc, "pos", [NG, 128, NTg], F32)
    cw = _dram_in(nc, "cw", [4, 128, 4 * (D_FF // 128)], F32)
    gsub_in = _dram_in(nc, "gsub", [128, 2], F32)
    Win = {k: _dram_in(nc, k, shp, F32) for k, shp in W_SPECS.items()}
    xbuf = [nc.dram_tensor(f"xbuf{i}", [S, D], F32).ap() for i in range(2)]
    xmid = nc.dram_tensor("xmid", [GT, D], F32).ap()
    hTd = nc.dram_tensor("hTd", [16, 128, GT], BF16).ap()
    hTd3 = nc.dram_tensor("hTd3", [NTg + 1, 128, 16 * 128], BF16).ap()
    qs = nc.dram_tensor("qs", [16 * 128, S], BF16).ap()
    ks = nc.dram_tensor("ks", [16 * 128, S], BF16).ap()
    qrs = nc.dram_tensor("qrs", [16, 64, S], BF16).ap()
    krs = nc.dram_tensor("krs", [64, S], BF16).ap()
    vs = nc.dram_tensor("vs", [128, NKT * 2048], BF16).ap()
    os_ = nc.dram_tensor("os", [16 * 128, S], BF16).ap()
    cqT = nc.dram_tensor("cqT", [4, 128, GT], BF16).ap()
    ckvT = nc.dram_tensor("ckvT", [4, 128, GT], BF16).ap()
    zx = nc.dram_tensor("zx", [128, D], F32).ap()
    zo = nc.dram_tensor("zo", [16, 128, 128], BF16).ap()
    P = Prog(nc)
    with contextlib.ExitStack() as es:
        T = make_T(es, nc)
        zt = T("zt", [128, D], F32)
        ztb = T("ztb", [128, 16 * 128], BF16)
        P.op("pool", lambda e: e.memset(zt[:], 0.0), writes=["zt"])
        P.op("pool", lambda e: e.memset(ztb[:], 0.0), writes=["ztb"])
        P.dma("sp", zx[:, :], zt[:], reads=["zt"], writes=["zx"])
        P.dma("sp", zo.rearrange("c p t -> p c t"), ztb[:].rearrange("p (c t) -> p c t", c=16), reads=["ztb"], writes=["zo"])
        P.emit()
    q16 = qs.rearrange("(h p) s -> h p s", p=128)
    k16 = ks.rearrange("(h p) s -> h p s", p=128)
    o16 = os_.rearrange("(h p) s -> h p s", p=128)
    x_cur = x_in
    for layer in range(4):
        mixer = MIXERS[layer]
        x_nxt = x_out if layer == 3 else xbuf[layer % 2]
        pre = mixer + "_"
        Gd = {}
        for g in range(NG):
            gs_ = slice(g * GT, (g + 1) * GT)
            xg = x_cur[gs_, :]
            phase_norm(P, nc, GT, xg, Win["attn_norm"][layer:layer + 1, :], hTd)
            tag = f"L{layer}g{g}"
            if mixer in ("moba", "diff"):
                Gd = dict(gq=Win[pre + "gq"], gk=Win[pre + "gk"])
                for wname, gname, dst in (("wq", "gq", q16), ("wk", "gk", k16)):
                    blocks = []
                    for b in range(4):
                        segs = [dict(off=j * 128, w=128, gain=gname, rot=32, kind="T", dst=[dst[b * 4 + j][:, gs_]])
                                for j in range(4)]
                        blocks.append((b * 512, 512, segs))
                    phase_proj(P, nc, GT, hTd, 16, 128, Win[pre + wname][0], blocks, Gd, pos[g], tag + wname)
                dvh = 128 if mixer == "moba" else 256
                vv = vs.rearrange("p (h t d) -> h p t d", t=NKT, d=dvh)
                blocks = []
                for b in range(4):
                    segs = []
                    for jj in range(512 // dvh):
                        h = b * (512 // dvh) + jj
                        segs.append(dict(off=jj * dvh, w=dvh, gain=None, rot=0, kind="M",
                                         dst=(lambda t, h=h, g=g: vv[h][:, g * NTg + t, :])))
                    blocks.append((b * 512, 512, segs))
                phase_proj(P, nc, GT, hTd, 16, 128, Win[pre + "wv"][0], blocks, Gd, pos[g], tag + "wv")
            elif mixer == "swa":
                Gd = dict(gq=Win["swa_gq"], gk=Win["swa_gk"])
                q32 = qs.rearrange("(h p) s -> h p s", p=64)
                k4 = ks[0:256, :].rearrange("(h p) s -> h p s", p=64)
                blocks = []
                for b in range(8):
                    segs = [dict(off=j * 64, w=64, gain="gq", rot=16, kind="T", dst=[q32[b * 4 + j][:, gs_]]) for j in range(4)]
                    blocks.append((b * 256, 256, segs))
                phase_proj(P, nc, GT, hTd, 16, 128, Win["swa_wq"][0], blocks, Gd, pos[g], tag + "wq")
                segs = [dict(off=j * 64, w=64, gain="gk", rot=16, kind="T", dst=[k4[j][:, gs_]]) for j in range(4)]
                phase_proj(P, nc, GT, hTd, 16, 128, Win["swa_wk"][0], [(0, 256, segs)], Gd, pos[g], tag + "wk")
                vv = vs[:, 0:NKT * 256].rearrange("p (h t d) -> h p t d", t=NKT, d=64)
                segs = [dict(off=j * 64, w=64, gain=None, rot=0, kind="M",
                             dst=(lambda t, j=j, g=g: vv[j][:, g * NTg + t, :])) for j in range(4)]
                phase_proj(P, nc, GT, hTd, 16, 128, Win["swa_wv"][0], [(0, 256, segs)], Gd, pos[g], tag + "wv")
            else:
                Gd = dict(g_qa=Win["mla_g_qa"], g_kva=Win["mla_g_kva"], g_qn=Win["mla_g_qn"], g_kn=Win["mla_g_kn"],
                          g_qr=Win["mla_g_qr"], g_kr=Win["mla_g_kr"])
                phase_proj(P, nc, GT, hTd, 16, 128, Win["mla_wq_a"][0],
                           [(0, 512, [dict(off=0, w=512, gain="g_qa", rot=0, kind="T", dst=[cqT[c] for c in range(4)])])],
                           Gd, pos[g], tag + "wqa")
                phase_proj(P, nc, GT, hTd, 16, 128, Win["mla_wkv_a"][0],
                           [(0, 512, [dict(off=0, w=512, gain="g_kva", rot=0, kind="T", dst=[ckvT[c] for c in range(4)])]),
                            (512, 64, [dict(off=0, w=64, gain="g_kr", rot=64, kind="T", dst=[krs[:, gs_]])])],
                           Gd, pos[g], tag + "wkva")
                blocks = []
                for b in range(8):
                    segs = []
                    for jj in range(2):
                        h = b * 2 + jj
                        segs.append(dict(off=jj * 192, w=128, gain="g_qn", rot=0, kind="T", dst=[q16[h][:, gs_]]))
                        segs.append(dict(off=jj * 192 + 128, w=64, gain="g_qr", rot=64, kind="T", dst=[qrs[h][:, gs_]]))
                    blocks.append((b * 384, 384, segs))
                phase_proj(P, nc, GT, cqT, 4, 128, Win["mla_wq_b"][0], blocks, Gd, pos[g], tag + "wqb")
                vv = vs.rearrange("p (h t d) -> h p t d", t=NKT, d=128)
                blocks = []
                for b in range(8):
                    segs = []
                    for jj in range(2):
                        h = b * 2 + jj
                        segs.append(dict(off=jj * 256, w=128, gain="g_kn", rot=0, kind="T", dst=[k16[h][:, gs_]]))
                        segs.append(dict(off=jj * 256 + 128, w=128, gain=None, rot=0, kind="M",
                                         dst=(lambda t, h=h, g=g: vv[h][:, g * NTg + t, :])))
                    blocks.append((b * 512, 512, segs))
                phase_proj(P, nc, GT, ckvT, 4, 128, Win["mla_wkv_b"][0], blocks, Gd, pos[g], tag + "wkvb")
        if mixer == "moba":
            io = dict(qA=q16, kA=k16, v=vs.rearrange("p (h t d) -> h p t d", t=NKT, d=128), oT=o16)
            phase_attn(P, nc, "moba", 16, S, io)
        elif mixer == "mla":
            io = dict(qA=q16, kA=k16, qB=qrs, kB=_Rep(krs), v=vs.rearrange("p (h t d) -> h p t d", t=NKT, d=128), oT=o16)
            phase_attn(P, nc, "mla", 16, S, io)
        elif mixer == "swa":
            io = dict(qT=qs.rearrange("(u h p) s -> u h p s", h=8, p=64), kT=ks[0:256, :].rearrange("(h p) s -> h p s", p=64),
                      v=vs[:, 0:NKT * 256].rearrange("p (h t d) -> h p t d", t=NKT, d=64),
                      sinks=Win["swa_sinks"].rearrange("o (u h) -> u o h", h=8),
                      oT=os_.rearrange("(u h p) s -> u h p s", h=8, p=64))
            phase_swa(P, nc, 4, S, io)
        else:
            lam_init = 0.8 - 0.6 * math.exp(-0.3 * layer)
            io = dict(qA=qs.rearrange("(h c p) s -> h c p s", c=2, p=128), kA=ks.rearrange("(h c p) s -> h c p s", c=2, p=128),
                      v=vs.rearrange("p (h t d) -> h p t d", t=NKT, d=256), oT=os_.rearrange("(h e) s -> h e s", e=256),
                      lq1=Win["diff_lq1"], lk1=Win["diff_lk1"], lq2=Win["diff_lq2"], lk2=Win["diff_lk2"], gsub=gsub_in)
            phase_attn(P, nc, "diff", 8, S, io, lam_init)
        for g in range(NG):
            gs_ = slice(g * GT, (g + 1) * GT)
            if g == 0:
                xh, oTh = zx, zo
            else:
                xh = x_cur[g * GT - 128:g * GT, :]
                oTh = o16[:, :, g * GT - 128:g * GT]
            phase_outproj_norm(P, nc, GT, 128, x_cur[gs_, :], xh, o16[:, :, gs_], oTh, Win[pre + "wo"][0],
                               Win["ffn_norm"][layer:layer + 1, :], xmid, hTd3)
            phase_ffn(P, nc, GT, xmid, hTd3, Win["ffn_w_gate"][layer], Win["ffn_w_up"][layer], Win["ffn_w_down"][layer],
                      cw[layer], x_nxt[gs_, :])
        x_cur = x_nxt
    P.close()
    return nc


def run_fused(inp, B, S):
    D = D_MODEL
    NG = 4
    GT = S // NG
    NTg = GT // 128
    nc = _get_nc(("fused", S), lambda: build_fused(S))
    x = _f32(inp["x"])
    posv = np.arange(S, dtype=np.float32).reshape(NG, NTg, 128).transpose(0, 2, 1)
    base = dict(pos=np.ascontiguousarray(posv),
                cw=np.stack([host_cw(_f32(inp["ffn_conv_w"][l]), _f32(inp["ffn_conv_b"][l])) for l in range(4)]),
                gsub=np.ascontiguousarray(_f32(inp["diff_g_sub"][0]).reshape(2, 128).T))
    for k, shp in W_SPECS.items():
        base[k] = _f32(inp[k]).reshape(shp)
    maps = []
    for b in range(B):
        m = dict(base)
        m["x"] = np.ascontiguousarray(x[b])
        maps.append(m)
    res = run_bass_kernel_spmd(nc, maps, core_ids=list(range(B)))
    return np.stack([np.asarray(res.results[b]["out"]) for b in range(B)]).astype(np.float32)


def kernel(**inputs):
    return run_fused(inputs, 2, 8192)
```

```python
import contextlib
import os
import math
import numpy as np
import ml_dtypes
import concourse.bass as bass
import concourse.mybir as mybir
from concourse.bass_utils import run_bass_kernel_spmd

F32 = mybir.dt.float32
BF16 = mybir.dt.bfloat16
ALU = mybir.AluOpType
AF = mybir.ActivationFunctionType
AX = mybir.AxisListType
NPBF16 = ml_dtypes.bfloat16

D_MODEL = 2048
D_FF = 5632
EPS = 1e-6
ROPE_THETA = 500000.0
NCORES = 8


class Prog:
    CENG = ("pe", "act", "dve", "pool")

    def __init__(self, nc, ndma=6):
        self.nc = nc
        self.es = contextlib.ExitStack()
        self.ops = {e: [] for e in ("pe", "act", "dve", "pool", "sp")}
        self.csem = {e: self.es.enter_context(nc.semaphore("c_" + e)) for e in self.CENG}
        self.ccnt = {e: 0 for e in self.CENG}
        self.dsem = {q: [self.es.enter_context(nc.semaphore(f"d_{q}{i}")) for i in range(ndma)]
                     for q in ("sp", "pool")}
        self.dval = {q: [0] * ndma for q in ("sp", "pool")}
        self.dnext = {q: 0 for q in ("sp", "pool")}
        self.seen = {e: {} for e in self.ops}
        self.lastw = {}
        self.readers = {}
        self.semobj = {}

    def close(self):
        self.es.close()

    def _key(self, s):
        k = id(s)
        self.semobj[k] = s
        return k

    def op(self, eng, fn, reads=(), writes=(), dma=False, excl=()):
        writes = list(writes) + list(excl)
        deps = {}

        def add(ev):
            if ev is None:
                return
            k, v = ev
            if deps.get(k, 0) < v:
                deps[k] = v

        for b in reads:
            add(self.lastw.get(b))
        for b in writes:
            add(self.lastw.get(b))
            for ev in self.readers.get(b, {}).items():
                add(ev)
        if dma:
            q = eng
            i = self.dnext[q]
            self.dnext[q] = (i + 1) % len(self.dsem[q])
            sem = self.dsem[q][i]
            k = self._key(sem)
            if self.dval[q][i] > 0:
                add((k, self.dval[q][i]))
            self.dval[q][i] += 16
            ev = (k, self.dval[q][i])
            inc = 16
        else:
            sem = self.csem[eng]
            k = self._key(sem)
            self.ccnt[eng] += 1
            ev = (k, self.ccnt[eng])
            inc = 1
            if eng == "pe":
                deps.pop(k, None)
        seen = self.seen[eng]
        waits = []
        for dk, dv in deps.items():
            if seen.get(dk, 0) >= dv:
                continue
            seen[dk] = dv
            waits.append((self.semobj[dk], dv))
        for b in writes:
            self.lastw[b] = ev
            self.readers[b] = {}
        for b in reads:
            r = self.readers.setdefault(b, {})
            if r.get(ev[0], 0) < ev[1]:
                r[ev[0]] = ev[1]
        self.ops[eng].append((fn, waits, sem, inc))
        return ev

    def dma(self, q, out, in_, reads=(), writes=()):
        return self.op(q, lambda e: e.dma_start(out=out, in_=in_), reads, writes, dma=True)

    def emit(self):
        finals = []
        for e in self.CENG:
            if self.ccnt[e] > 0:
                finals.append((self._key(self.csem[e]), self.csem[e], self.ccnt[e]))
        for q in ("sp", "pool"):
            for s, v in zip(self.dsem[q], self.dval[q]):
                if v > 0:
                    finals.append((self._key(s), s, v))
        ops = self.ops
        seen = self.seen

        def mk(ename):
            def body(engine):
                for fn, waits, sem, inc in ops[ename]:
                    for s, v in waits:
                        engine.wait_ge(s, v)
                    fn(engine).then_inc(sem, inc)
                for k, s, v in finals:
                    if seen[ename].get(k, 0) < v:
                        engine.wait_ge(s, v)
                        seen[ename][k] = v
            return body

        with self.nc.Block() as block:
            block.tensor(mk("pe"))
            block.scalar(mk("act"))
            block.vector(mk("dve"))
            block.gpsimd(mk("pool"))
            block.sync(mk("sp"))
        self.ops = {e: [] for e in self.ops}


def build_identity(P, nc, ident, es):
    it_p = es.enter_context(nc.sbuf_tensor(uname("it_p"), [128, 128], F32))
    it_j = es.enter_context(nc.sbuf_tensor(uname("it_j"), [128, 128], F32))
    P.op("pool", lambda e: e.iota(it_p[:], [[0, 128]], base=0, channel_multiplier=1,
                                  allow_small_or_imprecise_dtypes=True), writes=["it_p"])
    P.op("pool", lambda e: e.iota(it_j[:], [[1, 128]], base=0, channel_multiplier=0,
                                  allow_small_or_imprecise_dtypes=True), writes=["it_j"])
    P.op("dve", lambda e: e.tensor_tensor(ident[:], it_p[:], it_j[:], ALU.is_equal),
         reads=["it_p", "it_j"], writes=["ident"])
    return it_p, it_j


def emit_rstd(P, out, ss, okey, skey, inv_n, epsb):
    P.op("act", lambda e: e.activation(out, ss, AF.Sqrt, bias=epsb[:, 0:1], scale=inv_n),
         reads=[skey, "consts"], writes=[okey])
    P.op("dve", lambda e: e.reciprocal(out, out), reads=[okey], writes=[okey])


def emit_norm_transpose(P, nc, xt, xkeys, g_rep, ident, scr, hT_out, hT_key, pst, pst_key):
    D = D_MODEL
    sq, ss, rstd, hn = scr["sq"], scr["ss"], scr["rstd"], scr["hn"]
    P.op("act", lambda e: e.activation(sq[:], xt, AF.Square, accum_out=ss[:]),
         reads=list(xkeys), writes=["sq", "ss"])
    emit_rstd(P, rstd[:], ss[:], "rstd", "ss", 1.0 / D, scr["epsb"])
    P.op("dve", lambda e: e.scalar_tensor_tensor(hn[:], xt, rstd[:, 0:1], g_rep, ALU.mult, ALU.mult),
         reads=list(xkeys) + ["rstd", "consts"], writes=["hn"])
    for half in range(2):
        pt = pst[half]
        for j in range(8):
            c = half * 8 + j
            P.op("pe", lambda e, c=c, j=j, pt=pt: e.transpose(pt[:, j * 128:(j + 1) * 128],
                                                              hn[:, c * 128:(c + 1) * 128], ident[:]),
                 reads=["hn", "ident"], writes=[pst_key[half]])
        eng = "act" if half == 0 else "dve"
        if eng == "act":
            P.op("act", lambda e, pt=pt, half=half: e.activation(
                hT_out[:, half * 8:(half + 1) * 8, :], pt[:].rearrange("p (c t) -> p c t", c=8), AF.Copy),
                 reads=[pst_key[half]], writes=[hT_key])
        else:
            P.op("dve", lambda e, pt=pt, half=half: e.tensor_copy(
                hT_out[:, half * 8:(half + 1) * 8, :], pt[:].rearrange("p (c t) -> p c t", c=8)),
                 reads=[pst_key[half]], writes=[hT_key])


_UID = [0]


def uname(name):
    _UID[0] += 1
    return f"{name}_u{_UID[0]}"


def make_T(es, nc):
    def T(name, shape, dt):
        return es.enter_context(nc.sbuf_tensor(uname(name), shape, dt))
    return T


def psum_T(es, nc, name, shape, dt):
    return es.enter_context(nc.psum_tensor(uname(name), shape, dt))


def phase_outproj_norm(P, nc, TOK, ko, x, xh, oT, oTh, wo, gf, xmid, hTd):
    D = D_MODEL
    NT = TOK // 128
    NKo = D // ko
    with contextlib.ExitStack() as es:
        T = make_T(es, nc)
        wo_sb = T("wo_sb", [ko, NKo, D], BF16)
        oT_sb = T("oT_sb", [ko, NKo, TOK], BF16)
        oTh_sb = T("oTh_sb", [ko, NKo, 128], BF16)
        g_rep = T("g_rep", [128, D], F32)
        ident = T("ident", [128, 128], BF16)
        scr = dict(sq=T("sq", [128, D], F32), ss=T("ss", [128, 1], F32), rstd=T("rstd", [128, 1], F32),
                   hn=T("hn", [128, D], BF16), epsb=T("epsb", [128, 1], F32))
        xt = [T(f"xt{i}", [128, D], F32) for i in range(2)]
        xm = [T(f"xm{i}", [128, D], F32) for i in range(2)]
        hT_sb = [T(f"hT_sb{i}", [128, 16, 128], BF16) for i in range(2)]
        py = [psum_T(es, nc, f"py{i}", [128, 512], F32) for i in range(4)]
        pst = [psum_T(es, nc, f"pst{i}", [128, 1024], BF16) for i in range(2)]
        build_identity(P, nc, ident, es)
        P.op("pool", lambda e: e.memset(scr["epsb"][:], EPS), writes=["consts"])
        P.dma("sp", g_rep[:], gf[0:1, :].partition_broadcast(128), writes=["consts"])
        for c in range(NKo):
            P.dma("pool", wo_sb[:, c, :], wo[c * ko:(c + 1) * ko, :], writes=[f"wo{c}"])
        P.dma("sp", oTh_sb[:], oTh.rearrange("c p t -> p c t"), writes=["oTh"])
        for c in range(NKo):
            P.dma("sp", oT_sb[:, c, :], oT[c], writes=[f"oT{c}"])
        order = [NT] + list(range(NT))
        for n, i in enumerate(order):
            b = n % 2
            halo = (i == NT)
            xsrc = xh[:, :] if halo else x[i * 128:(i + 1) * 128, :]
            P.dma("sp", xt[b][:], xsrc, writes=[f"xt{b}"])
            for nb in range(4):
                for c in range(NKo):
                    if halo:
                        lhsT = oTh_sb[:, c, :]
                        rk = "oTh"
                    else:
                        lhsT = oT_sb[:, c, i * 128:(i + 1) * 128]
                        rk = f"oT{c}"
                    P.op("pe", lambda e, nb=nb, c=c, lhsT=lhsT: e.matmul(
                        py[nb][:], lhsT, wo_sb[:, c, nb * 512:(nb + 1) * 512], start=(c == 0), stop=(c == NKo - 1)),
                        reads=[rk, f"wo{c}"], writes=[f"py{nb}"])
                P.op("dve", lambda e, nb=nb, b=b: e.tensor_tensor(
                    xm[b][:, nb * 512:(nb + 1) * 512], xt[b][:, nb * 512:(nb + 1) * 512], py[nb][:], ALU.add),
                    reads=[f"xt{b}", f"py{nb}"], writes=[f"xm{b}_{nb}"])
            xkeys = [f"xm{b}_{nb}" for nb in range(4)]
            if not halo:
                P.dma("sp", xmid[i * 128:(i + 1) * 128, :], xm[b][:], reads=xkeys, writes=[f"xmid{i}"])
            emit_norm_transpose(P, nc, xm[b][:], xkeys, g_rep[:], ident, scr, hT_sb[b], f"hTsb{b}",
                                pst, ["pst0", "pst1"])
            P.dma("sp", hTd[i], hT_sb[b][:].rearrange("p c t -> p (c t)"), reads=[f"hTsb{b}"], writes=[f"hTd{i}"])
        P.emit()


def phase_ffn(P, nc, TOK, xmid, hTd, wg, wu, wd, cw, xo):
    D = D_MODEL
    NT = TOK // 128
    NSB = TOK // 512
    NFB = D_FF // 512
    NFC = D_FF // 128
    with contextlib.ExitStack() as es:
        T = make_T(es, nc)
        hT_sb = T("f_hT", [128, 16, 512], BF16)
        hTh = T("f_hTh", [128, 16, 128], BF16)
        xacc = T("f_xacc", [128, 4, D], F32)
        wg_sb = [T(f"f_wg{i}", [128, 16, 512], BF16) for i in range(2)]
        wu_sb = [T(f"f_wu{i}", [128, 16, 512], BF16) for i in range(2)]
        wd_sb = [T(f"f_wd{i}", [128, 4, D], BF16) for i in range(2)]
        aT = [T(f"f_aT{i}", [128, 4, 512], BF16) for i in range(2)]
        gs = [T(f"f_gs{i}", [128, 514], F32) for i in range(2)]
        c1 = [T(f"f_c1{i}", [128, 512], F32) for i in range(2)]
        carry = T("f_carry", [128, NFC, 2], F32)
        cw_sb = T("f_cw", [128, 4 * NFC], F32)
        psg = [psum_T(es, nc, f"psg{i}", [128, 512], F32) for i in range(2)]
        psu = [psum_T(es, nc, f"psu{i}", [128, 512], F32) for i in range(2)]
        pso = [psum_T(es, nc, f"pso{i}", [128, 512], F32) for i in range(3)]
        ph = psum_T(es, nc, "ph", [128, 2 * NFC], F32)
        P.dma("sp", cw_sb[:], cw[:, :], writes=["cw"])
        P.dma("sp", hTh[:], hTd[NT].rearrange("p (c t) -> p c t", c=16), writes=["hTh"])
        def load_fw(idx):
            fb_ = idx % NFB
            wb_ = idx % 2
            P.dma("pool", wg_sb[wb_][:], wg[:, fb_ * 512:(fb_ + 1) * 512].rearrange("(c p) f -> p c f", p=128),
                  writes=[f"wg{wb_}"])
            P.dma("pool", wu_sb[wb_][:], wu[:, fb_ * 512:(fb_ + 1) * 512].rearrange("(c p) f -> p c f", p=128),
                  writes=[f"wu{wb_}"])
            P.dma("pool", wd_sb[wb_][:], wd[fb_ * 512:(fb_ + 1) * 512, :].rearrange("(c p) n -> p c n", p=128),
                  writes=[f"wd{wb_}"])

        npo = 0
        for sb in range(NSB):
            for tt in range(4):
                P.dma("sp", hT_sb[:, :, tt * 128:(tt + 1) * 128],
                      hTd[sb * 4 + tt].rearrange("p (c t) -> p c t", c=16), writes=[f"hT{tt}"])
                P.dma("sp", xacc[:, tt, :], xmid[(sb * 4 + tt) * 128:(sb * 4 + tt + 1) * 128, :],
                      writes=[f"xacc{tt}_{nb}" for nb in range(4)])
            hkeys = [f"hT{tt}" for tt in range(4)]
            for fb in range(NFB):
                wb = (sb * NFB + fb) % 2
                if sb == 0 and fb == 0:
                    load_fw(0)
                nxt = sb * NFB + fb + 1
                if nxt < NSB * NFB:
                    load_fw(nxt)
                for fcl in range(4):
                    fc = fb * 4 + fcl
                    pb = fc % 2
                    for kc in range(16):
                        P.op("pe", lambda e, kc=kc, fcl=fcl, wb=wb, pb=pb: e.matmul(
                            psg[pb][:], wg_sb[wb][:, kc, fcl * 128:(fcl + 1) * 128], hT_sb[:, kc, :],
                            start=(kc == 0), stop=(kc == 15)),
                            reads=[f"wg{wb}"] + hkeys, writes=[f"psg{pb}"])
                    if sb == 0:
                        for kc in range(16):
                            P.op("pe", lambda e, kc=kc, fcl=fcl, wb=wb, fc=fc: e.matmul(
                                ph[:, 2 * fc:2 * fc + 2], wg_sb[wb][:, kc, fcl * 128:(fcl + 1) * 128],
                                hTh[:, kc, 126:128], start=(kc == 0), stop=(kc == 15)),
                                reads=[f"wg{wb}", "hTh"], writes=["ph"])
                        P.op("act", lambda e, fc=fc: e.activation(carry[:, fc, :], ph[:, 2 * fc:2 * fc + 2], AF.Copy),
                             excl=["ph"], writes=[f"carry{fc}"])
                    for kc in range(16):
                        P.op("pe", lambda e, kc=kc, fcl=fcl, wb=wb, pb=pb: e.matmul(
                            psu[pb][:], wu_sb[wb][:, kc, fcl * 128:(fcl + 1) * 128], hT_sb[:, kc, :],
                            start=(kc == 0), stop=(kc == 15)),
                            reads=[f"wu{wb}"] + hkeys, writes=[f"psu{pb}"])
                    g_ = gs[pb]
                    c_ = c1[pb]
                    P.op("pool", lambda e, g_=g_, fc=fc: e.tensor_copy(g_[:, 0:2], carry[:, fc, :]),
                         reads=[f"carry{fc}"], writes=[f"gsh{pb}"])
                    P.op("act", lambda e, g_=g_, pb=pb: e.activation(g_[:, 2:514], psg[pb][:], AF.Copy),
                         reads=[f"psg{pb}"], writes=[f"gs{pb}"])
                    P.op("pool", lambda e, g_=g_, fc=fc: e.tensor_copy(carry[:, fc, :], g_[:, 512:514]),
                         reads=[f"gs{pb}", f"gsh{pb}"], writes=[f"carry{fc}"])
                    P.op("act", lambda e, c_=c_, pb=pb, fc=fc: e.activation(
                        c_[:], psg[pb][:], AF.Identity, bias=cw_sb[:, 3 * NFC + fc:3 * NFC + fc + 1],
                        scale=cw_sb[:, 2 * NFC + fc:2 * NFC + fc + 1]),
                        reads=[f"psg{pb}", "cw"], writes=[f"c1{pb}"])
                    P.op("dve", lambda e, c_=c_, g_=g_, fc=fc: e.scalar_tensor_tensor(
                        c_[:], g_[:, 1:513], cw_sb[:, NFC + fc:NFC + fc + 1], c_[:], ALU.mult, ALU.add),
                        reads=[f"gs{pb}", f"gsh{pb}", "cw", f"c1{pb}"], writes=[f"c1{pb}"])
                    P.op("dve", lambda e, c_=c_, g_=g_, fc=fc: e.scalar_tensor_tensor(
                        c_[:], g_[:, 0:512], cw_sb[:, fc:fc + 1], c_[:], ALU.mult, ALU.add),
                        reads=[f"gs{pb}", f"gsh{pb}", "cw", f"c1{pb}"], writes=[f"c1{pb}"])
                    P.op("act", lambda e, c_=c_: e.activation(c_[:], c_[:], AF.Silu),
                         reads=[f"c1{pb}"], writes=[f"c1{pb}"])
                    P.op("dve", lambda e, c_=c_, wb=wb, fcl=fcl, pb=pb: e.tensor_tensor(
                        aT[wb][:, fcl, :], c_[:], psu[pb][:], ALU.mult),
                        reads=[f"c1{pb}", f"psu{pb}"], writes=[f"aT{wb}_{fcl}"])
                for tt in range(4):
                    for nb in range(4):
                        pi = npo % 3
                        npo += 1
                        for fcl in range(4):
                            P.op("pe", lambda e, pi=pi, wb=wb, fcl=fcl, tt=tt, nb=nb: e.matmul(
                                pso[pi][:], aT[wb][:, fcl, tt * 128:(tt + 1) * 128],
                                wd_sb[wb][:, fcl, nb * 512:(nb + 1) * 512], start=(fcl == 0), stop=(fcl == 3)),
                                reads=[f"aT{wb}_{fcl}", f"wd{wb}"], writes=[f"pso{pi}"])
                        P.op("dve", lambda e, pi=pi, tt=tt, nb=nb: e.tensor_tensor(
                            xacc[:, tt, nb * 512:(nb + 1) * 512], xacc[:, tt, nb * 512:(nb + 1) * 512],
                            pso[pi][:], ALU.add),
                            reads=[f"pso{pi}", f"xacc{tt}_{nb}"], writes=[f"xacc{tt}_{nb}"])
            for tt in range(4):
                P.dma("sp", xo[(sb * 4 + tt) * 128:(sb * 4 + tt + 1) * 128, :], xacc[:, tt, :],
                      reads=[f"xacc{tt}_{nb}" for nb in range(4)], writes=[f"xo{sb}_{tt}"])
        P.emit()


def build_k3(TOK, ko):
    nc = bass.Bass("TRN2", target_bir_lowering=False)
    D = D_MODEL
    NT = TOK // 128
    NKo = D // ko
    x = nc.dram_tensor("x", [TOK, D], F32, kind="ExternalInput").ap()
    xh = nc.dram_tensor("xh", [128, D], F32, kind="ExternalInput").ap()
    oT = nc.dram_tensor("oT", [NKo, ko, TOK], BF16, kind="ExternalInput").ap()
    oTh = nc.dram_tensor("oTh", [NKo, ko, 128], BF16, kind="ExternalInput").ap()
    wo = nc.dram_tensor("wo", [D, D], F32, kind="ExternalInput").ap()
    gf = nc.dram_tensor("gf", [1, D], F32, kind="ExternalInput").ap()
    wg = nc.dram_tensor("wg", [D, D_FF], F32, kind="ExternalInput").ap()
    wu = nc.dram_tensor("wu", [D, D_FF], F32, kind="ExternalInput").ap()
    wd = nc.dram_tensor("wd", [D_FF, D], F32, kind="ExternalInput").ap()
    cw = nc.dram_tensor("cw", [128, 4 * (D_FF // 128)], F32, kind="ExternalInput").ap()
    xo = nc.dram_tensor("xo", [TOK, D], F32, kind="ExternalOutput").ap()
    xmid = nc.dram_tensor("xmid", [TOK, D], F32).ap()
    hTd = nc.dram_tensor("hTd", [NT + 1, 128, 16 * 128], BF16).ap()
    P = Prog(nc)
    phase_outproj_norm(P, nc, TOK, ko, x, xh, oT, oTh, wo, gf, xmid, hTd)
    phase_ffn(P, nc, TOK, xmid, hTd, wg, wu, wd, cw, xo)
    P.close()
    return nc


def host_cw(conv_w, conv_b):
    NFC = D_FF // 128
    a = np.concatenate([conv_w, conv_b[None, :]], axis=0)
    return np.ascontiguousarray(a.reshape(4, NFC, 128).transpose(2, 0, 1).reshape(128, 4 * NFC))


def phase_norm(P, nc, TOK, x, g, hTd):
    D = D_MODEL
    NT = TOK // 128
    with contextlib.ExitStack() as es:
        T = make_T(es, nc)
        g_rep = T("n_g_rep", [128, D], F32)
        ident = T("n_ident", [128, 128], BF16)
        scr = dict(sq=T("n_sq", [128, D], F32), ss=T("n_ss", [128, 1], F32), rstd=T("n_rstd", [128, 1], F32),
                   hn=T("n_hn", [128, D], BF16), epsb=T("n_epsb", [128, 1], F32))
        xt = [T(f"n_xt{i}", [128, D], F32) for i in range(2)]
        hT_sb = [T(f"n_hT{i}", [128, 16, 128], BF16) for i in range(2)]
        pst = [psum_T(es, nc, f"n_pst{i}", [128, 1024], BF16) for i in range(2)]
        build_identity(P, nc, ident, es)
        P.op("pool", lambda e: e.memset(scr["epsb"][:], EPS), writes=["consts"])
        P.dma("sp", g_rep[:], g[0:1, :].partition_broadcast(128), writes=["consts"])
        for i in range(NT):
            b = i % 2
            P.dma("sp", xt[b][:], x[i * 128:(i + 1) * 128, :], writes=[f"xt{b}"])
            emit_norm_transpose(P, nc, xt[b][:], [f"xt{b}"], g_rep[:], ident, scr, hT_sb[b], f"hTsb{b}",
                                pst, ["pst0", "pst1"])
            P.dma("sp", hTd[:, :, i * 128:(i + 1) * 128].rearrange("c p t -> p c t"), hT_sb[b][:],
                  reads=[f"hTsb{b}"], writes=[f"hTd{i}"])
        P.emit()


def phase_proj(P, nc, TOK, actT, nk, Kp, W, blocks, gains, pos, tag):
    NT = TOK // 128
    rots = sorted({s["rot"] for _, _, segs in blocks for s in segs if s["rot"]})
    gnames = sorted({s["gain"] for _, _, segs in blocks for s in segs if s["gain"]})
    with contextlib.ExitStack() as es:
        T = make_T(es, nc)
        act = T(tag + "act", [Kp, nk, TOK], BF16)
        wsb = [T(f"{tag}w{i}", [Kp, nk, 512], BF16) for i in range(2)]
        ident = T(tag + "ident", [128, 128], BF16)
        epsb = T(tag + "epsb", [128, 1], F32)
        negpi = T(tag + "negpi", [128, 1], F32)
        pos_sb = T(tag + "pos", [128, NT], F32)
        grep = {}
        for gname in gnames:
            wdt = gains[gname].shape[1]
            grep[gname] = T(f"{tag}g_{gname}", [128, wdt], F32)
        tabs = {}
        for R in rots:
            half = R // 2
            tabs[R] = dict(cos=T(f"{tag}cos{R}", [128, NT, half], F32), sin=T(f"{tag}sin{R}", [128, NT, half], F32),
                           inv=T(f"{tag}inv{R}", [128, half], F32), ang=T(f"{tag}ang{R}", [128, NT, half], F32),
                           ki=T(f"{tag}ki{R}", [128, NT, half], mybir.dt.int32),
                           kf=T(f"{tag}kf{R}", [128, NT, half], F32))
        sq = T(tag + "sq", [128, 512], F32)
        ss = [T(f"{tag}ss{i}", [128, 8], F32) for i in range(4)]
        rstd = [T(f"{tag}rstd{i}", [128, 8], F32) for i in range(4)]
        qn = [T(f"{tag}qn{i}", [128, 512], F32) for i in range(4)]
        qb = [T(f"{tag}qb{i}", [128, 512], BF16) for i in range(4)]
        rt = [T(f"{tag}rt{i}", [128, 256], F32) for i in range(4)]
        stg = [T(f"{tag}stg{i}", [128, 4, TOK], BF16) for i in range(2)]
        ps = [psum_T(es, nc, f"{tag}ps{i}", [128, 512], F32) for i in range(4)]
        pT = [psum_T(es, nc, f"{tag}pT{i}", [128, 1024], BF16) for i in range(2)]
        build_identity(P, nc, ident, es)
        P.op("pool", lambda e: e.memset(epsb[:], EPS), writes=["c_eps"])
        P.op("pool", lambda e: e.memset(negpi[:], -math.pi), writes=["c_negpi"])
        P.dma("sp", pos_sb[:], pos[:, :], writes=["pos"])
        for gname in gnames:
            P.dma("sp", grep[gname][:], gains[gname][0:1, :].partition_broadcast(128), writes=["g_" + gname])
        for R in rots:
            half = R // 2
            tb = tabs[R]
            P.op("pool", lambda e, tb=tb, half=half: e.iota(tb["inv"][:], [[1, half]], base=0, channel_multiplier=0,
                                                            allow_small_or_imprecise_dtypes=True),
                 writes=[f"inv{R}"])
            P.op("act", lambda e, tb=tb, half=half: e.activation(tb["inv"][:], tb["inv"][:], AF.Exp,
                                                                 scale=-math.log(ROPE_THETA) / half),
                 reads=[f"inv{R}"], writes=[f"inv{R}"])
            for t in range(NT):
                P.op("dve", lambda e, tb=tb, t=t: e.tensor_scalar(tb["ang"][:, t, :], tb["inv"][:],
                                                                  pos_sb[:, t:t + 1], None, ALU.mult),
                     reads=[f"inv{R}", "pos"], writes=[f"ang{R}_{t}"])
            akeys = [f"ang{R}_{t}" for t in range(NT)]
            for nm, shift in (("sin", 0.0), ("cos", 0.5 * math.pi)):
                dst = tb[nm]
                key = f"{nm}{R}"
                P.op("dve", lambda e, tb=tb, dst=dst, shift=shift: e.tensor_scalar(
                    dst[:], tb["ang"][:], shift, None, ALU.add), reads=akeys, writes=[key])
                P.op("dve", lambda e, tb=tb, dst=dst: e.tensor_scalar(
                    tb["ki"][:], dst[:], 1.0 / (2 * math.pi), None, ALU.mult), reads=[key], writes=[f"ki{R}"])
                P.op("dve", lambda e, tb=tb: e.tensor_copy(tb["kf"][:], tb["ki"][:]),
                     reads=[f"ki{R}"], writes=[f"kf{R}"])
                P.op("dve", lambda e, tb=tb, dst=dst: e.scalar_tensor_tensor(
                    dst[:], tb["kf"][:], -2 * math.pi, dst[:], ALU.mult, ALU.add),
                    reads=[f"kf{R}", key], writes=[key])
                P.op("dve", lambda e, tb=tb, dst=dst: e.tensor_scalar(
                    tb["kf"][:], dst[:], math.pi, -2 * math.pi, ALU.is_gt, ALU.mult),
                    reads=[key], writes=[f"kf{R}"])
                P.op("dve", lambda e, tb=tb, dst=dst: e.tensor_tensor(dst[:], dst[:], tb["kf"][:], ALU.add),
                     reads=[key, f"kf{R}"], writes=[key])
                P.op("act", lambda e, dst=dst: e.activation(dst[:], dst[:], AF.Sin),
                     reads=[key], writes=[key])
        for c in range(nk):
            P.dma("sp", act[:, c, :], actT[c], writes=[f"act{c}"])
        actkeys = [f"act{c}" for c in range(nk)]
        n = 0
        def load_w(bi_):
            col0_, w_, _ = blocks[bi_]
            P.dma("pool", wsb[bi_ % 2][:, :, 0:w_], W[:, col0_:col0_ + w_].rearrange("(c p) f -> p c f", p=Kp),
                  writes=[f"w{bi_ % 2}"])

        load_w(0)
        for bi, (col0, w, segs) in enumerate(blocks):
            wb = bi % 2
            if bi + 1 < len(blocks):
                load_w(bi + 1)
            tch = []
            for si, s in enumerate(segs):
                if s["kind"] == "T":
                    for ci, o in enumerate(range(0, s["w"], 128)):
                        tch.append((si, ci, s["off"] + o, min(128, s["w"] - o), len(tch)))
            assert len(tch) <= 4
            sg = stg[bi % 2]
            for t in range(NT):
                pb = n % 4
                tpb = n % 2
                n += 1
                for kc in range(nk):
                    P.op("pe", lambda e, kc=kc, t=t, pb=pb, wb=wb, w=w: e.matmul(
                        ps[pb][:, 0:w], act[:, kc, t * 128:(t + 1) * 128], wsb[wb][:, kc, 0:w],
                        start=(kc == 0), stop=(kc == nk - 1)),
                        reads=actkeys + [f"w{wb}"], writes=[f"ps{pb}"])
                s0 = segs[0]
                uniform = (all(x["kind"] == "T" and x["gain"] == s0["gain"] and x["rot"] == s0["rot"] and x["w"] == s0["w"]
                               and x["off"] == i * s0["w"] for i, x in enumerate(segs))
                           and s0["gain"] is not None and s0["w"] <= 128)
                if uniform:
                    ns, sw = len(segs), s0["w"]
                    Wd = ns * sw
                    gr = grep[s0["gain"]]
                    ps3 = ps[pb][:, 0:Wd].rearrange("p (s w) -> p s w", s=ns)
                    qn3 = qn[pb][:, 0:Wd].rearrange("p (s w) -> p s w", s=ns)
                    qb3 = qb[pb][:, 0:Wd].rearrange("p (s w) -> p s w", s=ns)
                    sq3 = sq[:, 0:Wd].rearrange("p (s w) -> p s w", s=ns)
                    P.op("act", lambda e, pb=pb, Wd=Wd, sw=sw: e.activation(sq[:, 0:Wd], ps[pb][:, 0:Wd], AF.Square,
                                                                            scale=float(sw) ** -0.5),
                         excl=[f"ps{pb}"], writes=["sq"])
                    P.op("dve", lambda e, pb=pb, ns=ns, sq3=sq3: e.reduce_sum(ss[pb][:, 0:ns], sq3, AX.X),
                         reads=["sq"], writes=[f"ss{pb}"])
                    P.op("act", lambda e, pb=pb, ns=ns: e.activation(rstd[pb][:, 0:ns], ss[pb][:, 0:ns], AF.Sqrt,
                                                                     bias=epsb[:, 0:1]),
                         reads=[f"ss{pb}", "c_eps"], writes=[f"rstd{pb}"])
                    P.op("dve", lambda e, pb=pb, ns=ns: e.reciprocal(rstd[pb][:, 0:ns], rstd[pb][:, 0:ns]),
                         reads=[f"rstd{pb}"], writes=[f"rstd{pb}"])
                    P.op("dve", lambda e, pb=pb, ns=ns, sw=sw, ps3=ps3, qn3=qn3: e.tensor_tensor(
                        qn3, ps3, rstd[pb][:, 0:ns].unsqueeze(2).to_broadcast([128, ns, sw]), ALU.mult),
                        reads=[f"rstd{pb}"], excl=[f"ps{pb}"], writes=[f"qn{pb}"])
                    P.op("dve", lambda e, ns=ns, sw=sw, qn3=qn3, gr=gr: e.tensor_tensor(
                        qn3, qn3, gr[:, 0:sw].unsqueeze(1).to_broadcast([128, ns, sw]), ALU.mult),
                        reads=["g_" + s0["gain"], f"qn{pb}"], writes=[f"qn{pb}"])
                    P.op("act", lambda e, pb=pb, Wd=Wd: e.activation(qb[pb][:, 0:Wd], qn[pb][:, 0:Wd], AF.Copy),
                         reads=[f"qn{pb}"], writes=[f"qb{pb}"])
                    if s0["rot"]:
                        R = s0["rot"]
                        half = R // 2
                        tb = tabs[R]
                        x1 = qn3[:, :, 0:half]
                        x2 = qn3[:, :, half:R]
                        cs = tb["cos"][:, t, :].unsqueeze(1).to_broadcast([128, ns, half])
                        sn = tb["sin"][:, t, :].unsqueeze(1).to_broadcast([128, ns, half])
                        r = [rt[k][:, 0:ns * half].rearrange("p (s w) -> p s w", s=ns) for k in range(4)]
                        rk = [f"rt{k}" for k in range(4)]
                        tk = [f"cos{R}", f"sin{R}", f"qn{pb}"]
                        for k, (xa, tbv) in enumerate(((x1, cs), (x2, sn), (x2, cs), (x1, sn))):
                            P.op("pool" if k % 2 == 0 else "dve",
                                 lambda e, k=k, xa=xa, tbv=tbv, r=r: e.tensor_tensor(r[k], xa, tbv, ALU.mult),
                                 reads=tk, writes=[rk[k]])
                        P.op("dve", lambda e, r=r, qb3=qb3, half=half: e.tensor_tensor(
                            qb3[:, :, 0:half], r[0], r[1], ALU.subtract), reads=[rk[0], rk[1]], writes=[f"qb{pb}"])
                        P.op("dve", lambda e, r=r, qb3=qb3, half=half, R=R: e.tensor_tensor(
                            qb3[:, :, half:R], r[2], r[3], ALU.add), reads=[rk[2], rk[3]], writes=[f"qb{pb}"])
                    for (si, ci, co, wc, slot) in tch:
                        P.op("pe", lambda e, co=co, wc=wc, slot=slot, pb=pb, tpb=tpb: e.transpose(
                            pT[tpb][0:wc, slot * 128:(slot + 1) * 128], qb[pb][:, co:co + wc], ident[:]),
                            reads=[f"qb{pb}", "ident"], writes=[f"pT{tpb}"])
                    src3 = pT[tpb][0:sw, 0:ns * 128].rearrange("p (s t) -> p s t", s=ns)
                    dst3 = sg[0:sw, 0:ns, t * 128:(t + 1) * 128]
                    skeys = [f"stg{bi % 2}_{slot}_{t}" for slot in range(ns)]
                    if tpb == 0:
                        P.op("act", lambda e, src3=src3, dst3=dst3: e.activation(dst3, src3, AF.Copy),
                             excl=[f"pT{tpb}"], writes=skeys)
                    else:
                        P.op("dve", lambda e, src3=src3, dst3=dst3: e.tensor_copy(dst3, src3),
                             excl=[f"pT{tpb}"], writes=skeys)
                    continue
                normed = [(si, s) for si, s in enumerate(segs) if s["gain"]]
                for si, s in normed:
                    o, sw = s["off"], s["w"]
                    P.op("act", lambda e, o=o, sw=sw, si=si, pb=pb: e.activation(
                        sq[:, o:o + sw], ps[pb][:, o:o + sw], AF.Square, scale=float(sw) ** -0.5,
                        accum_out=ss[pb][:, si:si + 1]),
                        excl=[f"ps{pb}"], writes=[f"sq{si}", f"ss{pb}_{si}"])
                if normed:
                    ns = len(segs)
                    P.op("act", lambda e, pb=pb, ns=ns: e.activation(rstd[pb][:, 0:ns], ss[pb][:, 0:ns], AF.Sqrt,
                                                                     bias=epsb[:, 0:1]),
                         reads=[f"ss{pb}_{si}" for si, _ in normed] + ["c_eps"], writes=[f"rstd{pb}"])
                    P.op("dve", lambda e, pb=pb, ns=ns: e.reciprocal(rstd[pb][:, 0:ns], rstd[pb][:, 0:ns]),
                         reads=[f"rstd{pb}"], writes=[f"rstd{pb}"])
                for si, s in enumerate(segs):
                    o, sw = s["off"], s["w"]
                    if s["gain"]:
                        gr = grep[s["gain"]]
                        P.op("dve", lambda e, o=o, sw=sw, si=si, pb=pb, gr=gr: e.scalar_tensor_tensor(
                            qn[pb][:, o:o + sw], ps[pb][:, o:o + sw], rstd[pb][:, si:si + 1], gr[:, 0:sw],
                            ALU.mult, ALU.mult),
                            reads=[f"rstd{pb}", "g_" + s["gain"]], excl=[f"ps{pb}"], writes=[f"qn{pb}_{si}"])
                        P.op("act", lambda e, o=o, sw=sw, pb=pb: e.activation(qb[pb][:, o:o + sw], qn[pb][:, o:o + sw],
                                                                              AF.Copy),
                             reads=[f"qn{pb}_{si}"], writes=[f"qb{pb}_{si}"])
                    else:
                        P.op("act", lambda e, o=o, sw=sw, pb=pb: e.activation(qb[pb][:, o:o + sw], ps[pb][:, o:o + sw],
                                                                              AF.Copy),
                             excl=[f"ps{pb}"], writes=[f"qb{pb}_{si}"])
                    if s["rot"]:
                        R = s["rot"]
                        half = R // 2
                        tb = tabs[R]
                        x1 = qn[pb][:, o:o + half]
                        x2 = qn[pb][:, o + half:o + R]
                        cs = tb["cos"][:, t, :]
                        sn = tb["sin"][:, t, :]
                        r = [rt[k][:, 0:half] for k in range(4)]
                        rk = [f"rt{k}" for k in range(4)]
                        tk = [f"cos{R}", f"sin{R}", f"qn{pb}_{si}"]
                        P.op("pool", lambda e, r=r, x1=x1, cs=cs: e.tensor_tensor(r[0], x1, cs, ALU.mult),
                             reads=tk, writes=[rk[0]])
                        P.op("pool", lambda e, r=r, x2=x2, sn=sn: e.tensor_tensor(r[1], x2, sn, ALU.mult),
                             reads=tk, writes=[rk[1]])
                        P.op("pool", lambda e, r=r, x2=x2, cs=cs: e.tensor_tensor(r[2], x2, cs, ALU.mult),
                             reads=tk, writes=[rk[2]])
                        P.op("pool", lambda e, r=r, x1=x1, sn=sn: e.tensor_tensor(r[3], x1, sn, ALU.mult),
                             reads=tk, writes=[rk[3]])
                        P.op("dve", lambda e, r=r, o=o, half=half, pb=pb: e.tensor_tensor(
                            qb[pb][:, o:o + half], r[0], r[1], ALU.subtract),
                            reads=[rk[0], rk[1]], writes=[f"qb{pb}_{si}"])
                        P.op("dve", lambda e, r=r, o=o, half=half, R=R, pb=pb: e.tensor_tensor(
                            qb[pb][:, o + half:o + R], r[2], r[3], ALU.add),
                            reads=[rk[2], rk[3]], writes=[f"qb{pb}_{si}"])
                    if s["kind"] == "M":
                        mdst = s["dst"](t) if callable(s["dst"]) else s["dst"][t * 128:(t + 1) * 128, :]
                        P.dma("sp", mdst, qb[pb][:, o:o + sw],
                              reads=[f"qb{pb}_{si}"], writes=[f"{tag}M{bi}_{si}_{t}"])
                import os
                if tch and not os.environ.get("SKIP_TR"):
                    for (si, ci, co, wc, slot) in tch:
                        P.op("pe", lambda e, co=co, wc=wc, slot=slot, pb=pb, tpb=tpb: e.transpose(
                            pT[tpb][0:wc, slot * 128:(slot + 1) * 128], qb[pb][:, co:co + wc], ident[:]),
                            reads=[f"qb{pb}_{si}", "ident"], writes=[f"pT{tpb}"])
                    for (si, ci, co, wc, slot) in tch:
                        if tpb == 0:
                            P.op("act", lambda e, wc=wc, slot=slot, pb=pb, tpb=tpb, t=t, sg=sg: e.activation(
                                sg[0:wc, slot, t * 128:(t + 1) * 128], pT[tpb][0:wc, slot * 128:(slot + 1) * 128], AF.Copy),
                                excl=[f"pT{tpb}"], writes=[f"stg{bi % 2}_{slot}_{t}"])
                        else:
                            P.op("dve", lambda e, wc=wc, slot=slot, pb=pb, tpb=tpb, t=t, sg=sg: e.tensor_copy(
                                sg[0:wc, slot, t * 128:(t + 1) * 128], pT[tpb][0:wc, slot * 128:(slot + 1) * 128]),
                                excl=[f"pT{tpb}"], writes=[f"stg{bi % 2}_{slot}_{t}"])
            import os
            for (si, ci, co, wc, slot) in tch:
                if os.environ.get("SKIP_TDMA"):
                    continue
                P.dma("sp", segs[si]["dst"][ci], sg[0:wc, slot, :],
                      reads=[f"stg{bi % 2}_{slot}_{t}" for t in range(NT)], writes=[f"{tag}T{bi}_{slot}"])
        P.emit()


def _dram_in(nc, name, shape, dt):
    return nc.dram_tensor(name, list(shape), dt, kind="ExternalInput").ap()


def _dram_out(nc, name, shape, dt):
    return nc.dram_tensor(name, list(shape), dt, kind="ExternalOutput").ap()


def k1_body(P, nc, mixer, TOK, x, g, pos, Wd, Gd, out, hTd, scratch):
    D = D_MODEL
    phase_norm(P, nc, TOK, x, g, hTd)
    if mixer in ("moba", "diff"):
        for wname, gname, dst in (("wq", "gq", out["qT"]), ("wk", "gk", out["kT"])):
            blocks = []
            for b in range(4):
                segs = [dict(off=j * 128, w=128, gain=gname, rot=32, kind="T", dst=[dst[b * 4 + j]]) for j in range(4)]
                blocks.append((b * 512, 512, segs))
            phase_proj(P, nc, TOK, hTd, 16, 128, Wd[wname], blocks, Gd, pos, wname)
        blocks = [(b * 512, 512, [dict(off=0, w=512, gain=None, rot=0, kind="M", dst=out["v"][:, b * 512:(b + 1) * 512])])
                  for b in range(4)]
        phase_proj(P, nc, TOK, hTd, 16, 128, Wd["wv"], blocks, Gd, pos, "wv")
    elif mixer == "swa":
        blocks = []
        for b in range(4):
            segs = [dict(off=j * 64, w=64, gain="gq", rot=16, kind="T", dst=[out["qT"][b * 8 + j]]) for j in range(8)]
            blocks.append((b * 512, 512, segs))
        blocks2 = []
        for b in range(8):
            segs = [dict(off=j * 64, w=64, gain="gq", rot=16, kind="T", dst=[out["qT"][b * 4 + j]]) for j in range(4)]
            blocks2.append((b * 256, 256, segs))
        phase_proj(P, nc, TOK, hTd, 16, 128, Wd["wq"], blocks2, Gd, pos, "wq")
        segs = [dict(off=j * 64, w=64, gain="gk", rot=16, kind="T", dst=[out["kT"][j]]) for j in range(4)]
        phase_proj(P, nc, TOK, hTd, 16, 128, Wd["wk"], [(0, 256, segs)], Gd, pos, "wk")
        phase_proj(P, nc, TOK, hTd, 16, 128, Wd["wv"],
                   [(0, 256, [dict(off=0, w=256, gain=None, rot=0, kind="M", dst=out["v"])])], Gd, pos, "wv")
    elif mixer == "mla":
        cqT, ckvT = scratch["cqT"], scratch["ckvT"]
        phase_proj(P, nc, TOK, hTd, 16, 128, Wd["wq_a"],
                   [(0, 512, [dict(off=0, w=512, gain="g_qa", rot=0, kind="T", dst=[cqT[c] for c in range(4)])])],
                   Gd, pos, "wqa")
        phase_proj(P, nc, TOK, hTd, 16, 128, Wd["wkv_a"],
                   [(0, 512, [dict(off=0, w=512, gain="g_kva", rot=0, kind="T", dst=[ckvT[c] for c in range(4)])]),
                    (512, 64, [dict(off=0, w=64, gain="g_kr", rot=64, kind="T", dst=[out["krT"]])])],
                   Gd, pos, "wkva")
        blocks = []
        for b in range(8):
            segs = []
            for j in range(2):
                h = b * 2 + j
                segs.append(dict(off=j * 192, w=128, gain="g_qn", rot=0, kind="T", dst=[out["qnT"][h]]))
                segs.append(dict(off=j * 192 + 128, w=64, gain="g_qr", rot=64, kind="T", dst=[out["qrT"][h]]))
            blocks.append((b * 384, 384, segs))
        phase_proj(P, nc, TOK, cqT, 4, 128, Wd["wq_b"], blocks, Gd, pos, "wqb")
        blocks = []
        for b in range(8):
            segs = []
            for j in range(2):
                h = b * 2 + j
                segs.append(dict(off=j * 256, w=128, gain="g_kn", rot=0, kind="T", dst=[out["knT"][h]]))
                segs.append(dict(off=j * 256 + 128, w=128, gain=None, rot=0, kind="M",
                                 dst=out["v"][:, h * 128:(h + 1) * 128]))
            blocks.append((b * 512, 512, segs))
        phase_proj(P, nc, TOK, ckvT, 4, 128, Wd["wkv_b"], blocks, Gd, pos, "wkvb")


K1_SPECS = {
    "moba": dict(W=dict(wq=(2048, 2048), wk=(2048, 2048), wv=(2048, 2048)), G=dict(gq=128, gk=128),
                 out=dict(qT=(16, 128), kT=(16, 128), v=2048)),
    "diff": dict(W=dict(wq=(2048, 2048), wk=(2048, 2048), wv=(2048, 2048)), G=dict(gq=128, gk=128),
                 out=dict(qT=(16, 128), kT=(16, 128), v=2048)),
    "swa": dict(W=dict(wq=(2048, 2048), wk=(2048, 256), wv=(2048, 256)), G=dict(gq=64, gk=64),
                out=dict(qT=(32, 64), kT=(4, 64), v=256)),
    "mla": dict(W=dict(wq_a=(2048, 512), wkv_a=(2048, 576), wq_b=(512, 3072), wkv_b=(512, 4096)),
                G=dict(g_qa=512, g_kva=512, g_qn=128, g_kn=128, g_qr=64, g_kr=64),
                out=dict(qnT=(16, 128), qrT=(16, 64), knT=(16, 128), krT=(64,), v=2048)),
}


def build_k1(mixer, TOK):
    nc = bass.Bass("TRN2", target_bir_lowering=False)
    D = D_MODEL
    NT = TOK // 128
    spec = K1_SPECS[mixer]
    x = _dram_in(nc, "x", [TOK, D], F32)
    g = _dram_in(nc, "g", [1, D], F32)
    pos = _dram_in(nc, "pos", [128, NT], F32)
    Wd = {k: _dram_in(nc, k, shp, F32) for k, shp in spec["W"].items()}
    Gd = {k: _dram_in(nc, k, [1, w], F32) for k, w in spec["G"].items()}
    out = {}
    for k, shp in spec["out"].items():
        if k == "v":
            out[k] = _dram_out(nc, k, [TOK, shp], BF16)
        elif len(shp) == 1:
            out[k] = _dram_out(nc, k, [shp[0], TOK], BF16)
        else:
            out[k] = _dram_out(nc, k, [shp[0], shp[1], TOK], BF16)
    hTd = nc.dram_tensor("hTd", [16, 128, TOK], BF16).ap()
    scratch = {}
    if mixer == "mla":
        scratch["cqT"] = nc.dram_tensor("cqT", [4, 128, TOK], BF16).ap()
        scratch["ckvT"] = nc.dram_tensor("ckvT", [4, 128, TOK], BF16).ap()
    P = Prog(nc)
    k1_body(P, nc, mixer, TOK, x, g, pos, Wd, Gd, out, hTd, scratch)
    P.close()
    return nc


def phase_attn(P, nc, kind, U, S, io, lam_init=0.0):
    NKT = S // 128
    NQB = S // 512
    nmap = 2 if kind == "diff" else 1
    ndv = 2 if kind == "diff" else 1
    dv = 128 * ndv
    scale = {"moba": 128 ** -0.5, "mla": 192 ** -0.5, "diff": 128 ** -0.5}[kind]
    with contextlib.ExitStack() as es:
        T = make_T(es, nc)
        qA = [T(f"a_qA{m}", [128, S], BF16) for m in range(nmap)]
        kA = [T(f"a_kA{m}", [128, S], BF16) for m in range(nmap)]
        vS = T("a_v", [128, NKT, dv], BF16)
        if kind == "mla":
            qB = T("a_qB", [64, S], BF16)
            kB = T("a_kB", [64, S], BF16)
        ones = T("a_ones", [128, 128], BF16)
        mask = T("a_mask", [128, 4, 512], BF16)
        maskf = T("a_maskf", [128, 4, 512], F32)
        NPT = 3
        Pt = [[T(f"a_P{m}_{i}", [128, 512], BF16) for i in range(NPT)] for m in range(nmap)]
        Rr = [T(f"a_R{m}", [128, 512], F32) for m in range(nmap)]
        osb = [T(f"a_osb{i}", [128, 512], BF16) for i in range(2 * ndv)]
        S_ps = [psum_T(es, nc, f"a_S{i}", [128, 512], F32) for i in range(2)]
        if kind == "diff":
            O_ps = [[[psum_T(es, nc, f"a_O{m}{d}", [128, 512], F32) for d in range(2)] for m in range(2)]]
            L_ps = [[psum_T(es, nc, f"a_L{m}", [128, 512], F32) for m in range(2)]]
            nob = 1
        else:
            O_ps = [[[psum_T(es, nc, f"a_O{b}", [128, 512], F32)]] for b in range(2)]
            L_ps = [[psum_T(es, nc, f"a_L{b}", [128, 512], F32)] for b in range(2)]
            nob = 2
        P.op("pool", lambda e: e.memset(ones[:], 1.0), writes=["ones"])
        P.op("pool", lambda e: e.iota(maskf[:], [[-128, 4], [1, 512]], base=0, channel_multiplier=-1,
                                      allow_small_or_imprecise_dtypes=True), writes=["maskf"])
        P.op("dve", lambda e: e.tensor_single_scalar(mask[:], maskf[:], 0.0, ALU.is_ge),
             reads=["maskf"], writes=["mask"])
        if kind == "moba":
            NB = S // 256
            ident = T("a_ident", [128, 128], BF16)
            build_identity(P, nc, ident, es)
            Eall = T("a_Eall", [NB, NB, 128], BF16)
            Ef = T("a_Ef", [NB, NB, 128], F32)
            biasT = T("a_biasT", [NB, S], BF16)
            km = T("a_km", [128, NB], F32)
            kmh = T("a_kmh", [128, NB], BF16)
            kmhf = T("a_kmhf", [128, NB], F32)
            kml = T("a_kml", [128, NB], BF16)
            gpad = T("a_gpad", [128, NB], F32)
            m8 = T("a_m8", [128, 8], F32)
            brow = [T(f"a_brow{i}", [128, NB], BF16) for i in range(2)]
            g_ps = psum_T(es, nc, "a_gps", [128, 512], F32)
            b_ps = psum_T(es, nc, "a_bps", [128, 1024], BF16)
            P.op("pool", lambda e: e.iota(Ef[:], [[-1, NB], [0, 128]], base=0, channel_multiplier=1,
                                          allow_small_or_imprecise_dtypes=True), writes=["Ef"])
            P.op("dve", lambda e: e.tensor_single_scalar(Eall[:], Ef[:], 0.0, ALU.is_equal),
                 reads=["Ef"], writes=["Eall"])
        if kind == "diff":
            lv = {n: T("a_" + n, [128, 128], F32) for n in ("lq1", "lk1", "lq2", "lk2")}
            ltmp = T("a_ltmp", [128, 128], F32)
            lsum = T("a_lsum", [128, 2], F32)
            lam = T("a_lam", [128, 1], F32)
            gsub = T("a_gsub", [128, 2], F32)
            epsb = T("a_epsb", [128, 1], F32)
            od = [T(f"a_od{d}", [128, 512], F32) for d in range(2)]
            t2 = T("a_t2", [128, 512], F32)
            sqb = [T(f"a_sqb{d}", [128, 512], BF16) for d in range(2)]
            rs = T("a_rs", [128, 512], F32)
            P.op("pool", lambda e: e.memset(epsb[:], EPS), writes=["c_eps"])
            for n in lv:
                P.dma("sp", lv[n][:], io[n][0:1, :].partition_broadcast(128), writes=[n])
            P.dma("sp", gsub[:], io["gsub"][:, :], writes=["gsub"])
            for i, (a, b) in enumerate((("lq1", "lk1"), ("lq2", "lk2"))):
                P.op("dve", lambda e, a=a, b=b: e.tensor_tensor(ltmp[:], lv[a][:], lv[b][:], ALU.mult),
                     reads=[a, b], writes=["ltmp"])
                P.op("dve", lambda e, i=i: e.reduce_sum(lsum[:, i:i + 1], ltmp[:], AX.X),
                     reads=["ltmp"], writes=[f"lsum{i}"])
            P.op("act", lambda e: e.activation(lsum[:], lsum[:], AF.Exp), reads=["lsum0", "lsum1"], writes=["lsum"])
            P.op("dve", lambda e: e.tensor_tensor(lam[:], lsum[:, 0:1], lsum[:, 1:2], ALU.subtract),
                 reads=["lsum"], writes=["lam"])
            P.op("dve", lambda e: e.tensor_scalar(lam[:], lam[:], lam_init, None, ALU.add),
                 reads=["lam"], writes=["lam"])
            P.op("dve", lambda e: e.tensor_scalar(gsub[:], gsub[:], 1.0 - lam_init, None, ALU.mult),
                 reads=["gsub"], writes=["gsub"])

        def load_unit(u):
            for m in range(nmap):
                qsrc = io["qA"][u, m] if kind == "diff" else io["qA"][u]
                ksrc = io["kA"][u, m] if kind == "diff" else io["kA"][u]
                for h in range(4):
                    cs = slice(h * (S // 4), (h + 1) * (S // 4))
                    P.dma("sp", qA[m][:, cs], qsrc[:, cs], writes=[f"qA{m}_{h}"])
                    P.dma("sp", kA[m][:, cs], ksrc[:, cs], writes=[f"kA{m}_{h}"])
            for h in range(4):
                ts = slice(h * (NKT // 4), (h + 1) * (NKT // 4))
                P.dma("sp", vS[:, ts, :], io["v"][u][:, ts, :], writes=[f"v_{h}"])
            if kind == "mla":
                P.dma("sp", qB[:], io["qB"][u], writes=["qB"])
                P.dma("sp", kB[:], io["kB"][u], writes=["kB"])
            if kind == "moba":
                qkeys0 = [f"qA0_{h}" for h in range(4)]
                kkeys0 = [f"kA0_{h}" for h in range(4)]
                P.op("pool", lambda e: e.memset(biasT[:], 0.0), writes=["biasT"])
                P.op("pool", lambda e: e.memset(gpad[:], -1e30), writes=["gpad"])
                for i in range(2):
                    P.op("pool", lambda e, i=i: e.memset(brow[i][:], 0.0), writes=[f"brow{i}"])
                P.op("dve", lambda e: e.reduce_sum(km[:], kA[0][:].rearrange("p (b k) -> p b k", k=256), AX.X),
                     reads=kkeys0, writes=["km"])
                P.op("dve", lambda e: e.tensor_scalar(km[:], km[:], 1.0 / 256, None, ALU.mult), reads=["km"], writes=["km"])
                P.op("dve", lambda e: e.tensor_copy(kmh[:], km[:]), reads=["km"], writes=["kmh"])
                P.op("dve", lambda e: e.tensor_copy(kmhf[:], kmh[:]), reads=["kmh"], writes=["kmhf"])
                P.op("dve", lambda e: e.tensor_tensor(kml[:], km[:], kmhf[:], ALU.subtract),
                     reads=["km", "kmhf"], writes=["kml"])
                for qt in range(NKT):
                    ob = qt // 2
                    if ob <= 3:
                        continue
                    bi = qt % 2
                    P.op("pe", lambda e, qt=qt: e.matmul(g_ps[:, 0:NB], qA[0][:, qt * 128:(qt + 1) * 128], kmh[:],
                                                         start=True, stop=False),
                         reads=qkeys0 + ["kmh"], writes=["g_ps"])
                    P.op("pe", lambda e, qt=qt: e.matmul(g_ps[:, 0:NB], qA[0][:, qt * 128:(qt + 1) * 128], kml[:],
                                                         start=False, stop=True),
                         reads=qkeys0 + ["kml"], writes=["g_ps"])
                    P.op("dve", lambda e, ob=ob: e.tensor_copy(gpad[:, 0:ob], g_ps[:, 0:ob]),
                         excl=["g_ps"], writes=["gpad"])
                    P.op("dve", lambda e: e.max(m8[:], gpad[:]), reads=["gpad"], writes=["m8"])
                    P.op("dve", lambda e, ob=ob, bi=bi: e.tensor_scalar(brow[bi][:, 0:ob], gpad[:, 0:ob], m8[:, 2:3],
                                                                        -1000.0, ALU.is_lt, ALU.mult),
                         reads=["gpad", "m8"], writes=[f"brow{bi}"])
                    P.op("pe", lambda e, bi=bi: e.transpose(b_ps[0:NB, 0:128], brow[bi][:], ident[:]),
                         reads=[f"brow{bi}", "ident"], writes=["b_ps"])
                    P.op("act", lambda e, qt=qt: e.activation(biasT[:, qt * 128:(qt + 1) * 128], b_ps[0:NB, 0:128], AF.Copy),
                         excl=["b_ps"], writes=["biasT"])

        steps = []
        npt = 0
        nsb = 0
        nqb_glob = 0
        for u in range(U):
            for qb in range(NQB):
                ob_i = nqb_glob % nob
                nqb_glob += 1
                qi = dict(u=u, qb=qb, ob_i=ob_i, qcols=slice(qb * 512, (qb + 1) * 512), nkt=4 * (qb + 1),
                          Okey=[[f"O{ob_i}_{m}_{d}" for d in range(ndv)] for m in range(nmap)],
                          Lkey=[f"L{ob_i}_{m}" for m in range(nmap)], par=nqb_glob % 2)
                for kt in range(qi["nkt"]):
                    for m in range(nmap):
                        steps.append(dict(qi=qi, kt=kt, m=m, sb=nsb % 2, pi=npt % NPT,
                                          first=(qb == 0 and kt == 0 and m == 0),
                                          last=(kt == qi["nkt"] - 1 and m == nmap - 1)))
                        nsb += 1
                        npt += 1

        def emit_S(st):
            qi, kt, m, sb, pi = st["qi"], st["kt"], st["m"], st["sb"], st["pi"]
            qb, qcols = qi["qb"], qi["qcols"]
            j = kt - 4 * qb
            kcols = slice(kt * 128, (kt + 1) * 128)
            kh = (kt * 128) // (S // 4)
            qh = (qb * 512) // (S // 4)
            Sp = S_ps[sb]
            Pm = Pt[m][pi]
            last_s = (kind == "diff")
            P.op("pe", lambda e: e.matmul(Sp[:], kA[m][:, kcols], qA[m][:, qcols], start=True, stop=last_s),
                 reads=[f"kA{m}_{kh}", f"qA{m}_{qh}"], writes=[f"S{sb}"])
            if kind == "mla":
                P.op("pe", lambda e: e.matmul(Sp[:], kB[:, kcols], qB[:, qcols], start=False, stop=True),
                     reads=["kB", "qB"], writes=[f"S{sb}"])
            if kind == "moba":
                blk = kt // 2
                P.op("pe", lambda e: e.matmul(Sp[:], Eall[:, blk, :], biasT[:, qcols], start=False, stop=True),
                     reads=["Eall", "biasT"], writes=[f"S{sb}"])
            P.op("act", lambda e: e.activation(Pm[:], Sp[:], AF.Exp, scale=scale),
                 excl=[f"S{sb}"], writes=[f"P{m}_{pi}"])
            if j >= 0:
                P.op("pool", lambda e: e.tensor_tensor(Pm[:], Pm[:], mask[:, j, :], ALU.mult),
                     reads=["mask"], writes=[f"P{m}_{pi}"])

        def emit_PV(st):
            qi, kt, m, pi = st["qi"], st["kt"], st["m"], st["pi"]
            ob_i, nkt = qi["ob_i"], qi["nkt"]
            Pm = Pt[m][pi]
            for d in range(ndv):
                P.op("pe", lambda e, d=d: e.matmul(O_ps[ob_i][m][d][:], vS[:, kt, d * 128:(d + 1) * 128], Pm[:],
                                                   start=(kt == 0), stop=(kt == nkt - 1)),
                     reads=[f"v_{kt // (NKT // 4)}", f"P{m}_{pi}"], writes=[qi["Okey"][m][d]])
            P.op("pe", lambda e: e.matmul(L_ps[ob_i][m][:], ones[:], Pm[:], start=(kt == 0), stop=(kt == nkt - 1)),
                 reads=["ones", f"P{m}_{pi}"], writes=[qi["Lkey"][m]])

        def emit_final(qi):
            u, qb, ob_i, qcols, Okey, Lkey = qi["u"], qi["qb"], qi["ob_i"], qi["qcols"], qi["Okey"], qi["Lkey"]
            if kind != "diff":
                ob_sb = osb[ob_i]
                P.op("dve", lambda e: e.reciprocal(Rr[0][:], L_ps[ob_i][0][:]), excl=[Lkey[0]], writes=["R0"])
                P.op("dve", lambda e: e.tensor_tensor(ob_sb[:], O_ps[ob_i][0][0][:], Rr[0][:], ALU.mult),
                     reads=["R0"], excl=[Okey[0][0]], writes=[f"osb{ob_i}"])
                P.dma("sp", io["oT"][u][:, qcols], ob_sb[:], reads=[f"osb{ob_i}"], writes=[f"oT{u}_{qb}"])
            else:
                for m in range(2):
                    P.op("dve", lambda e, m=m: e.reciprocal(Rr[m][:], L_ps[0][m][:]), excl=[Lkey[m]], writes=[f"R{m}"])
                P.op("dve", lambda e: e.tensor_scalar(Rr[1][:], Rr[1][:], lam[:, 0:1], None, ALU.mult),
                     reads=["R1", "lam"], writes=["R1"])
                for d in range(2):
                    P.op("dve", lambda e, d=d: e.tensor_tensor(od[d][:], O_ps[0][0][d][:], Rr[0][:], ALU.mult),
                         reads=["R0"], excl=[Okey[0][d]], writes=[f"od{d}"])
                    P.op("dve", lambda e, d=d: e.tensor_tensor(t2[:], O_ps[0][1][d][:], Rr[1][:], ALU.mult),
                         reads=["R1"], excl=[Okey[1][d]], writes=["t2"])
                    P.op("dve", lambda e, d=d: e.tensor_tensor(od[d][:], od[d][:], t2[:], ALU.subtract),
                         reads=["t2", f"od{d}"], writes=[f"od{d}"])
                    P.op("act", lambda e, d=d: e.activation(sqb[d][:], od[d][:], AF.Square),
                         reads=[f"od{d}"], writes=[f"sqb{d}"])
                ssp = S_ps[0]
                for d in range(2):
                    P.op("pe", lambda e, d=d: e.matmul(ssp[:], ones[:], sqb[d][:], start=(d == 0), stop=(d == 1)),
                         reads=["ones", f"sqb{d}"], writes=["S0"])
                P.op("act", lambda e: e.activation(rs[:], ssp[:], AF.Sqrt, bias=epsb[:, 0:1], scale=1.0 / 256),
                     reads=["c_eps"], excl=["S0"], writes=["rs"])
                P.op("dve", lambda e: e.reciprocal(rs[:], rs[:]), reads=["rs"], writes=["rs"])
                for d in range(2):
                    ob_sb = osb[qi["par"] * 2 + d]
                    okey = f"osb{qi['par'] * 2 + d}"
                    P.op("dve", lambda e, d=d, ob_sb=ob_sb: e.scalar_tensor_tensor(
                        ob_sb[:], od[d][:], gsub[:, d:d + 1], rs[:], ALU.mult, ALU.mult),
                        reads=[f"od{d}", "gsub", "rs"], writes=[okey])
                    P.dma("sp", io["oT"][u][d * 128:(d + 1) * 128, qcols], ob_sb[:], reads=[okey],
                          writes=[f"oT{u}_{qb}_{d}"])

        for i in range(len(steps) + 1):
            if i < len(steps):
                st = steps[i]
                if st["first"]:
                    if i >= 1:
                        emit_PV(steps[i - 1])
                        if steps[i - 1]["last"]:
                            emit_final(steps[i - 1]["qi"])
                    load_unit(st["qi"]["u"])
                    emit_S(st)
                    continue
                emit_S(st)
            if i >= 1:
                pst = steps[i - 1]
                emit_PV(pst)
                if pst["last"]:
                    emit_final(pst["qi"])
        P.emit()


def build_k2(kind, U, S, lam_init=0.0):
    nc = bass.Bass("TRN2", target_bir_lowering=False)
    NKT = S // 128
    io = {}
    if kind == "diff":
        io["qA"] = _dram_in(nc, "qA", [U, 2, 128, S], BF16)
        io["kA"] = _dram_in(nc, "kA", [U, 2, 128, S], BF16)
        io["v"] = _dram_in(nc, "v", [U, 128, NKT, 256], BF16)
        io["oT"] = _dram_out(nc, "oT", [U, 256, S], BF16)
        for n in ("lq1", "lk1", "lq2", "lk2"):
            io[n] = _dram_in(nc, n, [1, 128], F32)
        io["gsub"] = _dram_in(nc, "gsub", [128, 2], F32)
    else:
        io["qA"] = _dram_in(nc, "qA", [U, 128, S], BF16)
        io["kA"] = _dram_in(nc, "kA", [U, 128, S], BF16)
        io["v"] = _dram_in(nc, "v", [U, 128, NKT, 128], BF16)
        io["oT"] = _dram_out(nc, "oT", [U, 128, S], BF16)
        if kind == "mla":
            io["qB"] = _dram_in(nc, "qB", [U, 64, S], BF16)
            io["kB"] = _dram_in(nc, "kB", [U, 64, S], BF16)
    P = Prog(nc)
    phase_attn(P, nc, kind, U, S, io, lam_init)
    P.close()
    return nc


def phase_swa(P, nc, U, S, io):
    NT = S // 128
    scale = 64 ** -0.5
    with contextlib.ExitStack() as es:
        T = make_T(es, nc)
        qS = T("w_q", [64, 8, S], BF16)
        kS = T("w_k", [64, S], BF16)
        vS = T("w_v", [128, NT, 64], BF16)
        ones = T("w_ones", [128, 64], BF16)
        maskf = T("w_maskf", [128, 128], F32)
        mask2 = T("w_mask2", [128, 2, 256], BF16)
        esink = T("w_esink", [128, 8], F32)
        Pt = [T(f"w_P{i}", [128, 512], BF16) for i in range(3)]
        Lp = [T(f"w_Lp{i}", [64, 256], F32) for i in range(2)]
        ostg = [T(f"w_ostg{i}", [64, 8, 512], BF16) for i in range(2)]
        S_ps = [psum_T(es, nc, f"w_S{i}", [128, 512], F32) for i in range(2)]
        OL_ps = [psum_T(es, nc, f"w_OL{i}", [64, 512], F32) for i in range(2)]
        P.op("pool", lambda e: e.memset(ones[:], 1.0), writes=["ones"])
        P.op("pool", lambda e: e.iota(maskf[:], [[1, 128]], base=0, channel_multiplier=-1,
                                      allow_small_or_imprecise_dtypes=True), writes=["maskf"])
        for h in range(2):
            P.op("dve", lambda e, h=h: e.tensor_single_scalar(mask2[:, h, 0:128], maskf[:], 0.0, ALU.is_ge),
                 reads=["maskf"], writes=[f"mask2_{h}c"])
            P.op("dve", lambda e, h=h: e.tensor_single_scalar(mask2[:, h, 128:256], maskf[:], 0.0, ALU.is_lt),
                 reads=["maskf"], writes=[f"mask2_{h}p"])
        mkeys = ["mask2_0c", "mask2_0p", "mask2_1c", "mask2_1p"]
        cnt = 0
        for u in range(U):
            for h in range(8):
                P.dma("sp", qS[:, h, :], io["qT"][u, h], writes=[f"q{h}"])
            P.dma("sp", kS[:], io["kT"][u], writes=["k"])
            for h in range(4):
                ts = slice(h * (NT // 4), (h + 1) * (NT // 4))
                P.dma("sp", vS[:, ts, :], io["v"][u][:, ts, :], writes=[f"v{h}"])
            vkeys = [f"v{h}" for h in range(4)]
            P.dma("sp", esink[:], io["sinks"][u].partition_broadcast(128), writes=["esink"])
            P.op("act", lambda e: e.activation(esink[:], esink[:], AF.Exp), reads=["esink"], writes=["esink"])
            for n in range(NT):
                g = n // 4
                og = ostg[g % 2]
                tcols = slice(n * 128, (n + 1) * 128)
                pcols = slice((n - 1) * 128, n * 128)
                for hp in range(4):
                    sb = cnt % 2
                    pi = cnt % 3
                    cnt += 1
                    Sp, Pm, OL, Lq = S_ps[sb], Pt[pi], OL_ps[sb], Lp[sb]
                    for hh in range(2):
                        h = hp * 2 + hh
                        P.op("pe", lambda e, Sp=Sp, hh=hh, h=h, tcols=tcols: e.matmul(
                            Sp[:, hh * 256:hh * 256 + 128], kS[:, tcols], qS[:, h, tcols], start=True, stop=True),
                            reads=["k", f"q{h}"], writes=[f"S{sb}"])
                        if n > 0:
                            P.op("pe", lambda e, Sp=Sp, hh=hh, h=h, tcols=tcols, pcols=pcols: e.matmul(
                                Sp[:, hh * 256 + 128:hh * 256 + 256], kS[:, pcols], qS[:, h, tcols], start=True, stop=True),
                                reads=["k", f"q{h}"], writes=[f"S{sb}"])
                    if n > 0:
                        P.op("act", lambda e, Sp=Sp, Pm=Pm: e.activation(Pm[:], Sp[:], AF.Exp, scale=scale),
                             excl=[f"S{sb}"], writes=[f"P{pi}"])
                        P.op("pool", lambda e, Pm=Pm: e.tensor_tensor(
                            Pm[:], Pm[:], mask2[:].rearrange("p h c -> p (h c)"), ALU.mult),
                            reads=mkeys, writes=[f"P{pi}"])
                    else:
                        P3 = Pm[:].rearrange("p (h c) -> p h c", h=2)[:, :, 0:128]
                        S3 = Sp[:].rearrange("p (h c) -> p h c", h=2)[:, :, 0:128]
                        P.op("act", lambda e, S3=S3, P3=P3: e.activation(P3, S3, AF.Exp, scale=scale),
                             excl=[f"S{sb}"], writes=[f"P{pi}"])
                        P.op("pool", lambda e, P3=P3: e.tensor_tensor(P3, P3, mask2[:, :, 0:128], ALU.mult),
                             reads=mkeys, writes=[f"P{pi}"])
                    for hh in range(2):
                        for (dst0, lhs_c, lhs_p) in ((hh * 128, vS[:, n, :], vS[:, n - 1, :] if n > 0 else None),
                                                     (256 + hh * 128, ones[:], ones[:] if n > 0 else None)):
                            P.op("pe", lambda e, OL=OL, dst0=dst0, lhs_c=lhs_c, Pm=Pm, hh=hh, n=n: e.matmul(
                                OL[:, dst0:dst0 + 128], lhs_c, Pm[:, hh * 256:hh * 256 + 128], start=True, stop=(n == 0)),
                                reads=vkeys + ["ones", f"P{pi}"], writes=[f"OL{sb}"])
                            if n > 0:
                                P.op("pe", lambda e, OL=OL, dst0=dst0, lhs_p=lhs_p, Pm=Pm, hh=hh: e.matmul(
                                    OL[:, dst0:dst0 + 128], lhs_p, Pm[:, hh * 256 + 128:hh * 256 + 256],
                                    start=False, stop=True),
                                    reads=vkeys + ["ones", f"P{pi}"], writes=[f"OL{sb}"])
                    for hh in range(2):
                        h = hp * 2 + hh
                        P.op("dve", lambda e, OL=OL, Lq=Lq, hh=hh, h=h: e.tensor_scalar(
                            Lq[:, hh * 128:(hh + 1) * 128], OL[:, 256 + hh * 128:256 + (hh + 1) * 128],
                            esink[0:64, h:h + 1], None, ALU.add),
                            reads=["esink"], excl=[f"OL{sb}"], writes=[f"Lp{sb}_{hh}"])
                    P.op("dve", lambda e, Lq=Lq: e.reciprocal(Lq[:], Lq[:]),
                         reads=[f"Lp{sb}_0", f"Lp{sb}_1"], writes=[f"Lp{sb}_0", f"Lp{sb}_1"])
                    lc = (n % 4) * 128
                    P.op("dve", lambda e, OL=OL, Lq=Lq, og=og, hp=hp, lc=lc: e.tensor_tensor(
                        og[:, hp * 2:hp * 2 + 2, lc:lc + 128], OL[:, 0:256].rearrange("p (h c) -> p h c", h=2),
                        Lq[:].rearrange("p (h c) -> p h c", h=2), ALU.mult),
                        reads=[f"Lp{sb}_0", f"Lp{sb}_1"], excl=[f"OL{sb}"], writes=[f"ostg{g % 2}_{hp}_{n % 4}"])
                if n % 4 == 3:
                    P.dma("sp", io["oT"][u][:, :, g * 512:(g + 1) * 512].rearrange("h p t -> p h t"), og[:],
                          reads=[f"ostg{g % 2}_{hp}_{k}" for hp in range(4) for k in range(4)],
                          writes=[f"oT{u}_{g}"])
        P.emit()


def build_k2_swa(U, S):
    nc = bass.Bass("TRN2", target_bir_lowering=False)
    io = dict(qT=_dram_in(nc, "qT", [U, 8, 64, S], BF16), kT=_dram_in(nc, "kT", [U, 64, S], BF16),
              v=_dram_in(nc, "v", [U, 128, S // 128, 64], BF16), sinks=_dram_in(nc, "sinks", [U, 1, 8], F32),
              oT=_dram_out(nc, "oT", [U, 8, 64, S], BF16))
    P = Prog(nc)
    phase_swa(P, nc, U, S, io)
    P.close()
    return nc


MIXERS = ("moba", "mla", "swa", "diff")
_NC_CACHE = {}


def _get_nc(key, builder):
    if key not in _NC_CACHE:
        _NC_CACHE[key] = builder()
    return _NC_CACHE[key]


def _run(nc, in_maps):
    res = run_bass_kernel_spmd(nc, in_maps, core_ids=list(range(NCORES)))
    return res.results


def _f32(a):
    return np.ascontiguousarray(np.asarray(a, dtype=np.float32))


def _v_units(vfull_b, col0, dv, S):
    a = vfull_b[:, col0:col0 + dv]
    return np.ascontiguousarray(a.reshape(S // 128, 128, dv).transpose(1, 0, 2))


def run_model(inp, B, S):
    D = D_MODEL
    TOK = B * S // NCORES
    CPB = NCORES // B
    NT = TOK // 128
    x = _f32(inp["x"]).reshape(B * S, D)
    xs = [np.ascontiguousarray(x[c * TOK:(c + 1) * TOK]) for c in range(NCORES)]
    pos = []
    for c in range(NCORES):
        p = ((c % CPB) * TOK + np.arange(TOK)).astype(np.float32)
        pos.append(np.ascontiguousarray(p.reshape(NT, 128).T))
    for k1kind in ("moba", "swa", "mla"):
        _get_nc(("k1", k1kind, TOK), lambda: build_k1(k1kind, TOK))
    for mx in ("moba", "mla"):
        _get_nc(("k2", mx, B * 16 // NCORES, S), lambda: build_k2(mx, B * 16 // NCORES, S))
    _get_nc(("k2swa", S), lambda: build_k2_swa(1, S))
    _get_nc(("k2diff", B * 8 // NCORES, S, 3), lambda: build_k2("diff", B * 8 // NCORES, S,
                                                                0.8 - 0.6 * math.exp(-0.3 * 3)))
    _get_nc(("k3", TOK, 128), lambda: build_k3(TOK, 128))
    for layer in range(4):
        mixer = MIXERS[layer % 4]
        j = layer // 4
        if mixer == "moba":
            W = dict(wq=inp["moba_wq"][j], wk=inp["moba_wk"][j], wv=inp["moba_wv"][j])
            G = dict(gq=inp["moba_gq"][j], gk=inp["moba_gk"][j])
        elif mixer == "diff":
            W = dict(wq=inp["diff_wq"][j], wk=inp["diff_wk"][j], wv=inp["diff_wv"][j])
            G = dict(gq=inp["diff_gq"][j], gk=inp["diff_gk"][j])
        elif mixer == "swa":
            W = dict(wq=inp["swa_wq"][j], wk=inp["swa_wk"][j], wv=inp["swa_wv"][j])
            G = dict(gq=inp["swa_gq"][j], gk=inp["swa_gk"][j])
        else:
            W = dict(wq_a=inp["mla_wq_a"][j], wkv_a=inp["mla_wkv_a"][j], wq_b=inp["mla_wq_b"][j],
                     wkv_b=inp["mla_wkv_b"][j])
            G = dict(g_qa=inp["mla_g_qa"][j], g_kva=inp["mla_g_kva"][j], g_qn=inp["mla_g_qn"][j],
                     g_kn=inp["mla_g_kn"][j], g_qr=inp["mla_g_qr"][j], g_kr=inp["mla_g_kr"][j])
        W = {k: _f32(v) for k, v in W.items()}
        G = {k: _f32(v).reshape(1, -1) for k, v in G.items()}
        k1kind = "moba" if mixer == "diff" else mixer
        nc1 = _get_nc(("k1", k1kind, TOK), lambda: build_k1(k1kind, TOK))
        g_attn = _f32(inp["attn_norm"][layer]).reshape(1, D)
        maps = []
        for c in range(NCORES):
            m = dict(x=xs[c], g=g_attn, pos=pos[c])
            m.update(W)
            m.update(G)
            maps.append(m)
        r1 = _run(nc1, maps)

        def cat(name, b):
            return np.concatenate([np.asarray(r1[b * CPB + cc][name]) for cc in range(CPB)], axis=-1)

        def catv(b):
            return np.concatenate([np.asarray(r1[b * CPB + cc]["v"]) for cc in range(CPB)], axis=0)

        if mixer in ("moba", "mla"):
            H = 16
            U = B * H // NCORES
            qn, kn = ("qT", "kT") if mixer == "moba" else ("qnT", "knT")
            qf = [cat(qn, b) for b in range(B)]
            kf = [cat(kn, b) for b in range(B)]
            vf = [catv(b) for b in range(B)]
            if mixer == "mla":
                qrf = [cat("qrT", b) for b in range(B)]
                krf = [cat("krT", b) for b in range(B)]
            maps = []
            for c in range(NCORES):
                units = [divmod(c * U + u, H) for u in range(U)]
                m = dict(qA=np.stack([qf[b][h] for b, h in units]), kA=np.stack([kf[b][h] for b, h in units]),
                         v=np.stack([_v_units(vf[b], h * 128, 128, S) for b, h in units]))
                if mixer == "mla":
                    m["qB"] = np.stack([qrf[b][h] for b, h in units])
                    m["kB"] = np.stack([krf[b] for b, h in units])
                maps.append(m)
            nc2 = _get_nc(("k2", mixer, U, S), lambda: build_k2(mixer, U, S))
            r2 = _run(nc2, maps)
            ofull = np.zeros((B, H, 128, S), NPBF16)
            for c in range(NCORES):
                o = np.asarray(r2[c]["oT"])
                for u in range(U):
                    b, h = divmod(c * U + u, H)
                    ofull[b, h] = o[u]
            ko = 128
            ochunks = ofull
        elif mixer == "swa":
            U = 1
            qf = [cat("qT", b) for b in range(B)]
            kf = [cat("kT", b) for b in range(B)]
            vf = [catv(b) for b in range(B)]
            sinks = _f32(inp["swa_sinks"][j])
            maps = []
            for c in range(NCORES):
                b, kvh = divmod(c, 4)
                maps.append(dict(qT=np.ascontiguousarray(qf[b][kvh * 8:(kvh + 1) * 8][None]),
                                 kT=np.ascontiguousarray(kf[b][kvh][None]),
                                 v=_v_units(vf[b], kvh * 64, 64, S)[None],
                                 sinks=np.ascontiguousarray(sinks[kvh * 8:(kvh + 1) * 8].reshape(1, 1, 8))))
            nc2 = _get_nc(("k2swa", S), lambda: build_k2_swa(1, S))
            r2 = _run(nc2, maps)
            ochunks = np.zeros((B, 32, 64, S), NPBF16)
            for c in range(NCORES):
                b, kvh = divmod(c, 4)
                ochunks[b, kvh * 8:(kvh + 1) * 8] = np.asarray(r2[c]["oT"])[0]
            ochunks = ochunks.reshape(B, 16, 128, S)
            ko = 128
        else:
            H = 8
            U = B * H // NCORES
            lam_init = 0.8 - 0.6 * math.exp(-0.3 * layer)
            qf = [cat("qT", b) for b in range(B)]
            kf = [cat("kT", b) for b in range(B)]
            vf = [catv(b) for b in range(B)]
            gs = _f32(inp["diff_g_sub"][j])
            maps = []
            for c in range(NCORES):
                units = [divmod(c * U + u, H) for u in range(U)]
                m = dict(qA=np.stack([qf[b][2 * h:2 * h + 2] for b, h in units]),
                         kA=np.stack([kf[b][2 * h:2 * h + 2] for b, h in units]),
                         v=np.stack([_v_units(vf[b], h * 256, 256, S) for b, h in units]),
                         gsub=np.ascontiguousarray(gs.reshape(2, 128).T))
                for n in ("lq1", "lk1", "lq2", "lk2"):
                    m[n] = _f32(inp["diff_" + n][j]).reshape(1, 128)
                maps.append(m)
            nc2 = _get_nc(("k2diff", U, S, layer), lambda: build_k2("diff", U, S, lam_init))
            r2 = _run(nc2, maps)
            ochunks = np.zeros((B, 16, 128, S), NPBF16)
            for c in range(NCORES):
                o = np.asarray(r2[c]["oT"])
                for u in range(U):
                    b, h = divmod(c * U + u, H)
                    ochunks[b, 2 * h:2 * h + 2] = o[u].reshape(2, 128, S)
            ko = 128
        wo = _f32(inp[f"{mixer}_wo"][j])
        gf = _f32(inp["ffn_norm"][layer]).reshape(1, D)
        wg = _f32(inp["ffn_w_gate"][layer])
        wu = _f32(inp["ffn_w_up"][layer])
        wd = _f32(inp["ffn_w_down"][layer])
        cw = host_cw(_f32(inp["ffn_conv_w"][layer]), _f32(inp["ffn_conv_b"][layer]))
        nc3 = _get_nc(("k3", TOK, ko), lambda: build_k3(TOK, ko))
        NKo = D // ko
        maps = []
        for c in range(NCORES):
            b, cc = divmod(c, CPB)
            t0 = cc * TOK
            oT = np.ascontiguousarray(ochunks[b][:, :, t0:t0 + TOK])
            if cc == 0:
                xh = np.zeros((128, D), np.float32)
                oTh = np.zeros((NKo, ko, 128), NPBF16)
            else:
                xh = np.ascontiguousarray(xs[c - 1][TOK - 128:])
                oTh = np.ascontiguousarray(ochunks[b][:, :, t0 - 128:t0])
            maps.append(dict(x=xs[c], xh=xh, oT=oT, oTh=oTh, wo=wo, gf=gf, wg=wg, wu=wu, wd=wd, cw=cw))
        r3 = _run(nc3, maps)
        xs = [np.asarray(r3[c]["xo"]) for c in range(NCORES)]
    return np.concatenate(xs, axis=0).reshape(B, S, D).astype(np.float32)


class _Rep:
    def __init__(self, ap):
        self.ap = ap

    def __getitem__(self, i):
        return self.ap


W_SPECS = dict(
    attn_norm=(4, 2048), ffn_norm=(4, 2048),
    moba_wq=(1, 2048, 2048), moba_wk=(1, 2048, 2048), moba_wv=(1, 2048, 2048), moba_gq=(1, 128), moba_gk=(1, 128),
    moba_wo=(1, 2048, 2048),
    mla_wq_a=(1, 2048, 512), mla_g_qa=(1, 512), mla_wq_b=(1, 512, 3072), mla_wkv_a=(1, 2048, 576), mla_g_kva=(1, 512),
    mla_wkv_b=(1, 512, 4096), mla_g_qn=(1, 128), mla_g_kn=(1, 128), mla_g_qr=(1, 64), mla_g_kr=(1, 64),
    mla_wo=(1, 2048, 2048),
    swa_wq=(1, 2048, 2048), swa_wk=(1, 2048, 256), swa_wv=(1, 2048, 256), swa_gq=(1, 64), swa_gk=(1, 64),
    swa_sinks=(1, 32), swa_wo=(1, 2048, 2048),
    diff_wq=(1, 2048, 2048), diff_wk=(1, 2048, 2048), diff_wv=(1, 2048, 2048), diff_gq=(1, 128), diff_gk=(1, 128),
    diff_lq1=(1, 128), diff_lk1=(1, 128), diff_lq2=(1, 128), diff_lk2=(1, 128), diff_wo=(1, 2048, 2048),
    ffn_w_gate=(4, 2048, 5632), ffn_w_up=(4, 2048, 5632), ffn_w_down=(4, 5632, 2048),
)


def build_fused(S):
    nc = bass.Bass("TRN2", target_bir_lowering=False)
    D = D_MODEL
    NG = 4
    GT = S // NG
    NTg = GT // 128
    NKT = S // 128
    x_in = _dram_in(nc, "x", [S, D], F32)
    x_out = _dram_out(nc, "out", [S, D], F32)
    pos = _dram_in(nc, "pos", [NG, 128, NTg], F32)
    cw = _dram_in(nc, "cw", [4, 128, 4 * (D_FF // 128)], F32)
    gsub_in = _dram_in(nc, "gsub", [128, 2], F32)
    Win = {k: _dram_in(nc, k, shp, F32) for k, shp in W_SPECS.items()}
    xbuf = [nc.dram_tensor(f"xbuf{i}", [S, D], F32).ap() for i in range(2)]
    xmid = nc.dram_tensor("xmid", [GT, D], F32).ap()
    hTd = nc.dram_tensor("hTd", [16, 128, GT], BF16).ap()
    hTd3 = nc.dram_tensor("hTd3", [NTg + 1, 128, 16 * 128], BF16).ap()
    qs = nc.dram_tensor("qs", [16 * 128, S], BF16).ap()
    ks = nc.dram_tensor("ks", [16 * 128, S], BF16).ap()
    qrs = nc.dram_tensor("qrs", [16, 64, S], BF16).ap()
    krs = nc.dram_tensor("krs", [64, S], BF16).ap()
    vs = nc.dram_tensor("vs", [128, NKT * 2048], BF16).ap()
    os_ = nc.dram_tensor("os", [16 * 128, S], BF16).ap()
    cqT = nc.dram_tensor("cqT", [4, 128, GT], BF16).ap()
    ckvT = nc.dram_tensor("ckvT", [4, 128, GT], BF16).ap()
    zx = nc.dram_tensor("zx", [128, D], F32).ap()
    zo = nc.dram_tensor("zo", [16, 128, 128], BF16).ap()
    P = Prog(nc)
    with contextlib.ExitStack() as es:
        T = make_T(es, nc)
        zt = T("zt", [128, D], F32)
        ztb = T("ztb", [128, 16 * 128], BF16)
        P.op("pool", lambda e: e.memset(zt[:], 0.0), writes=["zt"])
        P.op("pool", lambda e: e.memset(ztb[:], 0.0), writes=["ztb"])
        P.dma("sp", zx[:, :], zt[:], reads=["zt"], writes=["zx"])
        P.dma("sp", zo.rearrange("c p t -> p c t"), ztb[:].rearrange("p (c t) -> p c t", c=16), reads=["ztb"], writes=["zo"])
        P.emit()
    q16 = qs.rearrange("(h p) s -> h p s", p=128)
    k16 = ks.rearrange("(h p) s -> h p s", p=128)
    o16 = os_.rearrange("(h p) s -> h p s", p=128)
    x_cur = x_in
    for layer in range(4):
        mixer = MIXERS[layer]
        x_nxt = x_out if layer == 3 else xbuf[layer % 2]
        pre = mixer + "_"
        Gd = {}
        for g in range(NG):
            gs_ = slice(g * GT, (g + 1) * GT)
            xg = x_cur[gs_, :]
            phase_norm(P, nc, GT, xg, Win["attn_norm"][layer:layer + 1, :], hTd)
            tag = f"L{layer}g{g}"
            if mixer in ("moba", "diff"):
                Gd = dict(gq=Win[pre + "gq"], gk=Win[pre + "gk"])
                for wname, gname, dst in (("wq", "gq", q16), ("wk", "gk", k16)):
                    blocks = []
                    for b in range(4):
                        segs = [dict(off=j * 128, w=128, gain=gname, rot=32, kind="T", dst=[dst[b * 4 + j][:, gs_]])
                                for j in range(4)]
                        blocks.append((b * 512, 512, segs))
                    phase_proj(P, nc, GT, hTd, 16, 128, Win[pre + wname][0], blocks, Gd, pos[g], tag + wname)
                dvh = 128 if mixer == "moba" else 256
                vv = vs.rearrange("p (h t d) -> h p t d", t=NKT, d=dvh)
                blocks = []
                for b in range(4):
                    segs = []
                    for jj in range(512 // dvh):
                        h = b * (512 // dvh) + jj
                        segs.append(dict(off=jj * dvh, w=dvh, gain=None, rot=0, kind="M",
                                         dst=(lambda t, h=h, g=g: vv[h][:, g * NTg + t, :])))
                    blocks.append((b * 512, 512, segs))
                phase_proj(P, nc, GT, hTd, 16, 128, Win[pre + "wv"][0], blocks, Gd, pos[g], tag + "wv")
            elif mixer == "swa":
                Gd = dict(gq=Win["swa_gq"], gk=Win["swa_gk"])
                q32 = qs.rearrange("(h p) s -> h p s", p=64)
                k4 = ks[0:256, :].rearrange("(h p) s -> h p s", p=64)
                blocks = []
                for b in range(8):
                    segs = [dict(off=j * 64, w=64, gain="gq", rot=16, kind="T", dst=[q32[b * 4 + j][:, gs_]]) for j in range(4)]
                    blocks.append((b * 256, 256, segs))
                phase_proj(P, nc, GT, hTd, 16, 128, Win["swa_wq"][0], blocks, Gd, pos[g], tag + "wq")
                segs = [dict(off=j * 64, w=64, gain="gk", rot=16, kind="T", dst=[k4[j][:, gs_]]) for j in range(4)]
                phase_proj(P, nc, GT, hTd, 16, 128, Win["swa_wk"][0], [(0, 256, segs)], Gd, pos[g], tag + "wk")
                vv = vs[:, 0:NKT * 256].rearrange("p (h t d) -> h p t d", t=NKT, d=64)
                segs = [dict(off=j * 64, w=64, gain=None, rot=0, kind="M",
                             dst=(lambda t, j=j, g=g: vv[j][:, g * NTg + t, :])) for j in range(4)]
                phase_proj(P, nc, GT, hTd, 16, 128, Win["swa_wv"][0], [(0, 256, segs)], Gd, pos[g], tag + "wv")
            else:
                Gd = dict(g_qa=Win["mla_g_qa"], g_kva=Win["mla_g_kva"], g_qn=Win["mla_g_qn"], g_kn=Win["mla_g_kn"],
                          g_qr=Win["mla_g_qr"], g_kr=Win["mla_g_kr"])
                phase_proj(P, nc, GT, hTd, 16, 128, Win["mla_wq_a"][0],
                           [(0, 512, [dict(off=0, w=512, gain="g_qa", rot=0, kind="T", dst=[cqT[c] for c in range(4)])])],
                           Gd, pos[g], tag + "wqa")
                phase_proj(P, nc, GT, hTd, 16, 128, Win["mla_wkv_a"][0],
                           [(0, 512, [dict(off=0, w=512, gain="g_kva", rot=0, kind="T", dst=[ckvT[c] for c in range(4)])]),
                            (512, 64, [dict(off=0, w=64, gain="g_kr", rot=64, kind="T", dst=[krs[:, gs_]])])],
                           Gd, pos[g], tag + "wkva")
                blocks = []
                for b in range(8):
                    segs = []
                    for jj in range(2):
                        h = b * 2 + jj
                        segs.append(dict(off=jj * 192, w=128, gain="g_qn", rot=0, kind="T", dst=[q16[h][:, gs_]]))
                        segs.append(dict(off=jj * 192 + 128, w=64, gain="g_qr", rot=64, kind="T", dst=[qrs[h][:, gs_]]))
                    blocks.append((b * 384, 384, segs))
                phase_proj(P, nc, GT, cqT, 4, 128, Win["mla_wq_b"][0], blocks, Gd, pos[g], tag + "wqb")
                vv = vs.rearrange("p (h t d) -> h p t d", t=NKT, d=128)
                blocks = []
                for b in range(8):
                    segs = []
                    for jj in range(2):
                        h = b * 2 + jj
                        segs.append(dict(off=jj * 256, w=128, gain="g_kn", rot=0, kind="T", dst=[k16[h][:, gs_]]))
                        segs.append(dict(off=jj * 256 + 128, w=128, gain=None, rot=0, kind="M",
                                         dst=(lambda t, h=h, g=g: vv[h][:, g * NTg + t, :])))
                    blocks.append((b * 512, 512, segs))
                phase_proj(P, nc, GT, ckvT, 4, 128, Win["mla_wkv_b"][0], blocks, Gd, pos[g], tag + "wkvb")
        if mixer == "moba":
            io = dict(qA=q16, kA=k16, v=vs.rearrange("p (h t d) -> h p t d", t=NKT, d=128), oT=o16)
            phase_attn(P, nc, "moba", 16, S, io)
        elif mixer == "mla":
            io = dict(qA=q16, kA=k16, qB=qrs, kB=_Rep(krs), v=vs.rearrange("p (h t d) -> h p t d", t=NKT, d=128), oT=o16)
            phase_attn(P, nc, "mla", 16, S, io)
        elif mixer == "swa":
            io = dict(qT=qs.rearrange("(u h p) s -> u h p s", h=8, p=64), kT=ks[0:256, :].rearrange("(h p) s -> h p s", p=64),
                      v=vs[:, 0:NKT * 256].rearrange("p (h t d) -> h p t d", t=NKT, d=64),
                      sinks=Win["swa_sinks"].rearrange("o (u h) -> u o h", h=8),
                      oT=os_.rearrange("(u h p) s -> u h p s", h=8, p=64))
            phase_swa(P, nc, 4, S, io)
        else:
            lam_init = 0.8 - 0.6 * math.exp(-0.3 * layer)
            io = dict(qA=qs.rearrange("(h c p) s -> h c p s", c=2, p=128), kA=ks.rearrange("(h c p) s -> h c p s", c=2, p=128),
                      v=vs.rearrange("p (h t d) -> h p t d", t=NKT, d=256), oT=os_.rearrange("(h e) s -> h e s", e=256),
                      lq1=Win["diff_lq1"], lk1=Win["diff_lk1"], lq2=Win["diff_lq2"], lk2=Win["diff_lk2"], gsub=gsub_in)
            phase_attn(P, nc, "diff", 8, S, io, lam_init)
        for g in range(NG):
            gs_ = slice(g * GT, (g + 1) * GT)
            if g == 0:
                xh, oTh = zx, zo
            else:
                xh = x_cur[g * GT - 128:g * GT, :]
                oTh = o16[:, :, g * GT - 128:g * GT]
            phase_outproj_norm(P, nc, GT, 128, x_cur[gs_, :], xh, o16[:, :, gs_], oTh, Win[pre + "wo"][0],
                               Win["ffn_norm"][layer:layer + 1, :], xmid, hTd3)
            phase_ffn(P, nc, GT, xmid, hTd3, Win["ffn_w_gate"][layer], Win["ffn_w_up"][layer], Win["ffn_w_down"][layer],
                      cw[layer], x_nxt[gs_, :])
        x_cur = x_nxt
    P.close()
    return nc


def run_fused(inp, B, S):
    D = D_MODEL
    NG = 4
    GT = S // NG
    NTg = GT // 128
    nc = _get_nc(("fused", S), lambda: build_fused(S))
    x = _f32(inp["x"])
    posv = np.arange(S, dtype=np.float32).reshape(NG, NTg, 128).transpose(0, 2, 1)
    base = dict(pos=np.ascontiguousarray(posv),
                cw=np.stack([host_cw(_f32(inp["ffn_conv_w"][l]), _f32(inp["ffn_conv_b"][l])) for l in range(4)]),
                gsub=np.ascontiguousarray(_f32(inp["diff_g_sub"][0]).reshape(2, 128).T))
    for k, shp in W_SPECS.items():
        base[k] = _f32(inp[k]).reshape(shp)
    maps = []
    for b in range(B):
        m = dict(base)
        m["x"] = np.ascontiguousarray(x[b])
        maps.append(m)
    res = run_bass_kernel_spmd(nc, maps, core_ids=list(range(B)))
    return np.stack([np.asarray(res.results[b]["out"]) for b in range(B)]).astype(np.float32)


def kernel(**inputs):
    return run_fused(inputs, 2, 8192)
```

```python
import contextlib
import os
import math
import numpy as np
import ml_dtypes
import concourse.bass as bass
import concourse.mybir as mybir
from concourse.bass_utils import run_bass_kernel_spmd

F32 = mybir.dt.float32
BF16 = mybir.dt.bfloat16
ALU = mybir.AluOpType
AF = mybir.ActivationFunctionType
AX = mybir.AxisListType
NPBF16 = ml_dtypes.bfloat16

D_MODEL = 2048
D_FF = 5632
EPS = 1e-6
ROPE_THETA = 500000.0
NCORES = 8


class Prog:
    CENG = ("pe", "act", "dve", "pool")

    def __init__(self, nc, ndma=6):
        self.nc = nc
        self.es = contextlib.ExitStack()
        self.ops = {e: [] for e in ("pe", "act", "dve", "pool", "sp")}
        self.csem = {e: self.es.enter_context(nc.semaphore("c_" + e)) for e in self.CENG}
        self.ccnt = {e: 0 for e in self.CENG}
        self.dsem = {q: [self.es.enter_context(nc.semaphore(f"d_{q}{i}")) for i in range(ndma)]
                     for q in ("sp", "pool")}
        self.dval = {q: [0] * ndma for q in ("sp", "pool")}
        self.dnext = {q: 0 for q in ("sp", "pool")}
        self.seen = {e: {} for e in self.ops}
        self.lastw = {}
        self.readers = {}
        self.semobj = {}

    def close(self):
        self.es.close()

    def _key(self, s):
        k = id(s)
        self.semobj[k] = s
        return k

    def op(self, eng, fn, reads=(), writes=(), dma=False, excl=()):
        writes = list(writes) + list(excl)
        deps = {}

        def add(ev):
            if ev is None:
                return
            k, v = ev
            if deps.get(k, 0) < v:
                deps[k] = v

        for b in reads:
            add(self.lastw.get(b))
        for b in writes:
            add(self.lastw.get(b))
            for ev in self.readers.get(b, {}).items():
                add(ev)
        if dma:
            q = eng
            i = self.dnext[q]
            self.dnext[q] = (i + 1) % len(self.dsem[q])
            sem = self.dsem[q][i]
            k = self._key(sem)
            if self.dval[q][i] > 0:
                add((k, self.dval[q][i]))
            self.dval[q][i] += 16
            ev = (k, self.dval[q][i])
            inc = 16
        else:
            sem = self.csem[eng]
            k = self._key(sem)
            self.ccnt[eng] += 1
            ev = (k, self.ccnt[eng])
            inc = 1
            if eng == "pe":
                deps.pop(k, None)
        seen = self.seen[eng]
        waits = []
        for dk, dv in deps.items():
            if seen.get(dk, 0) >= dv:
                continue
            seen[dk] = dv
            waits.append((self.semobj[dk], dv))
        for b in writes:
            self.lastw[b] = ev
            self.readers[b] = {}
        for b in reads:
            r = self.readers.setdefault(b, {})
            if r.get(ev[0], 0) < ev[1]:
                r[ev[0]] = ev[1]
        self.ops[eng].append((fn, waits, sem, inc))
        return ev

    def dma(self, q, out, in_, reads=(), writes=()):
        return self.op(q, lambda e: e.dma_start(out=out, in_=in_), reads, writes, dma=True)

    def emit(self):
        finals = []
        for e in self.CENG:
            if self.ccnt[e] > 0:
                finals.append((self._key(self.csem[e]), self.csem[e], self.ccnt[e]))
        for q in ("sp", "pool"):
            for s, v in zip(self.dsem[q], self.dval[q]):
                if v > 0:
                    finals.append((self._key(s), s, v))
        ops = self.ops
        seen = self.seen

        def mk(ename):
            def body(engine):
                for fn, waits, sem, inc in ops[ename]:
                    for s, v in waits:
                        engine.wait_ge(s, v)
                    fn(engine).then_inc(sem, inc)
                for k, s, v in finals:
                    if seen[ename].get(k, 0) < v:
                        engine.wait_ge(s, v)
                        seen[ename][k] = v
            return body

        with self.nc.Block() as block:
            block.tensor(mk("pe"))
            block.scalar(mk("act"))
            block.vector(mk("dve"))
            block.gpsimd(mk("pool"))
            block.sync(mk("sp"))
        self.ops = {e: [] for e in self.ops}


def build_identity(P, nc, ident, es):
    it_p = es.enter_context(nc.sbuf_tensor(uname("it_p"), [128, 128], F32))
    it_j = es.enter_context(nc.sbuf_tensor(uname("it_j"), [128, 128], F32))
    P.op("pool", lambda e: e.iota(it_p[:], [[0, 128]], base=0, channel_multiplier=1,
                                  allow_small_or_imprecise_dtypes=True), writes=["it_p"])
    P.op("pool", lambda e: e.iota(it_j[:], [[1, 128]], base=0, channel_multiplier=0,
                                  allow_small_or_imprecise_dtypes=True), writes=["it_j"])
    P.op("dve", lambda e: e.tensor_tensor(ident[:], it_p[:], it_j[:], ALU.is_equal),
         reads=["it_p", "it_j"], writes=["ident"])
    return it_p, it_j


def emit_rstd(P, out, ss, okey, skey, inv_n, epsb):
    P.op("act", lambda e: e.activation(out, ss, AF.Sqrt, bias=epsb[:, 0:1], scale=inv_n),
         reads=[skey, "consts"], writes=[okey])
    P.op("dve", lambda e: e.reciprocal(out, out), reads=[okey], writes=[okey])


def emit_norm_transpose(P, nc, xt, xkeys, g_rep, ident, scr, hT_out, hT_key, pst, pst_key):
    D = D_MODEL
    sq, ss, rstd, hn = scr["sq"], scr["ss"], scr["rstd"], scr["hn"]
    P.op("act", lambda e: e.activation(sq[:], xt, AF.Square, accum_out=ss[:]),
         reads=list(xkeys), writes=["sq", "ss"])
    emit_rstd(P, rstd[:], ss[:], "rstd", "ss", 1.0 / D, scr["epsb"])
    P.op("dve", lambda e: e.scalar_tensor_tensor(hn[:], xt, rstd[:, 0:1], g_rep, ALU.mult, ALU.mult),
         reads=list(xkeys) + ["rstd", "consts"], writes=["hn"])
    for half in range(2):
        pt = pst[half]
        for j in range(8):
            c = half * 8 + j
            P.op("pe", lambda e, c=c, j=j, pt=pt: e.transpose(pt[:, j * 128:(j + 1) * 128],
                                                              hn[:, c * 128:(c + 1) * 128], ident[:]),
                 reads=["hn", "ident"], writes=[pst_key[half]])
        eng = "act" if half == 0 else "dve"
        if eng == "act":
            P.op("act", lambda e, pt=pt, half=half: e.activation(
                hT_out[:, half * 8:(half + 1) * 8, :], pt[:].rearrange("p (c t) -> p c t", c=8), AF.Copy),
                 reads=[pst_key[half]], writes=[hT_key])
        else:
            P.op("dve", lambda e, pt=pt, half=half: e.tensor_copy(
                hT_out[:, half * 8:(half + 1) * 8, :], pt[:].rearrange("p (c t) -> p c t", c=8)),
                 reads=[pst_key[half]], writes=[hT_key])


_UID = [0]


def uname(name):
    _UID[0] += 1
    return f"{name}_u{_UID[0]}"


def make_T(es, nc):
    def T(name, shape, dt):
        return es.enter_context(nc.sbuf_tensor(uname(name), shape, dt))
    return T


def psum_T(es, nc, name, shape, dt):
    return es.enter_context(nc.psum_tensor(uname(name), shape, dt))


def phase_outproj_norm(P, nc, TOK, ko, x, xh, oT, oTh, wo, gf, xmid, hTd):
    D = D_MODEL
    NT = TOK // 128
    NKo = D // ko
    with contextlib.ExitStack() as es:
        T = make_T(es, nc)
        wo_sb = T("wo_sb", [ko, NKo, D], BF16)
        oT_sb = T("oT_sb", [ko, NKo, TOK], BF16)
        oTh_sb = T("oTh_sb", [ko, NKo, 128], BF16)
        g_rep = T("g_rep", [128, D], F32)
        ident = T("ident", [128, 128], BF16)
        scr = dict(sq=T("sq", [128, D], F32), ss=T("ss", [128, 1], F32), rstd=T("rstd", [128, 1], F32),
                   hn=T("hn", [128, D], BF16), epsb=T("epsb", [128, 1], F32))
        xt = [T(f"xt{i}", [128, D], F32) for i in range(2)]
        xm = [T(f"xm{i}", [128, D], F32) for i in range(2)]
        hT_sb = [T(f"hT_sb{i}", [128, 16, 128], BF16) for i in range(2)]
        py = [psum_T(es, nc, f"py{i}", [128, 512], F32) for i in range(4)]
        pst = [psum_T(es, nc, f"pst{i}", [128, 1024], BF16) for i in range(2)]
        build_identity(P, nc, ident, es)
        P.op("pool", lambda e: e.memset(scr["epsb"][:], EPS), writes=["consts"])
        P.dma("sp", g_rep[:], gf[0:1, :].partition_broadcast(128), writes=["consts"])
        for c in range(NKo):
            P.dma("pool", wo_sb[:, c, :], wo[c * ko:(c + 1) * ko, :], writes=[f"wo{c}"])
        P.dma("sp", oTh_sb[:], oTh.rearrange("c p t -> p c t"), writes=["oTh"])
        for c in range(NKo):
            P.dma("sp", oT_sb[:, c, :], oT[c], writes=[f"oT{c}"])
        order = [NT] + list(range(NT))
        for n, i in enumerate(order):
            b = n % 2
            halo = (i == NT)
            xsrc = xh[:, :] if halo else x[i * 128:(i + 1) * 128, :]
            P.dma("sp", xt[b][:], xsrc, writes=[f"xt{b}"])
            for nb in range(4):
                for c in range(NKo):
                    if halo:
                        lhsT = oTh_sb[:, c, :]
                        rk = "oTh"
                    else:
                        lhsT = oT_sb[:, c, i * 128:(i + 1) * 128]
                        rk = f"oT{c}"
                    P.op("pe", lambda e, nb=nb, c=c, lhsT=lhsT: e.matmul(
                        py[nb][:], lhsT, wo_sb[:, c, nb * 512:(nb + 1) * 512], start=(c == 0), stop=(c == NKo - 1)),
                        reads=[rk, f"wo{c}"], writes=[f"py{nb}"])
                P.op("dve", lambda e, nb=nb, b=b: e.tensor_tensor(
                    xm[b][:, nb * 512:(nb + 1) * 512], xt[b][:, nb * 512:(nb + 1) * 512], py[nb][:], ALU.add),
                    reads=[f"xt{b}", f"py{nb}"], writes=[f"xm{b}_{nb}"])
            xkeys = [f"xm{b}_{nb}" for nb in range(4)]
            if not halo:
                P.dma("sp", xmid[i * 128:(i + 1) * 128, :], xm[b][:], reads=xkeys, writes=[f"xmid{i}"])
            emit_norm_transpose(P, nc, xm[b][:], xkeys, g_rep[:], ident, scr, hT_sb[b], f"hTsb{b}",
                                pst, ["pst0", "pst1"])
            P.dma("sp", hTd[i], hT_sb[b][:].rearrange("p c t -> p (c t)"), reads=[f"hTsb{b}"], writes=[f"hTd{i}"])
        P.emit()


def phase_ffn(P, nc, TOK, xmid, hTd, wg, wu, wd, cw, xo):
    D = D_MODEL
    NT = TOK // 128
    NSB = TOK // 512
    NFB = D_FF // 512
    NFC = D_FF // 128
    with contextlib.ExitStack() as es:
        T = make_T(es, nc)
        hT_sb = T("f_hT", [128, 16, 512], BF16)
        hTh = T("f_hTh", [128, 16, 128], BF16)
        xacc = T("f_xacc", [128, 4, D], F32)
        wg_sb = [T(f"f_wg{i}", [128, 16, 512], BF16) for i in range(2)]
        wu_sb = [T(f"f_wu{i}", [128, 16, 512], BF16) for i in range(2)]
        wd_sb = [T(f"f_wd{i}", [128, 4, D], BF16) for i in range(2)]
        aT = [T(f"f_aT{i}", [128, 4, 512], BF16) for i in range(2)]
        gs = [T(f"f_gs{i}", [128, 514], F32) for i in range(2)]
        c1 = [T(f"f_c1{i}", [128, 512], F32) for i in range(2)]
        carry = T("f_carry", [128, NFC, 2], F32)
        cw_sb = T("f_cw", [128, 4 * NFC], F32)
        psg = [psum_T(es, nc, f"psg{i}", [128, 512], F32) for i in range(2)]
        psu = [psum_T(es, nc, f"psu{i}", [128, 512], F32) for i in range(2)]
        pso = [psum_T(es, nc, f"pso{i}", [128, 512], F32) for i in range(3)]
        ph = psum_T(es, nc, "ph", [128, 2 * NFC], F32)
        P.dma("sp", cw_sb[:], cw[:, :], writes=["cw"])
        P.dma("sp", hTh[:], hTd[NT].rearrange("p (c t) -> p c t", c=16), writes=["hTh"])
        def load_fw(idx):
            fb_ = idx % NFB
            wb_ = idx % 2
            P.dma("pool", wg_sb[wb_][:], wg[:, fb_ * 512:(fb_ + 1) * 512].rearrange("(c p) f -> p c f", p=128),
                  writes=[f"wg{wb_}"])
            P.dma("pool", wu_sb[wb_][:], wu[:, fb_ * 512:(fb_ + 1) * 512].rearrange("(c p) f -> p c f", p=128),
                  writes=[f"wu{wb_}"])
            P.dma("pool", wd_sb[wb_][:], wd[fb_ * 512:(fb_ + 1) * 512, :].rearrange("(c p) n -> p c n", p=128),
                  writes=[f"wd{wb_}"])

        npo = 0
        for sb in range(NSB):
            for tt in range(4):
                P.dma("sp", hT_sb[:, :, tt * 128:(tt + 1) * 128],
                      hTd[sb * 4 + tt].rearrange("p (c t) -> p c t", c=16), writes=[f"hT{tt}"])
                P.dma("sp", xacc[:, tt, :], xmid[(sb * 4 + tt) * 128:(sb * 4 + tt + 1) * 128, :],
                      writes=[f"xacc{tt}_{nb}" for nb in range(4)])
            hkeys = [f"hT{tt}" for tt in range(4)]
            for fb in range(NFB):
                wb = (sb * NFB + fb) % 2
                if sb == 0 and fb == 0:
                    load_fw(0)
                nxt = sb * NFB + fb + 1
                if nxt < NSB * NFB:
                    load_fw(nxt)
                for fcl in range(4):
                    fc = fb * 4 + fcl
                    pb = fc % 2
                    for kc in range(16):
                        P.op("pe", lambda e, kc=kc, fcl=fcl, wb=wb, pb=pb: e.matmul(
                            psg[pb][:], wg_sb[wb][:, kc, fcl * 128:(fcl + 1) * 128], hT_sb[:, kc, :],
                            start=(kc == 0), stop=(kc == 15)),
                            reads=[f"wg{wb}"] + hkeys, writes=[f"psg{pb}"])
                    if sb == 0:
                        for kc in range(16):
                            P.op("pe", lambda e, kc=kc, fcl=fcl, wb=wb, fc=fc: e.matmul(
                                ph[:, 2 * fc:2 * fc + 2], wg_sb[wb][:, kc, fcl * 128:(fcl + 1) * 128],
                                hTh[:, kc, 126:128], start=(kc == 0), stop=(kc == 15)),
                                reads=[f"wg{wb}", "hTh"], writes=["ph"])
                        P.op("act", lambda e, fc=fc: e.activation(carry[:, fc, :], ph[:, 2 * fc:2 * fc + 2], AF.Copy),
                             excl=["ph"], writes=[f"carry{fc}"])
                    for kc in range(16):
                        P.op("pe", lambda e, kc=kc, fcl=fcl, wb=wb, pb=pb: e.matmul(
                            psu[pb][:], wu_sb[wb][:, kc, fcl * 128:(fcl + 1) * 128], hT_sb[:, kc, :],
                            start=(kc == 0), stop=(kc == 15)),
                            reads=[f"wu{wb}"] + hkeys, writes=[f"psu{pb}"])
                    g_ = gs[pb]
                    c_ = c1[pb]
                    P.op("pool", lambda e, g_=g_, fc=fc: e.tensor_copy(g_[:, 0:2], carry[:, fc, :]),
                         reads=[f"carry{fc}"], writes=[f"gsh{pb}"])
                    P.op("act", lambda e, g_=g_, pb=pb: e.activation(g_[:, 2:514], psg[pb][:], AF.Copy),
                         reads=[f"psg{pb}"], writes=[f"gs{pb}"])
                    P.op("pool", lambda e, g_=g_, fc=fc: e.tensor_copy(carry[:, fc, :], g_[:, 512:514]),
                         reads=[f"gs{pb}", f"gsh{pb}"], writes=[f"carry{fc}"])
                    P.op("act", lambda e, c_=c_, pb=pb, fc=fc: e.activation(
                        c_[:], psg[pb][:], AF.Identity, bias=cw_sb[:, 3 * NFC + fc:3 * NFC + fc + 1],
                        scale=cw_sb[:, 2 * NFC + fc:2 * NFC + fc + 1]),
                        reads=[f"psg{pb}", "cw"], writes=[f"c1{pb}"])
                    P.op("dve", lambda e, c_=c_, g_=g_, fc=fc: e.scalar_tensor_tensor(
                        c_[:], g_[:, 1:513], cw_sb[:, NFC + fc:NFC + fc + 1], c_[:], ALU.mult, ALU.add),
                        reads=[f"gs{pb}", f"gsh{pb}", "cw", f"c1{pb}"], writes=[f"c1{pb}"])
                    P.op("dve", lambda e, c_=c_, g_=g_, fc=fc: e.scalar_tensor_tensor(
                        c_[:], g_[:, 0:512], cw_sb[:, fc:fc + 1], c_[:], ALU.mult, ALU.add),
                        reads=[f"gs{pb}", f"gsh{pb}", "cw", f"c1{pb}"], writes=[f"c1{pb}"])
                    P.op("act", lambda e, c_=c_: e.activation(c_[:], c_[:], AF.Silu),
                         reads=[f"c1{pb}"], writes=[f"c1{pb}"])
                    P.op("dve", lambda e, c_=c_, wb=wb, fcl=fcl, pb=pb: e.tensor_tensor(
                        aT[wb][:, fcl, :], c_[:], psu[pb][:], ALU.mult),
                        reads=[f"c1{pb}", f"psu{pb}"], writes=[f"aT{wb}_{fcl}"])
                for tt in range(4):
                    for nb in range(4):
                        pi = npo % 3
                        npo += 1
                        for fcl in range(4):
                            P.op("pe", lambda e, pi=pi, wb=wb, fcl=fcl, tt=tt, nb=nb: e.matmul(
                                pso[pi][:], aT[wb][:, fcl, tt * 128:(tt + 1) * 128],
                                wd_sb[wb][:, fcl, nb * 512:(nb + 1) * 512], start=(fcl == 0), stop=(fcl == 3)),
                                reads=[f"aT{wb}_{fcl}", f"wd{wb}"], writes=[f"pso{pi}"])
                        P.op("dve", lambda e, pi=pi, tt=tt, nb=nb: e.tensor_tensor(
                            xacc[:, tt, nb * 512:(nb + 1) * 512], xacc[:, tt, nb * 512:(nb + 1) * 512],
                            pso[pi][:], ALU.add),
                            reads=[f"pso{pi}", f"xacc{tt}_{nb}"], writes=[f"xacc{tt}_{nb}"])
            for tt in range(4):
                P.dma("sp", xo[(sb * 4 + tt) * 128:(sb * 4 + tt + 1) * 128, :], xacc[:, tt, :],
                      reads=[f"xacc{tt}_{nb}" for nb in range(4)], writes=[f"xo{sb}_{tt}"])
        P.emit()


def build_k3(TOK, ko):
    nc = bass.Bass("TRN2", target_bir_lowering=False)
    D = D_MODEL
    NT = TOK // 128
    NKo = D // ko
    x = nc.dram_tensor("x", [TOK, D], F32, kind="ExternalInput").ap()
    xh = nc.dram_tensor("xh", [128, D], F32, kind="ExternalInput").ap()
    oT = nc.dram_tensor("oT", [NKo, ko, TOK], BF16, kind="ExternalInput").ap()
    oTh = nc.dram_tensor("oTh", [NKo, ko, 128], BF16, kind="ExternalInput").ap()
    wo = nc.dram_tensor("wo", [D, D], F32, kind="ExternalInput").ap()
    gf = nc.dram_tensor("gf", [1, D], F32, kind="ExternalInput").ap()
    wg = nc.dram_tensor("wg", [D, D_FF], F32, kind="ExternalInput").ap()
    wu = nc.dram_tensor("wu", [D, D_FF], F32, kind="ExternalInput").ap()
    wd = nc.dram_tensor("wd", [D_FF, D], F32, kind="ExternalInput").ap()
    cw = nc.dram_tensor("cw", [128, 4 * (D_FF // 128)], F32, kind="ExternalInput").ap()
    xo = nc.dram_tensor("xo", [TOK, D], F32, kind="ExternalOutput").ap()
    xmid = nc.dram_tensor("xmid", [TOK, D], F32).ap()
    hTd = nc.dram_tensor("hTd", [NT + 1, 128, 16 * 128], BF16).ap()
    P = Prog(nc)
    phase_outproj_norm(P, nc, TOK, ko, x, xh, oT, oTh, wo, gf, xmid, hTd)
    phase_ffn(P, nc, TOK, xmid, hTd, wg, wu, wd, cw, xo)
    P.close()
    return nc


def host_cw(conv_w, conv_b):
    NFC = D_FF // 128
    a = np.concatenate([conv_w, conv_b[None, :]], axis=0)
    return np.ascontiguousarray(a.reshape(4, NFC, 128).transpose(2, 0, 1).reshape(128, 4 * NFC))


def phase_norm(P, nc, TOK, x, g, hTd):
    D = D_MODEL
    NT = TOK // 128
    with contextlib.ExitStack() as es:
        T = make_T(es, nc)
        g_rep = T("n_g_rep", [128, D], F32)
        ident = T("n_ident", [128, 128], BF16)
        scr = dict(sq=T("n_sq", [128, D], F32), ss=T("n_ss", [128, 1], F32), rstd=T("n_rstd", [128, 1], F32),
                   hn=T("n_hn", [128, D], BF16), epsb=T("n_epsb", [128, 1], F32))
        xt = [T(f"n_xt{i}", [128, D], F32) for i in range(2)]
        hT_sb = [T(f"n_hT{i}", [128, 16, 128], BF16) for i in range(2)]
        pst = [psum_T(es, nc, f"n_pst{i}", [128, 1024], BF16) for i in range(2)]
        build_identity(P, nc, ident, es)
        P.op("pool", lambda e: e.memset(scr["epsb"][:], EPS), writes=["consts"])
        P.dma("sp", g_rep[:], g[0:1, :].partition_broadcast(128), writes=["consts"])
        for i in range(NT):
            b = i % 2
            P.dma("sp", xt[b][:], x[i * 128:(i + 1) * 128, :], writes=[f"xt{b}"])
            emit_norm_transpose(P, nc, xt[b][:], [f"xt{b}"], g_rep[:], ident, scr, hT_sb[b], f"hTsb{b}",
                                pst, ["pst0", "pst1"])
            P.dma("sp", hTd[:, :, i * 128:(i + 1) * 128].rearrange("c p t -> p c t"), hT_sb[b][:],
                  reads=[f"hTsb{b}"], writes=[f"hTd{i}"])
        P.emit()


def phase_proj(P, nc, TOK, actT, nk, Kp, W, blocks, gains, pos, tag):
    NT = TOK // 128
    rots = sorted({s["rot"] for b_ in blocks for s in b_[2] if s["rot"]})
    gnames = sorted({s["gain"] for b_ in blocks for s in b_[2] if s["gain"]})
    with contextlib.ExitStack() as es:
        T = make_T(es, nc)
        act = T(tag + "act", [Kp, nk, TOK], BF16)
        wsb = [T(f"{tag}w{i}", [Kp, nk, 512], BF16) for i in range(2)]
        ident = T(tag + "ident", [128, 128], BF16)
        epsb = T(tag + "epsb", [128, 1], F32)
        negpi = T(tag + "negpi", [128, 1], F32)
        pos_sb = T(tag + "pos", [128, NT], F32)
        grep = {}
        for gname in gnames:
            wdt = gains[gname].shape[1]
            grep[gname] = T(f"{tag}g_{gname}", [128, wdt], F32)
        tabs = {}
        for R in rots:
            half = R // 2
            tabs[R] = dict(cos=T(f"{tag}cos{R}", [128, NT, half], F32), sin=T(f"{tag}sin{R}", [128, NT, half], F32),
                           inv=T(f"{tag}inv{R}", [128, half], F32), ang=T(f"{tag}ang{R}", [128, NT, half], F32),
                           ki=T(f"{tag}ki{R}", [128, NT, half], mybir.dt.int32),
                           kf=T(f"{tag}kf{R}", [128, NT, half], F32))
        sq = T(tag + "sq", [128, 512], F32)
        ss = [T(f"{tag}ss{i}", [128, 8], F32) for i in range(4)]
        rstd = [T(f"{tag}rstd{i}", [128, 8], F32) for i in range(4)]
        qn = [T(f"{tag}qn{i}", [128, 512], F32) for i in range(4)]
        qb = [T(f"{tag}qb{i}", [128, 512], BF16) for i in range(4)]
        rt = [T(f"{tag}rt{i}", [128, 256], F32) for i in range(4)]
        stg = [T(f"{tag}stg{i}", [128, 4, TOK], BF16) for i in range(2)]
        ps = [psum_T(es, nc, f"{tag}ps{i}", [128, 512], F32) for i in range(4)]
        pT = [psum_T(es, nc, f"{tag}pT{i}", [128, 1024], BF16) for i in range(2)]
        build_identity(P, nc, ident, es)
        P.op("pool", lambda e: e.memset(epsb[:], EPS), writes=["c_eps"])
        P.op("pool", lambda e: e.memset(negpi[:], -math.pi), writes=["c_negpi"])
        P.dma("sp", pos_sb[:], pos[:, :], writes=["pos"])
        for gname in gnames:
            P.dma("sp", grep[gname][:], gains[gname][0:1, :].partition_broadcast(128), writes=["g_" + gname])
        for R in rots:
            half = R // 2
            tb = tabs[R]
            P.op("pool", lambda e, tb=tb, half=half: e.iota(tb["inv"][:], [[1, half]], base=0, channel_multiplier=0,
                                                            allow_small_or_imprecise_dtypes=True),
                 writes=[f"inv{R}"])
            P.op("act", lambda e, tb=tb, half=half: e.activation(tb["inv"][:], tb["inv"][:], AF.Exp,
                                                                 scale=-math.log(ROPE_THETA) / half),
                 reads=[f"inv{R}"], writes=[f"inv{R}"])
            for t in range(NT):
                P.op("dve", lambda e, tb=tb, t=t: e.tensor_scalar(tb["ang"][:, t, :], tb["inv"][:],
                                                                  pos_sb[:, t:t + 1], None, ALU.mult),
                     reads=[f"inv{R}", "pos"], writes=[f"ang{R}_{t}"])
            akeys = [f"ang{R}_{t}" for t in range(NT)]
            for nm, shift in (("sin", 0.0), ("cos", 0.5 * math.pi)):
                dst = tb[nm]
                key = f"{nm}{R}"
                P.op("dve", lambda e, tb=tb, dst=dst, shift=shift: e.tensor_scalar(
                    dst[:], tb["ang"][:], shift, None, ALU.add), reads=akeys, writes=[key])
                P.op("dve", lambda e, tb=tb, dst=dst: e.tensor_scalar(
                    tb["ki"][:], dst[:], 1.0 / (2 * math.pi), None, ALU.mult), reads=[key], writes=[f"ki{R}"])
                P.op("dve", lambda e, tb=tb: e.tensor_copy(tb["kf"][:], tb["ki"][:]),
                     reads=[f"ki{R}"], writes=[f"kf{R}"])
                P.op("dve", lambda e, tb=tb, dst=dst: e.scalar_tensor_tensor(
                    dst[:], tb["kf"][:], -2 * math.pi, dst[:], ALU.mult, ALU.add),
                    reads=[f"kf{R}", key], writes=[key])
                P.op("dve", lambda e, tb=tb, dst=dst: e.tensor_scalar(
                    tb["kf"][:], dst[:], math.pi, -2 * math.pi, ALU.is_gt, ALU.mult),
                    reads=[key], writes=[f"kf{R}"])
                P.op("dve", lambda e, tb=tb, dst=dst: e.tensor_tensor(dst[:], dst[:], tb["kf"][:], ALU.add),
                     reads=[key, f"kf{R}"], writes=[key])
                P.op("act", lambda e, dst=dst: e.activation(dst[:], dst[:], AF.Sin),
                     reads=[key], writes=[key])
        for c in range(nk):
            P.dma("sp", act[:, c, :], actT[c], writes=[f"act{c}"])
        actkeys = [f"act{c}" for c in range(nk)]
        n = 0
        blocks = [b if len(b) == 4 else (b[0], b[1], b[2], W) for b in blocks]

        def load_w(bi_):
            col0_, w_, _, W_ = blocks[bi_]
            P.dma("pool", wsb[bi_ % 2][:, :, 0:w_], W_[:, col0_:col0_ + w_].rearrange("(c p) f -> p c f", p=Kp),
                  writes=[f"w{bi_ % 2}"])

        load_w(0)
        for bi, (col0, w, segs, _W) in enumerate(blocks):
            wb = bi % 2
            if bi + 1 < len(blocks):
                load_w(bi + 1)
            tch = []
            for si, s in enumerate(segs):
                if s["kind"] == "T":
                    for ci, o in enumerate(range(0, s["w"], 128)):
                        tch.append((si, ci, s["off"] + o, min(128, s["w"] - o), len(tch)))
            assert len(tch) <= 4
            sg = stg[bi % 2]
            for t in range(NT):
                pb = n % 4
                tpb = n % 2
                n += 1
                for kc in range(nk):
                    P.op("pe", lambda e, kc=kc, t=t, pb=pb, wb=wb, w=w: e.matmul(
                        ps[pb][:, 0:w], act[:, kc, t * 128:(t + 1) * 128], wsb[wb][:, kc, 0:w],
                        start=(kc == 0), stop=(kc == nk - 1)),
                        reads=actkeys + [f"w{wb}"], writes=[f"ps{pb}"])
                s0 = segs[0]
                uniform = (all(x["kind"] == "T" and x["gain"] == s0["gain"] and x["rot"] == s0["rot"] and x["w"] == s0["w"]
                               and x["off"] == i * s0["w"] for i, x in enumerate(segs))
                           and s0["gain"] is not None and s0["w"] <= 128)
                if uniform:
                    ns, sw = len(segs), s0["w"]
                    Wd = ns * sw
                    gr = grep[s0["gain"]]
                    ps3 = ps[pb][:, 0:Wd].rearrange("p (s w) -> p s w", s=ns)
                    qn3 = qn[pb][:, 0:Wd].rearrange("p (s w) -> p s w", s=ns)
                    qb3 = qb[pb][:, 0:Wd].rearrange("p (s w) -> p s w", s=ns)
                    sq3 = sq[:, 0:Wd].rearrange("p (s w) -> p s w", s=ns)
                    P.op("act", lambda e, pb=pb, Wd=Wd, sw=sw: e.activation(sq[:, 0:Wd], ps[pb][:, 0:Wd], AF.Square,
                                                                            scale=float(sw) ** -0.5),
                         excl=[f"ps{pb}"], writes=["sq"])
                    P.op("dve", lambda e, pb=pb, ns=ns, sq3=sq3: e.reduce_sum(ss[pb][:, 0:ns], sq3, AX.X),
                         reads=["sq"], writes=[f"ss{pb}"])
                    P.op("act", lambda e, pb=pb, ns=ns: e.activation(rstd[pb][:, 0:ns], ss[pb][:, 0:ns], AF.Sqrt,
                                                                     bias=epsb[:, 0:1]),
                         reads=[f"ss{pb}", "c_eps"], writes=[f"rstd{pb}"])
                    P.op("dve", lambda e, pb=pb, ns=ns: e.reciprocal(rstd[pb][:, 0:ns], rstd[pb][:, 0:ns]),
                         reads=[f"rstd{pb}"], writes=[f"rstd{pb}"])
                    P.op("dve", lambda e, pb=pb, ns=ns, sw=sw, ps3=ps3, qn3=qn3: e.tensor_tensor(
                        qn3, ps3, rstd[pb][:, 0:ns].unsqueeze(2).to_broadcast([128, ns, sw]), ALU.mult),
                        reads=[f"rstd{pb}"], excl=[f"ps{pb}"], writes=[f"qn{pb}"])
                    P.op("dve", lambda e, ns=ns, sw=sw, qn3=qn3, gr=gr: e.tensor_tensor(
                        qn3, qn3, gr[:, 0:sw].unsqueeze(1).to_broadcast([128, ns, sw]), ALU.mult),
                        reads=["g_" + s0["gain"], f"qn{pb}"], writes=[f"qn{pb}"])
                    P.op("act", lambda e, pb=pb, Wd=Wd: e.activation(qb[pb][:, 0:Wd], qn[pb][:, 0:Wd], AF.Copy),
                         reads=[f"qn{pb}"], writes=[f"qb{pb}"])
                    if s0["rot"]:
                        R = s0["rot"]
                        half = R // 2
                        tb = tabs[R]
                        x1 = qn3[:, :, 0:half]
                        x2 = qn3[:, :, half:R]
                        cs = tb["cos"][:, t, :].unsqueeze(1).to_broadcast([128, ns, half])
                        sn = tb["sin"][:, t, :].unsqueeze(1).to_broadcast([128, ns, half])
                        r = [rt[k][:, 0:ns * half].rearrange("p (s w) -> p s w", s=ns) for k in range(4)]
                        rk = [f"rt{k}" for k in range(4)]
                        tk = [f"cos{R}", f"sin{R}", f"qn{pb}"]
                        for k, (xa, tbv) in enumerate(((x1, cs), (x2, sn), (x2, cs), (x1, sn))):
                            P.op("pool" if k % 2 == 0 else "dve",
                                 lambda e, k=k, xa=xa, tbv=tbv, r=r: e.tensor_tensor(r[k], xa, tbv, ALU.mult),
                                 reads=tk, writes=[rk[k]])
                        P.op("dve", lambda e, r=r, qb3=qb3, half=half: e.tensor_tensor(
                            qb3[:, :, 0:half], r[0], r[1], ALU.subtract), reads=[rk[0], rk[1]], writes=[f"qb{pb}"])
                        P.op("dve", lambda e, r=r, qb3=qb3, half=half, R=R: e.tensor_tensor(
                            qb3[:, :, half:R], r[2], r[3], ALU.add), reads=[rk[2], rk[3]], writes=[f"qb{pb}"])
                    for (si, ci, co, wc, slot) in tch:
                        P.op("pe", lambda e, co=co, wc=wc, slot=slot, pb=pb, tpb=tpb: e.transpose(
                            pT[tpb][0:wc, slot * 128:(slot + 1) * 128], qb[pb][:, co:co + wc], ident[:]),
                            reads=[f"qb{pb}", "ident"], writes=[f"pT{tpb}"])
                    src3 = pT[tpb][0:sw, 0:ns * 128].rearrange("p (s t) -> p s t", s=ns)
                    dst3 = sg[0:sw, 0:ns, t * 128:(t + 1) * 128]
                    skeys = [f"stg{bi % 2}_{slot}_{t}" for slot in range(ns)]
                    if tpb == 0:
                        P.op("act", lambda e, src3=src3, dst3=dst3: e.activation(dst3, src3, AF.Copy),
                             excl=[f"pT{tpb}"], writes=skeys)
                    else:
                        P.op("dve", lambda e, src3=src3, dst3=dst3: e.tensor_copy(dst3, src3),
                             excl=[f"pT{tpb}"], writes=skeys)
                    continue
                normed = [(si, s) for si, s in enumerate(segs) if s["gain"]]
                for si, s in normed:
                    o, sw = s["off"], s["w"]
                    P.op("act", lambda e, o=o, sw=sw, si=si, pb=pb: e.activation(
                        sq[:, o:o + sw], ps[pb][:, o:o + sw], AF.Square, scale=float(sw) ** -0.5,
                        accum_out=ss[pb][:, si:si + 1]),
                        excl=[f"ps{pb}"], writes=[f"sq{si}", f"ss{pb}_{si}"])
                if normed:
                    ns = len(segs)
                    P.op("act", lambda e, pb=pb, ns=ns: e.activation(rstd[pb][:, 0:ns], ss[pb][:, 0:ns], AF.Sqrt,
                                                                     bias=epsb[:, 0:1]),
                         reads=[f"ss{pb}_{si}" for si, _ in normed] + ["c_eps"], writes=[f"rstd{pb}"])
                    P.op("dve", lambda e, pb=pb, ns=ns: e.reciprocal(rstd[pb][:, 0:ns], rstd[pb][:, 0:ns]),
                         reads=[f"rstd{pb}"], writes=[f"rstd{pb}"])
                for si, s in enumerate(segs):
                    o, sw = s["off"], s["w"]
                    if s["gain"]:
                        gr = grep[s["gain"]]
                        P.op("dve", lambda e, o=o, sw=sw, si=si, pb=pb, gr=gr: e.scalar_tensor_tensor(
                            qn[pb][:, o:o + sw], ps[pb][:, o:o + sw], rstd[pb][:, si:si + 1], gr[:, 0:sw],
                            ALU.mult, ALU.mult),
                            reads=[f"rstd{pb}", "g_" + s["gain"]], excl=[f"ps{pb}"], writes=[f"qn{pb}_{si}"])
                        P.op("act", lambda e, o=o, sw=sw, pb=pb: e.activation(qb[pb][:, o:o + sw], qn[pb][:, o:o + sw],
                                                                              AF.Copy),
                             reads=[f"qn{pb}_{si}"], writes=[f"qb{pb}_{si}"])
                    else:
                        P.op("act", lambda e, o=o, sw=sw, pb=pb: e.activation(qb[pb][:, o:o + sw], ps[pb][:, o:o + sw],
                                                                              AF.Copy),
                             excl=[f"ps{pb}"], writes=[f"qb{pb}_{si}"])
                    if s["rot"]:
                        R = s["rot"]
                        half = R // 2
                        tb = tabs[R]
                        x1 = qn[pb][:, o:o + half]
                        x2 = qn[pb][:, o + half:o + R]
                        cs = tb["cos"][:, t, :]
                        sn = tb["sin"][:, t, :]
                        r = [rt[k][:, 0:half] for k in range(4)]
                        rk = [f"rt{k}" for k in range(4)]
                        tk = [f"cos{R}", f"sin{R}", f"qn{pb}_{si}"]
                        P.op("pool", lambda e, r=r, x1=x1, cs=cs: e.tensor_tensor(r[0], x1, cs, ALU.mult),
                             reads=tk, writes=[rk[0]])
                        P.op("pool", lambda e, r=r, x2=x2, sn=sn: e.tensor_tensor(r[1], x2, sn, ALU.mult),
                             reads=tk, writes=[rk[1]])
                        P.op("pool", lambda e, r=r, x2=x2, cs=cs: e.tensor_tensor(r[2], x2, cs, ALU.mult),
                             reads=tk, writes=[rk[2]])
                        P.op("pool", lambda e, r=r, x1=x1, sn=sn: e.tensor_tensor(r[3], x1, sn, ALU.mult),
                             reads=tk, writes=[rk[3]])
                        P.op("dve", lambda e, r=r, o=o, half=half, pb=pb: e.tensor_tensor(
                            qb[pb][:, o:o + half], r[0], r[1], ALU.subtract),
                            reads=[rk[0], rk[1]], writes=[f"qb{pb}_{si}"])
                        P.op("dve", lambda e, r=r, o=o, half=half, R=R, pb=pb: e.tensor_tensor(
                            qb[pb][:, o + half:o + R], r[2], r[3], ALU.add),
                            reads=[rk[2], rk[3]], writes=[f"qb{pb}_{si}"])
                    if s["kind"] == "M":
                        mdst = s["dst"](t) if callable(s["dst"]) else s["dst"][t * 128:(t + 1) * 128, :]
                        P.dma("sp", mdst, qb[pb][:, o:o + sw],
                              reads=[f"qb{pb}_{si}"], writes=[f"{tag}M{bi}_{si}_{t}"])
                import os
                if tch and not os.environ.get("SKIP_TR"):
                    for (si, ci, co, wc, slot) in tch:
                        P.op("pe", lambda e, co=co, wc=wc, slot=slot, pb=pb, tpb=tpb: e.transpose(
                            pT[tpb][0:wc, slot * 128:(slot + 1) * 128], qb[pb][:, co:co + wc], ident[:]),
                            reads=[f"qb{pb}_{si}", "ident"], writes=[f"pT{tpb}"])
                    for (si, ci, co, wc, slot) in tch:
                        if tpb == 0:
                            P.op("act", lambda e, wc=wc, slot=slot, pb=pb, tpb=tpb, t=t, sg=sg: e.activation(
                                sg[0:wc, slot, t * 128:(t + 1) * 128], pT[tpb][0:wc, slot * 128:(slot + 1) * 128], AF.Copy),
                                excl=[f"pT{tpb}"], writes=[f"stg{bi % 2}_{slot}_{t}"])
                        else:
                            P.op("dve", lambda e, wc=wc, slot=slot, pb=pb, tpb=tpb, t=t, sg=sg: e.tensor_copy(
                                sg[0:wc, slot, t * 128:(t + 1) * 128], pT[tpb][0:wc, slot * 128:(slot + 1) * 128]),
                                excl=[f"pT{tpb}"], writes=[f"stg{bi % 2}_{slot}_{t}"])
            import os
            for (si, ci, co, wc, slot) in tch:
                if os.environ.get("SKIP_TDMA"):
                    continue
                P.dma("sp", segs[si]["dst"][ci], sg[0:wc, slot, :],
                      reads=[f"stg{bi % 2}_{slot}_{t}" for t in range(NT)], writes=[f"{tag}T{bi}_{slot}"])
        P.emit()


def _dram_in(nc, name, shape, dt):
    return nc.dram_tensor(name, list(shape), dt, kind="ExternalInput").ap()


def _dram_out(nc, name, shape, dt):
    return nc.dram_tensor(name, list(shape), dt, kind="ExternalOutput").ap()


def k1_body(P, nc, mixer, TOK, x, g, pos, Wd, Gd, out, hTd, scratch):
    D = D_MODEL
    phase_norm(P, nc, TOK, x, g, hTd)
    if mixer in ("moba", "diff"):
        blocks = []
        for wname, gname, dst in (("wq", "gq", out["qT"]), ("wk", "gk", out["kT"])):
            for b in range(4):
                segs = [dict(off=j * 128, w=128, gain=gname, rot=32, kind="T", dst=[dst[b * 4 + j]]) for j in range(4)]
                blocks.append((b * 512, 512, segs, Wd[wname]))
        for b in range(4):
            blocks.append((b * 512, 512, [dict(off=0, w=512, gain=None, rot=0, kind="M",
                                              dst=out["v"][:, b * 512:(b + 1) * 512])], Wd["wv"]))
        phase_proj(P, nc, TOK, hTd, 16, 128, None, blocks, Gd, pos, "qkv")
    elif mixer == "swa":
        blocks = []
        for b in range(8):
            segs = [dict(off=j * 64, w=64, gain="gq", rot=16, kind="T", dst=[out["qT"][b * 4 + j]]) for j in range(4)]
            blocks.append((b * 256, 256, segs, Wd["wq"]))
        segs = [dict(off=j * 64, w=64, gain="gk", rot=16, kind="T", dst=[out["kT"][j]]) for j in range(4)]
        blocks.append((0, 256, segs, Wd["wk"]))
        blocks.append((0, 256, [dict(off=0, w=256, gain=None, rot=0, kind="M", dst=out["v"])], Wd["wv"]))
        phase_proj(P, nc, TOK, hTd, 16, 128, None, blocks, Gd, pos, "qkv")
    elif mixer == "mla":
        cqT, ckvT = scratch["cqT"], scratch["ckvT"]
        phase_proj(P, nc, TOK, hTd, 16, 128, None,
                   [(0, 512, [dict(off=0, w=512, gain="g_qa", rot=0, kind="T", dst=[cqT[c] for c in range(4)])], Wd["wq_a"]),
                    (0, 512, [dict(off=0, w=512, gain="g_kva", rot=0, kind="T", dst=[ckvT[c] for c in range(4)])], Wd["wkv_a"]),
                    (512, 64, [dict(off=0, w=64, gain="g_kr", rot=64, kind="T", dst=[out["krT"]])], Wd["wkv_a"])],
                   Gd, pos, "wa")
        blocks = []
        for b in range(8):
            segs = []
            for j in range(2):
                h = b * 2 + j
                segs.append(dict(off=j * 192, w=128, gain="g_qn", rot=0, kind="T", dst=[out["qnT"][h]]))
                segs.append(dict(off=j * 192 + 128, w=64, gain="g_qr", rot=64, kind="T", dst=[out["qrT"][h]]))
            blocks.append((b * 384, 384, segs))
        phase_proj(P, nc, TOK, cqT, 4, 128, Wd["wq_b"], blocks, Gd, pos, "wqb")
        blocks = []
        for b in range(8):
            segs = []
            for j in range(2):
                h = b * 2 + j
                segs.append(dict(off=j * 256, w=128, gain="g_kn", rot=0, kind="T", dst=[out["knT"][h]]))
                segs.append(dict(off=j * 256 + 128, w=128, gain=None, rot=0, kind="M",
                                 dst=out["v"][:, h * 128:(h + 1) * 128]))
            blocks.append((b * 512, 512, segs))
        phase_proj(P, nc, TOK, ckvT, 4, 128, Wd["wkv_b"], blocks, Gd, pos, "wkvb")


K1_SPECS = {
    "moba": dict(W=dict(wq=(2048, 2048), wk=(2048, 2048), wv=(2048, 2048)), G=dict(gq=128, gk=128),
                 out=dict(qT=(16, 128), kT=(16, 128), v=2048)),
    "diff": dict(W=dict(wq=(2048, 2048), wk=(2048, 2048), wv=(2048, 2048)), G=dict(gq=128, gk=128),
                 out=dict(qT=(16, 128), kT=(16, 128), v=2048)),
    "swa": dict(W=dict(wq=(2048, 2048), wk=(2048, 256), wv=(2048, 256)), G=dict(gq=64, gk=64),
                out=dict(qT=(32, 64), kT=(4, 64), v=256)),
    "mla": dict(W=dict(wq_a=(2048, 512), wkv_a=(2048, 576), wq_b=(512, 3072), wkv_b=(512, 4096)),
                G=dict(g_qa=512, g_kva=512, g_qn=128, g_kn=128, g_qr=64, g_kr=64),
                out=dict(qnT=(16, 128), qrT=(16, 64), knT=(16, 128), krT=(64,), v=2048)),
}


def build_k1(mixer, TOK):
    nc = bass.Bass("TRN2", target_bir_lowering=False)
    D = D_MODEL
    NT = TOK // 128
    spec = K1_SPECS[mixer]
    x = _dram_in(nc, "x", [TOK, D], F32)
    g = _dram_in(nc, "g", [1, D], F32)
    pos = _dram_in(nc, "pos", [128, NT], F32)
    Wd = {k: _dram_in(nc, k, shp, F32) for k, shp in spec["W"].items()}
    Gd = {k: _dram_in(nc, k, [1, w], F32) for k, w in spec["G"].items()}
    out = {}
    for k, shp in spec["out"].items():
        if k == "v":
            out[k] = _dram_out(nc, k, [TOK, shp], BF16)
        elif len(shp) == 1:
            out[k] = _dram_out(nc, k, [shp[0], TOK], BF16)
        else:
            out[k] = _dram_out(nc, k, [shp[0], shp[1], TOK], BF16)
    hTd = nc.dram_tensor("hTd", [16, 128, TOK], BF16).ap()
    scratch = {}
    if mixer == "mla":
        scratch["cqT"] = nc.dram_tensor("cqT", [4, 128, TOK], BF16).ap()
        scratch["ckvT"] = nc.dram_tensor("ckvT", [4, 128, TOK], BF16).ap()
    P = Prog(nc)
    k1_body(P, nc, mixer, TOK, x, g, pos, Wd, Gd, out, hTd, scratch)
    P.close()
    return nc


def phase_attn(P, nc, kind, U, S, io, lam_init=0.0):
    NKT = S // 128
    NQB = S // 512
    nmap = 2 if kind == "diff" else 1
    ndv = 2 if kind == "diff" else 1
    dv = 128 * ndv
    scale = {"moba": 128 ** -0.5, "mla": 192 ** -0.5, "diff": 128 ** -0.5}[kind]
    with contextlib.ExitStack() as es:
        T = make_T(es, nc)
        qA = [T(f"a_qA{m}", [128, S], BF16) for m in range(nmap)]
        kA = [T(f"a_kA{m}", [128, S], BF16) for m in range(nmap)]
        vS = T("a_v", [128, NKT, dv], BF16)
        if kind == "mla":
            qB = T("a_qB", [64, S], BF16)
            kB = T("a_kB", [64, S], BF16)
        ones = T("a_ones", [128, 128], BF16)
        mask = T("a_mask", [128, 4, 512], BF16)
        maskf = T("a_maskf", [128, 4, 512], F32)
        NPT = 3
        Pt = [[T(f"a_P{m}_{i}", [128, 512], BF16) for i in range(NPT)] for m in range(nmap)]
        Rr = [T(f"a_R{m}", [128, 512], F32) for m in range(nmap)]
        osb = [T(f"a_osb{i}", [128, 512], BF16) for i in range(2 * ndv)]
        S_ps = [psum_T(es, nc, f"a_S{i}", [128, 512], F32) for i in range(2)]
        if kind == "diff":
            O_ps = [[[psum_T(es, nc, f"a_O{m}{d}", [128, 512], F32) for d in range(2)] for m in range(2)]]
            L_ps = [[psum_T(es, nc, f"a_L{m}", [128, 512], F32) for m in range(2)]]
            nob = 1
        else:
            O_ps = [[[psum_T(es, nc, f"a_O{b}", [128, 512], F32)]] for b in range(2)]
            L_ps = [[psum_T(es, nc, f"a_L{b}", [128, 512], F32)] for b in range(2)]
            nob = 2
        P.op("pool", lambda e: e.memset(ones[:], 1.0), writes=["ones"])
        P.op("pool", lambda e: e.iota(maskf[:], [[-128, 4], [1, 512]], base=0, channel_multiplier=-1,
                                      allow_small_or_imprecise_dtypes=True), writes=["maskf"])
        P.op("dve", lambda e: e.tensor_single_scalar(mask[:], maskf[:], 0.0, ALU.is_ge),
             reads=["maskf"], writes=["mask"])
        if kind == "moba":
            NB = S // 256
            ident = T("a_ident", [128, 128], BF16)
            build_identity(P, nc, ident, es)
            Eall = T("a_Eall", [NB, NB, 128], BF16)
            Ef = T("a_Ef", [NB, NB, 128], F32)
            biasT = T("a_biasT", [NB, S], BF16)
            km = T("a_km", [128, NB], F32)
            kmh = T("a_kmh", [128, NB], BF16)
            kmhf = T("a_kmhf", [128, NB], F32)
            kml = T("a_kml", [128, NB], BF16)
            gpad = T("a_gpad", [128, NB], F32)
            m8 = T("a_m8", [128, 8], F32)
            brow = [T(f"a_brow{i}", [128, NB], BF16) for i in range(2)]
            g_ps = psum_T(es, nc, "a_gps", [128, 512], F32)
            b_ps = psum_T(es, nc, "a_bps", [128, 1024], BF16)
            P.op("pool", lambda e: e.iota(Ef[:], [[-1, NB], [0, 128]], base=0, channel_multiplier=1,
                                          allow_small_or_imprecise_dtypes=True), writes=["Ef"])
            P.op("dve", lambda e: e.tensor_single_scalar(Eall[:], Ef[:], 0.0, ALU.is_equal),
                 reads=["Ef"], writes=["Eall"])
        if kind == "diff":
            lv = {n: T("a_" + n, [128, 128], F32) for n in ("lq1", "lk1", "lq2", "lk2")}
            ltmp = T("a_ltmp", [128, 128], F32)
            lsum = T("a_lsum", [128, 2], F32)
            lam = T("a_lam", [128, 1], F32)
            gsub = T("a_gsub", [128, 2], F32)
            epsb = T("a_epsb", [128, 1], F32)
            od = [T(f"a_od{d}", [128, 512], F32) for d in range(2)]
            t2 = T("a_t2", [128, 512], F32)
            sqb = [T(f"a_sqb{d}", [128, 512], BF16) for d in range(2)]
            rs = T("a_rs", [128, 512], F32)
            P.op("pool", lambda e: e.memset(epsb[:], EPS), writes=["c_eps"])
            for n in lv:
                P.dma("sp", lv[n][:], io[n][0:1, :].partition_broadcast(128), writes=[n])
            P.dma("sp", gsub[:], io["gsub"][:, :], writes=["gsub"])
            for i, (a, b) in enumerate((("lq1", "lk1"), ("lq2", "lk2"))):
                P.op("dve", lambda e, a=a, b=b: e.tensor_tensor(ltmp[:], lv[a][:], lv[b][:], ALU.mult),
                     reads=[a, b], writes=["ltmp"])
                P.op("dve", lambda e, i=i: e.reduce_sum(lsum[:, i:i + 1], ltmp[:], AX.X),
                     reads=["ltmp"], writes=[f"lsum{i}"])
            P.op("act", lambda e: e.activation(lsum[:], lsum[:], AF.Exp), reads=["lsum0", "lsum1"], writes=["lsum"])
            P.op("dve", lambda e: e.tensor_tensor(lam[:], lsum[:, 0:1], lsum[:, 1:2], ALU.subtract),
                 reads=["lsum"], writes=["lam"])
            P.op("dve", lambda e: e.tensor_scalar(lam[:], lam[:], lam_init, None, ALU.add),
                 reads=["lam"], writes=["lam"])
            P.op("dve", lambda e: e.tensor_scalar(gsub[:], gsub[:], 1.0 - lam_init, None, ALU.mult),
                 reads=["gsub"], writes=["gsub"])

        def load_unit(u):
            for m in range(nmap):
                qsrc = io["qA"][u, m] if kind == "diff" else io["qA"][u]
                ksrc = io["kA"][u, m] if kind == "diff" else io["kA"][u]
                for h in range(4):
                    cs = slice(h * (S // 4), (h + 1) * (S // 4))
                    P.dma("sp", qA[m][:, cs], qsrc[:, cs], writes=[f"qA{m}_{h}"])
                    P.dma("sp", kA[m][:, cs], ksrc[:, cs], writes=[f"kA{m}_{h}"])
            for h in range(4):
                ts = slice(h * (NKT // 4), (h + 1) * (NKT // 4))
                P.dma("sp", vS[:, ts, :], io["v"][u][:, ts, :], writes=[f"v_{h}"])
            if kind == "mla":
                P.dma("sp", qB[:], io["qB"][u], writes=["qB"])
                P.dma("sp", kB[:], io["kB"][u], writes=["kB"])
            if kind == "moba":
                qkeys0 = [f"qA0_{h}" for h in range(4)]
                kkeys0 = [f"kA0_{h}" for h in range(4)]
                P.op("pool", lambda e: e.memset(biasT[:], 0.0), writes=["biasT"])
                P.op("pool", lambda e: e.memset(gpad[:], -1e30), writes=["gpad"])
                for i in range(2):
                    P.op("pool", lambda e, i=i: e.memset(brow[i][:], 0.0), writes=[f"brow{i}"])
                P.op("dve", lambda e: e.reduce_sum(km[:], kA[0][:].rearrange("p (b k) -> p b k", k=256), AX.X),
                     reads=kkeys0, writes=["km"])
                P.op("dve", lambda e: e.tensor_scalar(km[:], km[:], 1.0 / 256, None, ALU.mult), reads=["km"], writes=["km"])
                P.op("dve", lambda e: e.tensor_copy(kmh[:], km[:]), reads=["km"], writes=["kmh"])
                P.op("dve", lambda e: e.tensor_copy(kmhf[:], kmh[:]), reads=["kmh"], writes=["kmhf"])
                P.op("dve", lambda e: e.tensor_tensor(kml[:], km[:], kmhf[:], ALU.subtract),
                     reads=["km", "kmhf"], writes=["kml"])
                for qt in range(NKT):
                    ob = qt // 2
                    if ob <= 3:
                        continue
                    bi = qt % 2
                    P.op("pe", lambda e, qt=qt: e.matmul(g_ps[:, 0:NB], qA[0][:, qt * 128:(qt + 1) * 128], kmh[:],
                                                         start=True, stop=False),
                         reads=qkeys0 + ["kmh"], writes=["g_ps"])
                    P.op("pe", lambda e, qt=qt: e.matmul(g_ps[:, 0:NB], qA[0][:, qt * 128:(qt + 1) * 128], kml[:],
                                                         start=False, stop=True),
                         reads=qkeys0 + ["kml"], writes=["g_ps"])
                    P.op("dve", lambda e, ob=ob: e.tensor_copy(gpad[:, 0:ob], g_ps[:, 0:ob]),
                         excl=["g_ps"], writes=["gpad"])
                    P.op("dve", lambda e: e.max(m8[:], gpad[:]), reads=["gpad"], writes=["m8"])
                    P.op("dve", lambda e, ob=ob, bi=bi: e.tensor_scalar(brow[bi][:, 0:ob], gpad[:, 0:ob], m8[:, 2:3],
                                                                        -1000.0, ALU.is_lt, ALU.mult),
                         reads=["gpad", "m8"], writes=[f"brow{bi}"])
                    P.op("pe", lambda e, bi=bi: e.transpose(b_ps[0:NB, 0:128], brow[bi][:], ident[:]),
                         reads=[f"brow{bi}", "ident"], writes=["b_ps"])
                    P.op("act", lambda e, qt=qt: e.activation(biasT[:, qt * 128:(qt + 1) * 128], b_ps[0:NB, 0:128], AF.Copy),
                         excl=["b_ps"], writes=["biasT"])

        steps = []
        npt = 0
        nsb = 0
        nqb_glob = 0
        for u in range(U):
            for qb in range(NQB):
                ob_i = nqb_glob % nob
                nqb_glob += 1
                qi = dict(u=u, qb=qb, ob_i=ob_i, qcols=slice(qb * 512, (qb + 1) * 512), nkt=4 * (qb + 1),
                          Okey=[[f"O{ob_i}_{m}_{d}" for d in range(ndv)] for m in range(nmap)],
                          Lkey=[f"L{ob_i}_{m}" for m in range(nmap)], par=nqb_glob % 2)
                for kt in range(qi["nkt"]):
                    for m in range(nmap):
                        steps.append(dict(qi=qi, kt=kt, m=m, sb=nsb % 2, pi=npt % NPT,
                                          first=(qb == 0 and kt == 0 and m == 0),
                                          last=(kt == qi["nkt"] - 1 and m == nmap - 1)))
                        nsb += 1
                        npt += 1

        def emit_S(st):
            qi, kt, m, sb, pi = st["qi"], st["kt"], st["m"], st["sb"], st["pi"]
            qb, qcols = qi["qb"], qi["qcols"]
            j = kt - 4 * qb
            kcols = slice(kt * 128, (kt + 1) * 128)
            kh = (kt * 128) // (S // 4)
            qh = (qb * 512) // (S // 4)
            Sp = S_ps[sb]
            Pm = Pt[m][pi]
            last_s = (kind == "diff")
            P.op("pe", lambda e: e.matmul(Sp[:], kA[m][:, kcols], qA[m][:, qcols], start=True, stop=last_s),
                 reads=[f"kA{m}_{kh}", f"qA{m}_{qh}"], writes=[f"S{sb}"])
            if kind == "mla":
                P.op("pe", lambda e: e.matmul(Sp[:], kB[:, kcols], qB[:, qcols], start=False, stop=True),
                     reads=["kB", "qB"], writes=[f"S{sb}"])
            if kind == "moba":
                blk = kt // 2
                P.op("pe", lambda e: e.matmul(Sp[:], Eall[:, blk, :], biasT[:, qcols], start=False, stop=True),
                     reads=["Eall", "biasT"], writes=[f"S{sb}"])
            P.op("act", lambda e: e.activation(Pm[:], Sp[:], AF.Exp, scale=scale),
                 excl=[f"S{sb}"], writes=[f"P{m}_{pi}"])
            if j >= 0:
                P.op("pool", lambda e: e.tensor_tensor(Pm[:], Pm[:], mask[:, j, :], ALU.mult),
                     reads=["mask"], writes=[f"P{m}_{pi}"])

        def emit_PV(st):
            qi, kt, m, pi = st["qi"], st["kt"], st["m"], st["pi"]
            ob_i, nkt = qi["ob_i"], qi["nkt"]
            Pm = Pt[m][pi]
            for d in range(ndv):
                P.op("pe", lambda e, d=d: e.matmul(O_ps[ob_i][m][d][:], vS[:, kt, d * 128:(d + 1) * 128], Pm[:],
                                                   start=(kt == 0), stop=(kt == nkt - 1)),
                     reads=[f"v_{kt // (NKT // 4)}", f"P{m}_{pi}"], writes=[qi["Okey"][m][d]])
            P.op("pe", lambda e: e.matmul(L_ps[ob_i][m][:], ones[:], Pm[:], start=(kt == 0), stop=(kt == nkt - 1)),
                 reads=["ones", f"P{m}_{pi}"], writes=[qi["Lkey"][m]])

        def emit_final(qi):
            u, qb, ob_i, qcols, Okey, Lkey = qi["u"], qi["qb"], qi["ob_i"], qi["qcols"], qi["Okey"], qi["Lkey"]
            if kind != "diff":
                ob_sb = osb[ob_i]
                P.op("dve", lambda e: e.reciprocal(Rr[0][:], L_ps[ob_i][0][:]), excl=[Lkey[0]], writes=["R0"])
                P.op("dve", lambda e: e.tensor_tensor(ob_sb[:], O_ps[ob_i][0][0][:], Rr[0][:], ALU.mult),
                     reads=["R0"], excl=[Okey[0][0]], writes=[f"osb{ob_i}"])
                P.dma("sp", io["oT"][u][:, qcols], ob_sb[:], reads=[f"osb{ob_i}"], writes=[f"oT{u}_{qb}"])
            else:
                for m in range(2):
                    P.op("dve", lambda e, m=m: e.reciprocal(Rr[m][:], L_ps[0][m][:]), excl=[Lkey[m]], writes=[f"R{m}"])
                P.op("dve", lambda e: e.tensor_scalar(Rr[1][:], Rr[1][:], lam[:, 0:1], None, ALU.mult),
                     reads=["R1", "lam"], writes=["R1"])
                for d in range(2):
                    P.op("dve", lambda e, d=d: e.tensor_tensor(od[d][:], O_ps[0][0][d][:], Rr[0][:], ALU.mult),
                         reads=["R0"], excl=[Okey[0][d]], writes=[f"od{d}"])
                    P.op("dve", lambda e, d=d: e.tensor_tensor(t2[:], O_ps[0][1][d][:], Rr[1][:], ALU.mult),
                         reads=["R1"], excl=[Okey[1][d]], writes=["t2"])
                    P.op("dve", lambda e, d=d: e.tensor_tensor(od[d][:], od[d][:], t2[:], ALU.subtract),
                         reads=["t2", f"od{d}"], writes=[f"od{d}"])
                    P.op("act", lambda e, d=d: e.activation(sqb[d][:], od[d][:], AF.Square),
                         reads=[f"od{d}"], writes=[f"sqb{d}"])
                ssp = S_ps[0]
                for d in range(2):
                    P.op("pe", lambda e, d=d: e.matmul(ssp[:], ones[:], sqb[d][:], start=(d == 0), stop=(d == 1)),
                         reads=["ones", f"sqb{d}"], writes=["S0"])
                P.op("act", lambda e: e.activation(rs[:], ssp[:], AF.Sqrt, bias=epsb[:, 0:1], scale=1.0 / 256),
                     reads=["c_eps"], excl=["S0"], writes=["rs"])
                P.op("dve", lambda e: e.reciprocal(rs[:], rs[:]), reads=["rs"], writes=["rs"])
                for d in range(2):
                    ob_sb = osb[qi["par"] * 2 + d]
                    okey = f"osb{qi['par'] * 2 + d}"
                    P.op("dve", lambda e, d=d, ob_sb=ob_sb: e.scalar_tensor_tensor(
                        ob_sb[:], od[d][:], gsub[:, d:d + 1], rs[:], ALU.mult, ALU.mult),
                        reads=[f"od{d}", "gsub", "rs"], writes=[okey])
                    P.dma("sp", io["oT"][u][d * 128:(d + 1) * 128, qcols], ob_sb[:], reads=[okey],
                          writes=[f"oT{u}_{qb}_{d}"])

        for i in range(len(steps) + 1):
            if i < len(steps):
                st = steps[i]
                if st["first"]:
                    if i >= 1:
                        emit_PV(steps[i - 1])
                        if steps[i - 1]["last"]:
                            emit_final(steps[i - 1]["qi"])
                    load_unit(st["qi"]["u"])
                    emit_S(st)
                    continue
                emit_S(st)
            if i >= 1:
                pst = steps[i - 1]
                emit_PV(pst)
                if pst["last"]:
                    emit_final(pst["qi"])
        P.emit()


def build_k2(kind, U, S, lam_init=0.0):
    nc = bass.Bass("TRN2", target_bir_lowering=False)
    NKT = S // 128
    io = {}
    if kind == "diff":
        io["qA"] = _dram_in(nc, "qA", [U, 2, 128, S], BF16)
        io["kA"] = _dram_in(nc, "kA", [U, 2, 128, S], BF16)
        io["v"] = _dram_in(nc, "v", [U, 128, NKT, 256], BF16)
        io["oT"] = _dram_out(nc, "oT", [U, 256, S], BF16)
        for n in ("lq1", "lk1", "lq2", "lk2"):
            io[n] = _dram_in(nc, n, [1, 128], F32)
        io["gsub"] = _dram_in(nc, "gsub", [128, 2], F32)
    else:
        io["qA"] = _dram_in(nc, "qA", [U, 128, S], BF16)
        io["kA"] = _dram_in(nc, "kA", [U, 128, S], BF16)
        io["v"] = _dram_in(nc, "v", [U, 128, NKT, 128], BF16)
        io["oT"] = _dram_out(nc, "oT", [U, 128, S], BF16)
        if kind == "mla":
            io["qB"] = _dram_in(nc, "qB", [U, 64, S], BF16)
            io["kB"] = _dram_in(nc, "kB", [U, 64, S], BF16)
    P = Prog(nc)
    phase_attn(P, nc, kind, U, S, io, lam_init)
    P.close()
    return nc


def phase_swa(P, nc, U, S, io):
    NT = S // 128
    scale = 64 ** -0.5
    with contextlib.ExitStack() as es:
        T = make_T(es, nc)
        qS = T("w_q", [64, 8, S], BF16)
        kS = T("w_k", [64, S], BF16)
        vS = T("w_v", [128, NT, 64], BF16)
        ones = T("w_ones", [128, 64], BF16)
        maskf = T("w_maskf", [128, 128], F32)
        mask2 = T("w_mask2", [128, 2, 256], BF16)
        esink = T("w_esink", [128, 8], F32)
        Pt = [T(f"w_P{i}", [128, 512], BF16) for i in range(3)]
        Lp = [T(f"w_Lp{i}", [64, 256], F32) for i in range(2)]
        ostg = [T(f"w_ostg{i}", [64, 8, 512], BF16) for i in range(2)]
        S_ps = [psum_T(es, nc, f"w_S{i}", [128, 512], F32) for i in range(2)]
        OL_ps = [psum_T(es, nc, f"w_OL{i}", [64, 512], F32) for i in range(2)]
        P.op("pool", lambda e: e.memset(ones[:], 1.0), writes=["ones"])
        P.op("pool", lambda e: e.iota(maskf[:], [[1, 128]], base=0, channel_multiplier=-1,
                                      allow_small_or_imprecise_dtypes=True), writes=["maskf"])
        for h in range(2):
            P.op("dve", lambda e, h=h: e.tensor_single_scalar(mask2[:, h, 0:128], maskf[:], 0.0, ALU.is_ge),
                 reads=["maskf"], writes=[f"mask2_{h}c"])
            P.op("dve", lambda e, h=h: e.tensor_single_scalar(mask2[:, h, 128:256], maskf[:], 0.0, ALU.is_lt),
                 reads=["maskf"], writes=[f"mask2_{h}p"])
        mkeys = ["mask2_0c", "mask2_0p", "mask2_1c", "mask2_1p"]
        cnt = 0
        for u in range(U):
            for h in range(8):
                P.dma("sp", qS[:, h, :], io["qT"][u, h], writes=[f"q{h}"])
            P.dma("sp", kS[:], io["kT"][u], writes=["k"])
            for h in range(4):
                ts = slice(h * (NT // 4), (h + 1) * (NT // 4))
                P.dma("sp", vS[:, ts, :], io["v"][u][:, ts, :], writes=[f"v{h}"])
            vkeys = [f"v{h}" for h in range(4)]
            P.dma("sp", esink[:], io["sinks"][u].partition_broadcast(128), writes=["esink"])
            P.op("act", lambda e: e.activation(esink[:], esink[:], AF.Exp), reads=["esink"], writes=["esink"])
            for n in range(NT):
                g = n // 4
                og = ostg[g % 2]
                tcols = slice(n * 128, (n + 1) * 128)
                pcols = slice((n - 1) * 128, n * 128)
                for hp in range(4):
                    sb = cnt % 2
                    pi = cnt % 3
                    cnt += 1
                    Sp, Pm, OL, Lq = S_ps[sb], Pt[pi], OL_ps[sb], Lp[sb]
                    for hh in range(2):
                        h = hp * 2 + hh
                        P.op("pe", lambda e, Sp=Sp, hh=hh, h=h, tcols=tcols: e.matmul(
                            Sp[:, hh * 256:hh * 256 + 128], kS[:, tcols], qS[:, h, tcols], start=True, stop=True),
                            reads=["k", f"q{h}"], writes=[f"S{sb}"])
                        if n > 0:
                            P.op("pe", lambda e, Sp=Sp, hh=hh, h=h, tcols=tcols, pcols=pcols: e.matmul(
                                Sp[:, hh * 256 + 128:hh * 256 + 256], kS[:, pcols], qS[:, h, tcols], start=True, stop=True),
                                reads=["k", f"q{h}"], writes=[f"S{sb}"])
                    if n > 0:
                        P.op("act", lambda e, Sp=Sp, Pm=Pm: e.activation(Pm[:], Sp[:], AF.Exp, scale=scale),
                             excl=[f"S{sb}"], writes=[f"P{pi}"])
                        P.op("pool", lambda e, Pm=Pm: e.tensor_tensor(
                            Pm[:], Pm[:], mask2[:].rearrange("p h c -> p (h c)"), ALU.mult),
                            reads=mkeys, writes=[f"P{pi}"])
                    else:
                        P3 = Pm[:].rearrange("p (h c) -> p h c", h=2)[:, :, 0:128]
                        S3 = Sp[:].rearrange("p (h c) -> p h c", h=2)[:, :, 0:128]
                        P.op("act", lambda e, S3=S3, P3=P3: e.activation(P3, S3, AF.Exp, scale=scale),
                             excl=[f"S{sb}"], writes=[f"P{pi}"])
                        P.op("pool", lambda e, P3=P3: e.tensor_tensor(P3, P3, mask2[:, :, 0:128], ALU.mult),
                             reads=mkeys, writes=[f"P{pi}"])
                    for hh in range(2):
                        for (dst0, lhs_c, lhs_p) in ((hh * 128, vS[:, n, :], vS[:, n - 1, :] if n > 0 else None),
                                                     (256 + hh * 128, ones[:], ones[:] if n > 0 else None)):
                            P.op("pe", lambda e, OL=OL, dst0=dst0, lhs_c=lhs_c, Pm=Pm, hh=hh, n=n: e.matmul(
                                OL[:, dst0:dst0 + 128], lhs_c, Pm[:, hh * 256:hh * 256 + 128], start=True, stop=(n == 0)),
                                reads=vkeys + ["ones", f"P{pi}"], writes=[f"OL{sb}"])
                            if n > 0:
                                P.op("pe", lambda e, OL=OL, dst0=dst0, lhs_p=lhs_p, Pm=Pm, hh=hh: e.matmul(
                                    OL[:, dst0:dst0 + 128], lhs_p, Pm[:, hh * 256 + 128:hh * 256 + 256],
                                    start=False, stop=True),
                                    reads=vkeys + ["ones", f"P{pi}"], writes=[f"OL{sb}"])
                    for hh in range(2):
                        h = hp * 2 + hh
                        P.op("dve", lambda e, OL=OL, Lq=Lq, hh=hh, h=h: e.tensor_scalar(
                            Lq[:, hh * 128:(hh + 1) * 128], OL[:, 256 + hh * 128:256 + (hh + 1) * 128],
                            esink[0:64, h:h + 1], None, ALU.add),
                            reads=["esink"], excl=[f"OL{sb}"], writes=[f"Lp{sb}_{hh}"])
                    P.op("dve", lambda e, Lq=Lq: e.reciprocal(Lq[:], Lq[:]),
                         reads=[f"Lp{sb}_0", f"Lp{sb}_1"], writes=[f"Lp{sb}_0", f"Lp{sb}_1"])
                    lc = (n % 4) * 128
                    P.op("dve", lambda e, OL=OL, Lq=Lq, og=og, hp=hp, lc=lc: e.tensor_tensor(
                        og[:, hp * 2:hp * 2 + 2, lc:lc + 128], OL[:, 0:256].rearrange("p (h c) -> p h c", h=2),
                        Lq[:].rearrange("p (h c) -> p h c", h=2), ALU.mult),
                        reads=[f"Lp{sb}_0", f"Lp{sb}_1"], excl=[f"OL{sb}"], writes=[f"ostg{g % 2}_{hp}_{n % 4}"])
                if n % 4 == 3:
                    P.dma("sp", io["oT"][u][:, :, g * 512:(g + 1) * 512].rearrange("h p t -> p h t"), og[:],
                          reads=[f"ostg{g % 2}_{hp}_{k}" for hp in range(4) for k in range(4)],
                          writes=[f"oT{u}_{g}"])
        P.emit()


def build_k2_swa(U, S):
    nc = bass.Bass("TRN2", target_bir_lowering=False)
    io = dict(qT=_dram_in(nc, "qT", [U, 8, 64, S], BF16), kT=_dram_in(nc, "kT", [U, 64, S], BF16),
              v=_dram_in(nc, "v", [U, 128, S // 128, 64], BF16), sinks=_dram_in(nc, "sinks", [U, 1, 8], F32),
              oT=_dram_out(nc, "oT", [U, 8, 64, S], BF16))
    P = Prog(nc)
    phase_swa(P, nc, U, S, io)
    P.close()
    return nc


MIXERS = ("moba", "mla", "swa", "diff")
_NC_CACHE = {}


def _get_nc(key, builder):
    if key not in _NC_CACHE:
        _NC_CACHE[key] = builder()
    return _NC_CACHE[key]


def _run(nc, in_maps):
    res = run_bass_kernel_spmd(nc, in_maps, core_ids=list(range(NCORES)))
    return res.results


def _f32(a):
    return np.ascontiguousarray(np.asarray(a, dtype=np.float32))


def _v_units(vfull_b, col0, dv, S):
    a = vfull_b[:, col0:col0 + dv]
    return np.ascontiguousarray(a.reshape(S // 128, 128, dv).transpose(1, 0, 2))


def run_model(inp, B, S):
    D = D_MODEL
    TOK = B * S // NCORES
    CPB = NCORES // B
    NT = TOK // 128
    x = _f32(inp["x"]).reshape(B * S, D)
    xs = [np.ascontiguousarray(x[c * TOK:(c + 1) * TOK]) for c in range(NCORES)]
    pos = []
    for c in range(NCORES):
        p = ((c % CPB) * TOK + np.arange(TOK)).astype(np.float32)
        pos.append(np.ascontiguousarray(p.reshape(NT, 128).T))
    for k1kind in ("moba", "swa", "mla"):
        _get_nc(("k1", k1kind, TOK), lambda: build_k1(k1kind, TOK))
    for mx in ("moba", "mla"):
        _get_nc(("k2", mx, B * 16 // NCORES, S), lambda: build_k2(mx, B * 16 // NCORES, S))
    _get_nc(("k2swa", S), lambda: build_k2_swa(1, S))
    _get_nc(("k2diff", B * 8 // NCORES, S, 3), lambda: build_k2("diff", B * 8 // NCORES, S,
                                                                0.8 - 0.6 * math.exp(-0.3 * 3)))
    _get_nc(("k3", TOK, 128), lambda: build_k3(TOK, 128))
    for layer in range(4):
        mixer = MIXERS[layer % 4]
        j = layer // 4
        if mixer == "moba":
            W = dict(wq=inp["moba_wq"][j], wk=inp["moba_wk"][j], wv=inp["moba_wv"][j])
            G = dict(gq=inp["moba_gq"][j], gk=inp["moba_gk"][j])
        elif mixer == "diff":
            W = dict(wq=inp["diff_wq"][j], wk=inp["diff_wk"][j], wv=inp["diff_wv"][j])
            G = dict(gq=inp["diff_gq"][j], gk=inp["diff_gk"][j])
        elif mixer == "swa":
            W = dict(wq=inp["swa_wq"][j], wk=inp["swa_wk"][j], wv=inp["swa_wv"][j])
            G = dict(gq=inp["swa_gq"][j], gk=inp["swa_gk"][j])
        else:
            W = dict(wq_a=inp["mla_wq_a"][j], wkv_a=inp["mla_wkv_a"][j], wq_b=inp["mla_wq_b"][j],
                     wkv_b=inp["mla_wkv_b"][j])
            G = dict(g_qa=inp["mla_g_qa"][j], g_kva=inp["mla_g_kva"][j], g_qn=inp["mla_g_qn"][j],
                     g_kn=inp["mla_g_kn"][j], g_qr=inp["mla_g_qr"][j], g_kr=inp["mla_g_kr"][j])
        W = {k: _f32(v) for k, v in W.items()}
        G = {k: _f32(v).reshape(1, -1) for k, v in G.items()}
        k1kind = "moba" if mixer == "diff" else mixer
        nc1 = _get_nc(("k1", k1kind, TOK), lambda: build_k1(k1kind, TOK))
        g_attn = _f32(inp["attn_norm"][layer]).reshape(1, D)
        maps = []
        for c in range(NCORES):
            m = dict(x=xs[c], g=g_attn, pos=pos[c])
            m.update(W)
            m.update(G)
            maps.append(m)
        r1 = _run(nc1, maps)

        def cat(name, b):
            return np.concatenate([np.asarray(r1[b * CPB + cc][name]) for cc in range(CPB)], axis=-1)

        def catv(b):
            return np.concatenate([np.asarray(r1[b * CPB + cc]["v"]) for cc in range(CPB)], axis=0)

        if mixer in ("moba", "mla"):
            H = 16
            U = B * H // NCORES
            qn, kn = ("qT", "kT") if mixer == "moba" else ("qnT", "knT")
            qf = [cat(qn, b) for b in range(B)]
            kf = [cat(kn, b) for b in range(B)]
            vf = [catv(b) for b in range(B)]
            if mixer == "mla":
                qrf = [cat("qrT", b) for b in range(B)]
                krf = [cat("krT", b) for b in range(B)]
            maps = []
            for c in range(NCORES):
                units = [divmod(c * U + u, H) for u in range(U)]
                m = dict(qA=np.stack([qf[b][h] for b, h in units]), kA=np.stack([kf[b][h] for b, h in units]),
                         v=np.stack([_v_units(vf[b], h * 128, 128, S) for b, h in units]))
                if mixer == "mla":
                    m["qB"] = np.stack([qrf[b][h] for b, h in units])
                    m["kB"] = np.stack([krf[b] for b, h in units])
                maps.append(m)
            nc2 = _get_nc(("k2", mixer, U, S), lambda: build_k2(mixer, U, S))
            r2 = _run(nc2, maps)
            ofull = np.zeros((B, H, 128, S), NPBF16)
            for c in range(NCORES):
                o = np.asarray(r2[c]["oT"])
                for u in range(U):
                    b, h = divmod(c * U + u, H)
                    ofull[b, h] = o[u]
            ko = 128
            ochunks = ofull
        elif mixer == "swa":
            U = 1
            qf = [cat("qT", b) for b in range(B)]
            kf = [cat("kT", b) for b in range(B)]
            vf = [catv(b) for b in range(B)]
            sinks = _f32(inp["swa_sinks"][j])
            maps = []
            for c in range(NCORES):
                b, kvh = divmod(c, 4)
                maps.append(dict(qT=np.ascontiguousarray(qf[b][kvh * 8:(kvh + 1) * 8][None]),
                                 kT=np.ascontiguousarray(kf[b][kvh][None]),
                                 v=_v_units(vf[b], kvh * 64, 64, S)[None],
                                 sinks=np.ascontiguousarray(sinks[kvh * 8:(kvh + 1) * 8].reshape(1, 1, 8))))
            nc2 = _get_nc(("k2swa", S), lambda: build_k2_swa(1, S))
            r2 = _run(nc2, maps)
            ochunks = np.zeros((B, 32, 64, S), NPBF16)
            for c in range(NCORES):
                b, kvh = divmod(c, 4)
                ochunks[b, kvh * 8:(kvh + 1) * 8] = np.asarray(r2[c]["oT"])[0]
            ochunks = ochunks.reshape(B, 16, 128, S)
            ko = 128
        else:
            H = 8
            U = B * H // NCORES
            lam_init = 0.8 - 0.6 * math.exp(-0.3 * layer)
            qf = [cat("qT", b) for b in range(B)]
            kf = [cat("kT", b) for b in range(B)]
            vf = [catv(b) for b in range(B)]
            gs = _f32(inp["diff_g_sub"][j])
            maps = []
            for c in range(NCORES):
                units = [divmod(c * U + u, H) for u in range(U)]
                m = dict(qA=np.stack([qf[b][2 * h:2 * h + 2] for b, h in units]),
                         kA=np.stack([kf[b][2 * h:2 * h + 2] for b, h in units]),
                         v=np.stack([_v_units(vf[b], h * 256, 256, S) for b, h in units]),
                         gsub=np.ascontiguousarray(gs.reshape(2, 128).T))
                for n in ("lq1", "lk1", "lq2", "lk2"):
                    m[n] = _f32(inp["diff_" + n][j]).reshape(1, 128)
                maps.append(m)
            nc2 = _get_nc(("k2diff", U, S, layer), lambda: build_k2("diff", U, S, lam_init))
            r2 = _run(nc2, maps)
            ochunks = np.zeros((B, 16, 128, S), NPBF16)
            for c in range(NCORES):
                o = np.asarray(r2[c]["oT"])
                for u in range(U):
                    b, h = divmod(c * U + u, H)
                    ochunks[b, 2 * h:2 * h + 2] = o[u].reshape(2, 128, S)
            ko = 128
        wo = _f32(inp[f"{mixer}_wo"][j])
        gf = _f32(inp["ffn_norm"][layer]).reshape(1, D)
        wg = _f32(inp["ffn_w_gate"][layer])
        wu = _f32(inp["ffn_w_up"][layer])
        wd = _f32(inp["ffn_w_down"][layer])
        cw = host_cw(_f32(inp["ffn_conv_w"][layer]), _f32(inp["ffn_conv_b"][layer]))
        nc3 = _get_nc(("k3", TOK, ko), lambda: build_k3(TOK, ko))
        NKo = D // ko
        maps = []
        for c in range(NCORES):
            b, cc = divmod(c, CPB)
            t0 = cc * TOK
            oT = np.ascontiguousarray(ochunks[b][:, :, t0:t0 + TOK])
            if cc == 0:
                xh = np.zeros((128, D), np.float32)
                oTh = np.zeros((NKo, ko, 128), NPBF16)
            else:
                xh = np.ascontiguousarray(xs[c - 1][TOK - 128:])
                oTh = np.ascontiguousarray(ochunks[b][:, :, t0 - 128:t0])
            maps.append(dict(x=xs[c], xh=xh, oT=oT, oTh=oTh, wo=wo, gf=gf, wg=wg, wu=wu, wd=wd, cw=cw))
        r3 = _run(nc3, maps)
        xs = [np.asarray(r3[c]["xo"]) for c in range(NCORES)]
    return np.concatenate(xs, axis=0).reshape(B, S, D).astype(np.float32)


class _Rep:
    def __init__(self, ap):
        self.ap = ap

    def __getitem__(self, i):
        return self.ap


W_SPECS = dict(
    attn_norm=(4, 2048), ffn_norm=(4, 2048),
    moba_wq=(1, 2048, 2048), moba_wk=(1, 2048, 2048), moba_wv=(1, 2048, 2048), moba_gq=(1, 128), moba_gk=(1, 128),
    moba_wo=(1, 2048, 2048),
    mla_wq_a=(1, 2048, 512), mla_g_qa=(1, 512), mla_wq_b=(1, 512, 3072), mla_wkv_a=(1, 2048, 576), mla_g_kva=(1, 512),
    mla_wkv_b=(1, 512, 4096), mla_g_qn=(1, 128), mla_g_kn=(1, 128), mla_g_qr=(1, 64), mla_g_kr=(1, 64),
    mla_wo=(1, 2048, 2048),
    swa_wq=(1, 2048, 2048), swa_wk=(1, 2048, 256), swa_wv=(1, 2048, 256), swa_gq=(1, 64), swa_gk=(1, 64),
    swa_sinks=(1, 32), swa_wo=(1, 2048, 2048),
    diff_wq=(1, 2048, 2048), diff_wk=(1, 2048, 2048), diff_wv=(1, 2048, 2048), diff_gq=(1, 128), diff_gk=(1, 128),
    diff_lq1=(1, 128), diff_lk1=(1, 128), diff_lq2=(1, 128), diff_lk2=(1, 128), diff_wo=(1, 2048, 2048),
    ffn_w_gate=(4, 2048, 5632), ffn_w_up=(4, 2048, 5632), ffn_w_down=(4, 5632, 2048),
)


def build_fused(S):
    nc = bass.Bass("TRN2", target_bir_lowering=False)
    D = D_MODEL
    NG = 4
    GT = S // NG
    NTg = GT // 128
    NKT = S // 128
    x_in = _dram_in(nc, "x", [S, D], F32)
    x_out = _dram_out(nc, "out", [S, D], F32)
    pos = _dram_in(nc, "pos", [NG, 128, NTg], F32)
    cw = _dram_in(nc, "cw", [4, 128, 4 * (D_FF // 128)], F32)
    gsub_in = _dram_in(nc, "gsub", [128, 2], F32)
    Win = {k: _dram_in(nc, k, shp, F32) for k, shp in W_SPECS.items()}
    xbuf = [nc.dram_tensor(f"xbuf{i}", [S, D], F32).ap() for i in range(2)]
    xmid = nc.dram_tensor("xmid", [GT, D], F32).ap()
    hTd = nc.dram_tensor("hTd", [16, 128, GT], BF16).ap()
    hTd3 = nc.dram_tensor("hTd3", [NTg + 1, 128, 16 * 128], BF16).ap()
    qs = nc.dram_tensor("qs", [16 * 128, S], BF16).ap()
    ks = nc.dram_tensor("ks", [16 * 128, S], BF16).ap()
    qrs = nc.dram_tensor("qrs", [16, 64, S], BF16).ap()
    krs = nc.dram_tensor("krs", [64, S], BF16).ap()
    vs = nc.dram_tensor("vs", [128, NKT * 2048], BF16).ap()
    os_ = nc.dram_tensor("os", [16 * 128, S], BF16).ap()
    cqT = nc.dram_tensor("cqT", [4, 128, GT], BF16).ap()
    ckvT = nc.dram_tensor("ckvT", [4, 128, GT], BF16).ap()
    zx = nc.dram_tensor("zx", [128, D], F32).ap()
    zo = nc.dram_tensor("zo", [16, 128, 128], BF16).ap()
    P = Prog(nc)
    with contextlib.ExitStack() as es:
        T = make_T(es, nc)
        zt = T("zt", [128, D], F32)
        ztb = T("ztb", [128, 16 * 128], BF16)
        P.op("pool", lambda e: e.memset(zt[:], 0.0), writes=["zt"])
        P.op("pool", lambda e: e.memset(ztb[:], 0.0), writes=["ztb"])
        P.dma("sp", zx[:, :], zt[:], reads=["zt"], writes=["zx"])
        P.dma("sp", zo.rearrange("c p t -> p c t"), ztb[:].rearrange("p (c t) -> p c t", c=16), reads=["ztb"], writes=["zo"])
        P.emit()
    q16 = qs.rearrange("(h p) s -> h p s", p=128)
    k16 = ks.rearrange("(h p) s -> h p s", p=128)
    o16 = os_.rearrange("(h p) s -> h p s", p=128)
    x_cur = x_in
    for layer in range(4):
        mixer = MIXERS[layer]
        x_nxt = x_out if layer == 3 else xbuf[layer % 2]
        pre = mixer + "_"
        Gd = {}
        for g in range(NG):
            gs_ = slice(g * GT, (g + 1) * GT)
            xg = x_cur[gs_, :]
            phase_norm(P, nc, GT, xg, Win["attn_norm"][layer:layer + 1, :], hTd)
            tag = f"L{layer}g{g}"
            if mixer in ("moba", "diff"):
                Gd = dict(gq=Win[pre + "gq"], gk=Win[pre + "gk"])
                blocks = []
                for wname, gname, dst in (("wq", "gq", q16), ("wk", "gk", k16)):
                    for b in range(4):
                        segs = [dict(off=j * 128, w=128, gain=gname, rot=32, kind="T", dst=[dst[b * 4 + j][:, gs_]])
                                for j in range(4)]
                        blocks.append((b * 512, 512, segs, Win[pre + wname][0]))
                dvh = 128 if mixer == "moba" else 256
                vv = vs.rearrange("p (h t d) -> h p t d", t=NKT, d=dvh)
                for b in range(4):
                    segs = []
                    for jj in range(512 // dvh):
                        h = b * (512 // dvh) + jj
                        segs.append(dict(off=jj * dvh, w=dvh, gain=None, rot=0, kind="M",
                                         dst=(lambda t, h=h, g=g: vv[h][:, g * NTg + t, :])))
                    blocks.append((b * 512, 512, segs, Win[pre + "wv"][0]))
                phase_proj(P, nc, GT, hTd, 16, 128, None, blocks, Gd, pos[g], tag + "qkv")
            elif mixer == "swa":
                Gd = dict(gq=Win["swa_gq"], gk=Win["swa_gk"])
                q32 = qs.rearrange("(h p) s -> h p s", p=64)
                k4 = ks[0:256, :].rearrange("(h p) s -> h p s", p=64)
                blocks = []
                for b in range(8):
                    segs = [dict(off=j * 64, w=64, gain="gq", rot=16, kind="T", dst=[q32[b * 4 + j][:, gs_]]) for j in range(4)]
                    blocks.append((b * 256, 256, segs, Win["swa_wq"][0]))
                segs = [dict(off=j * 64, w=64, gain="gk", rot=16, kind="T", dst=[k4[j][:, gs_]]) for j in range(4)]
                blocks.append((0, 256, segs, Win["swa_wk"][0]))
                vv = vs[:, 0:NKT * 256].rearrange("p (h t d) -> h p t d", t=NKT, d=64)
                segs = [dict(off=j * 64, w=64, gain=None, rot=0, kind="M",
                             dst=(lambda t, j=j, g=g: vv[j][:, g * NTg + t, :])) for j in range(4)]
                blocks.append((0, 256, segs, Win["swa_wv"][0]))
                phase_proj(P, nc, GT, hTd, 16, 128, None, blocks, Gd, pos[g], tag + "qkv")
            else:
                Gd = dict(g_qa=Win["mla_g_qa"], g_kva=Win["mla_g_kva"], g_qn=Win["mla_g_qn"], g_kn=Win["mla_g_kn"],
                          g_qr=Win["mla_g_qr"], g_kr=Win["mla_g_kr"])
                phase_proj(P, nc, GT, hTd, 16, 128, None,
                           [(0, 512, [dict(off=0, w=512, gain="g_qa", rot=0, kind="T", dst=[cqT[c] for c in range(4)])],
                             Win["mla_wq_a"][0]),
                            (0, 512, [dict(off=0, w=512, gain="g_kva", rot=0, kind="T", dst=[ckvT[c] for c in range(4)])],
                             Win["mla_wkv_a"][0]),
                            (512, 64, [dict(off=0, w=64, gain="g_kr", rot=64, kind="T", dst=[krs[:, gs_]])],
                             Win["mla_wkv_a"][0])],
                           Gd, pos[g], tag + "wa")
                blocks = []
                for b in range(8):
                    segs = []
                    for jj in range(2):
                        h = b * 2 + jj
                        segs.append(dict(off=jj * 192, w=128, gain="g_qn", rot=0, kind="T", dst=[q16[h][:, gs_]]))
                        segs.append(dict(off=jj * 192 + 128, w=64, gain="g_qr", rot=64, kind="T", dst=[qrs[h][:, gs_]]))
                    blocks.append((b * 384, 384, segs))
                phase_proj(P, nc, GT, cqT, 4, 128, Win["mla_wq_b"][0], blocks, Gd, pos[g], tag + "wqb")
                vv = vs.rearrange("p (h t d) -> h p t d", t=NKT, d=128)
                blocks = []
                for b in range(8):
                    segs = []
                    for jj in range(2):
                        h = b * 2 + jj
                        segs.append(dict(off=jj * 256, w=128, gain="g_kn", rot=0, kind="T", dst=[k16[h][:, gs_]]))
                        segs.append(dict(off=jj * 256 + 128, w=128, gain=None, rot=0, kind="M",
                                         dst=(lambda t, h=h, g=g: vv[h][:, g * NTg + t, :])))
                    blocks.append((b * 512, 512, segs))
                phase_proj(P, nc, GT, ckvT, 4, 128, Win["mla_wkv_b"][0], blocks, Gd, pos[g], tag + "wkvb")
        if mixer == "moba":
            io = dict(qA=q16, kA=k16, v=vs.rearrange("p (h t d) -> h p t d", t=NKT, d=128), oT=o16)
            phase_attn(P, nc, "moba", 16, S, io)
        elif mixer == "mla":
            io = dict(qA=q16, kA=k16, qB=qrs, kB=_Rep(krs), v=vs.rearrange("p (h t d) -> h p t d", t=NKT, d=128), oT=o16)
            phase_attn(P, nc, "mla", 16, S, io)
        elif mixer == "swa":
            io = dict(qT=qs.rearrange("(u h p) s -> u h p s", h=8, p=64), kT=ks[0:256, :].rearrange("(h p) s -> h p s", p=64),
                      v=vs[:, 0:NKT * 256].rearrange("p (h t d) -> h p t d", t=NKT, d=64),
                      sinks=Win["swa_sinks"].rearrange("o (u h) -> u o h", h=8),
                      oT=os_.rearrange("(u h p) s -> u h p s", h=8, p=64))
            phase_swa(P, nc, 4, S, io)
        else:
            lam_init = 0.8 - 0.6 * math.exp(-0.3 * layer)
            io = dict(qA=qs.rearrange("(h c p) s -> h c p s", c=2, p=128), kA=ks.rearrange("(h c p) s -> h c p s", c=2, p=128),
                      v=vs.rearrange("p (h t d) -> h p t d", t=NKT, d=256), oT=os_.rearrange("(h e) s -> h e s", e=256),
                      lq1=Win["diff_lq1"], lk1=Win["diff_lk1"], lq2=Win["diff_lq2"], lk2=Win["diff_lk2"], gsub=gsub_in)
            phase_attn(P, nc, "diff", 8, S, io, lam_init)
        for g in range(NG):
            gs_ = slice(g * GT, (g + 1) * GT)
            if g == 0:
                xh, oTh = zx, zo
            else:
                xh = x_cur[g * GT - 128:g * GT, :]
                oTh = o16[:, :, g * GT - 128:g * GT]
            phase_outproj_norm(P, nc, GT, 128, x_cur[gs_, :], xh, o16[:, :, gs_], oTh, Win[pre + "wo"][0],
                               Win["ffn_norm"][layer:layer + 1, :], xmid, hTd3)
            phase_ffn(P, nc, GT, xmid, hTd3, Win["ffn_w_gate"][layer], Win["ffn_w_up"][layer], Win["ffn_w_down"][layer],
                      cw[layer], x_nxt[gs_, :])
        x_cur = x_nxt
    P.close()
    return nc


def run_fused(inp, B, S):
    D = D_MODEL
    NG = 4
    GT = S // NG
    NTg = GT // 128
    nc = _get_nc(("fused", S), lambda: build_fused(S))
    x = _f32(inp["x"])
    posv = np.arange(S, dtype=np.float32).reshape(NG, NTg, 128).transpose(0, 2, 1)
    base = dict(pos=np.ascontiguousarray(posv),
                cw=np.stack([host_cw(_f32(inp["ffn_conv_w"][l]), _f32(inp["ffn_conv_b"][l])) for l in range(4)]),
                gsub=np.ascontiguousarray(_f32(inp["diff_g_sub"][0]).reshape(2, 128).T))
    for k, shp in W_SPECS.items():
        base[k] = _f32(inp[k]).reshape(shp)
    maps = []
    for b in range(B):
        m = dict(base)
        m["x"] = np.ascontiguousarray(x[b])
        maps.append(m)
    res = run_bass_kernel_spmd(nc, maps, core_ids=list(range(B)))
    return np.stack([np.asarray(res.results[b]["out"]) for b in range(B)]).astype(np.float32)


def kernel(**inputs):
    return run_model(inputs, 2, 8192)
```
